# Optimizing a Trainium2 kernel written in Bass

```python
import math
import jax, jax.numpy as jnp
from jax import lax
import numpy as np

D_MODEL = 1024
BATCH = 8
SEQ = 2048
DEPTH = 2

N_MIXERS = 2
N_GLA = (DEPTH + 1) // 2
N_DIFF = DEPTH // 2

GLA_HEADS = 4
GLA_DK = D_MODEL // 2
GLA_DV = D_MODEL
GLA_HK = GLA_DK // GLA_HEADS
GLA_HV = GLA_DV // GLA_HEADS
GLA_GATE_RANK = 16
GLA_TAU = 16.0
GLA_CHUNK = 64

DIFF_HEADS = 8
DIFF_HEAD_DIM = D_MODEL // (2 * DIFF_HEADS)
Q_BLOCK = 128

D_FF = -(-8 * D_MODEL // (3 * 256)) * 256

NORM_EPS = 1e-6
SUBLN_EPS = 1e-5

kernel_name = "hybrid_gla_diffattn_swiglu"


def rmsnorm(x, g, eps=NORM_EPS):
    xf = x.astype(jnp.float32)
    y = xf * lax.rsqrt(jnp.mean(xf * xf, axis=-1, keepdims=True) + eps)
    return (y * g.astype(jnp.float32)).astype(x.dtype)


def gla_mixer(h, w_in, w_gate_a, w_gate_b, b_gate, g_norm, w_out):
    B, S, _ = h.shape
    H, HK, HV, C = GLA_HEADS, GLA_HK, GLA_HV, GLA_CHUNK
    N = S // C
    proj = h @ w_in
    q, k, v, r = jnp.split(proj, [GLA_DK, 2 * GLA_DK, 2 * GLA_DK + GLA_DV], axis=-1)
    gk = jax.nn.log_sigmoid(((h @ w_gate_a) @ w_gate_b + b_gate).astype(jnp.float32)) / GLA_TAU

    def to_chunks(t, d):
        return t.reshape(B, N, C, H, d).transpose(1, 0, 3, 2, 4).astype(jnp.float32)

    qc = to_chunks(q, HK) * (HK ** -0.5)
    kc = to_chunks(k, HK)
    vc = to_chunks(v, HV)
    gc = to_chunks(gk, HK)
    causal = jnp.tril(jnp.ones((C, C), dtype=bool))

    def step(state, inp):
        qi, ki, vi, gi = inp
        b = jnp.cumsum(gi, axis=-2)
        o_inter = jnp.einsum('bhck,bhkv->bhcv', qi * jnp.exp(b), state)
        rel = b[:, :, :, None, :] - b[:, :, None, :, :]
        rel = jnp.where(causal[:, :, None], rel, -jnp.inf)
        attn = jnp.einsum('bhik,bhjk,bhijk->bhij', qi, ki, jnp.exp(rel))
        o = o_inter + jnp.einsum('bhij,bhjv->bhiv', attn, vi)
        b_last = b[:, :, -1:, :]
        new_state = (jnp.exp(b_last[:, :, 0, :, None]) * state
                     + jnp.einsum('bhck,bhcv->bhkv', ki * jnp.exp(b_last - b), vi))
        return new_state, o

    state0 = jnp.zeros((B, H, HK, HV), jnp.float32)
    _, o = lax.scan(step, state0, (qc, kc, vc, gc))
    o = o.transpose(1, 0, 3, 2, 4).reshape(B, S, H, HV)
    o = rmsnorm(o, g_norm).astype(h.dtype)
    o = o.reshape(B, S, GLA_DV) * jax.nn.silu(r)
    return o @ w_out


def diff_mixer(h, w_in, lam_q1, lam_k1, lam_q2, lam_k2, g_norm, w_out, lambda_init):
    B, S, _ = h.shape
    H, d = DIFF_HEADS, DIFF_HEAD_DIM
    proj = h @ w_in
    q, k, v = jnp.split(proj, 3, axis=-1)
    q = q.reshape(B, S, H, 2, d).transpose(0, 2, 3, 1, 4)
    k = k.reshape(B, S, H, 2, d).transpose(0, 2, 3, 1, 4)
    v = v.reshape(B, S, H, 2 * d).transpose(0, 2, 1, 3)
    lam = (jnp.exp(jnp.sum(lam_q1.astype(jnp.float32) * lam_k1.astype(jnp.float32)))
           - jnp.exp(jnp.sum(lam_q2.astype(jnp.float32) * lam_k2.astype(jnp.float32)))
           + lambda_init)
    scale = d ** -0.5
    outs = []
    for blk in range(S // Q_BLOCK):
        q0, kend = blk * Q_BLOCK, (blk + 1) * Q_BLOCK
        qb = q[:, :, :, q0:kend]
        kb = k[:, :, :, :kend]
        vb = v[:, :, :kend]
        s = jnp.einsum('bhcqd,bhckd->bhcqk', qb, kb).astype(jnp.float32) * scale
        qpos = q0 + jnp.arange(Q_BLOCK)
        kpos = jnp.arange(kend)
        s = jnp.where(kpos[None, :] <= qpos[:, None], s, -jnp.inf)
        p = jax.nn.softmax(s, axis=-1)
        a = p[:, :, 0] - lam * p[:, :, 1]
        outs.append(jnp.einsum('bhqk,bhkv->bhqv', a.astype(vb.dtype), vb))
    o = jnp.concatenate(outs, axis=2)
    o = rmsnorm(o, g_norm, SUBLN_EPS) * (1.0 - lambda_init)
    o = o.astype(h.dtype).transpose(0, 2, 1, 3).reshape(B, S, H * 2 * d)
    return o @ w_out


def swiglu(h, w_in, w_out):
    gate, up = jnp.split(h @ w_in, 2, axis=-1)
    return (jax.nn.silu(gate) * up) @ w_out


def setup_inputs(seed: int = 0) -> dict:
    key = jax.random.key(seed)
    ks = jax.random.split(key, 20)
    D = D_MODEL

    def w(k, shape, fan_in):
        return jax.random.normal(k, shape, jnp.float32) * fan_in ** -0.5

    def gain(k, shape):
        return 1.0 + 0.02 * jax.random.normal(k, shape, jnp.float32)

    gla_in_w = 2 * GLA_DK + 2 * GLA_DV
    return {
        "x": jax.random.normal(ks[0], (BATCH, SEQ, D), jnp.float32),
        "gla_w_in": w(ks[1], (N_GLA, D, gla_in_w), D),
        "gla_w_gate_a": w(ks[2], (N_GLA, D, GLA_GATE_RANK), D),
        "gla_w_gate_b": w(ks[3], (N_GLA, GLA_GATE_RANK, GLA_DK), GLA_GATE_RANK),
        "gla_b_gate": 0.1 * jax.random.normal(ks[4], (N_GLA, GLA_DK), jnp.float32),
        "gla_norm": gain(ks[5], (N_GLA, GLA_HV)),
        "gla_w_out": w(ks[6], (N_GLA, GLA_DV, D), GLA_DV),
        "diff_w_in": w(ks[7], (N_DIFF, D, 3 * D), D),
        "diff_lam_q1": 0.1 * jax.random.normal(ks[8], (N_DIFF, DIFF_HEAD_DIM), jnp.float32),
        "diff_lam_k1": 0.1 * jax.random.normal(ks[9], (N_DIFF, DIFF_HEAD_DIM), jnp.float32),
        "diff_lam_q2": 0.1 * jax.random.normal(ks[10], (N_DIFF, DIFF_HEAD_DIM), jnp.float32),
        "diff_lam_k2": 0.1 * jax.random.normal(ks[11], (N_DIFF, DIFF_HEAD_DIM), jnp.float32),
        "diff_norm": gain(ks[12], (N_DIFF, 2 * DIFF_HEAD_DIM)),
        "diff_w_out": w(ks[13], (N_DIFF, D, D), D),
        "norm_mixer": gain(ks[14], (DEPTH, D)),
        "norm_ffn": gain(ks[15], (DEPTH, D)),
        "ffn_w_in": w(ks[16], (DEPTH, D, 2 * D_FF), D),
        "ffn_w_out": w(ks[17], (DEPTH, D_FF, D), D_FF),
        "norm_final": gain(ks[18], (D,)),
    }


def reference(x, gla_w_in, gla_w_gate_a, gla_w_gate_b, gla_b_gate, gla_norm, gla_w_out,
              diff_w_in, diff_lam_q1, diff_lam_k1, diff_lam_q2, diff_lam_k2, diff_norm,
              diff_w_out, norm_mixer, norm_ffn, ffn_w_in, ffn_w_out, norm_final):
    for i in range(DEPTH):
        h = rmsnorm(x, norm_mixer[i])
        j = i // N_MIXERS
        if i % N_MIXERS == 0:
            y = gla_mixer(h, gla_w_in[j], gla_w_gate_a[j], gla_w_gate_b[j], gla_b_gate[j],
                          gla_norm[j], gla_w_out[j])
        else:
            lambda_init = 0.8 - 0.6 * math.exp(-0.3 * i)
            y = diff_mixer(h, diff_w_in[j], diff_lam_q1[j], diff_lam_k1[j], diff_lam_q2[j],
                           diff_lam_k2[j], diff_norm[j], diff_w_out[j], lambda_init)
        x = x + y
        x = x + swiglu(rmsnorm(x, norm_ffn[i]), ffn_w_in[i], ffn_w_out[i])
    return rmsnorm(x, norm_final)
```

```python
import math
import os
import numpy as np
import concourse.bass as bass
import concourse.mybir as mybir
from concourse.bass_utils import run_bass_kernel_spmd
from contextlib import ExitStack

F32 = mybir.dt.float32
BF16 = mybir.dt.bfloat16
AF = mybir.ActivationFunctionType
ALU = mybir.AluOpType
AX = mybir.AxisListType

D = 1024
SEQ = 2048
NT = 16
DFF = 2816
NF = 22
HK = 128
HV = 256
EPS = 1e-6
SUBLN_EPS = 1e-5
LAMBDA_INIT = 0.8 - 0.6 * math.exp(-0.3 * 1)
RING_N = 6144
NRING = 4
NDMASEM = 48

V_NORM = 0
V_GLAGN = 5 * 1024
V_DIFFGN = V_GLAGN + 256
V_LAM = V_DIFFGN + 128
V_TOT = V_LAM + 256


class Tok:
    __slots__ = ("sem", "key", "val", "clock")

    def __init__(self, sem, key, val, clock):
        self.sem, self.key, self.val, self.clock = sem, key, val, clock


class Res:
    __slots__ = ("name", "w", "r", "dma", "excl")

    def __init__(self, name, excl=False):
        self.name = name
        self.w = None
        self.r = {}
        self.dma = None
        self.excl = excl


class Eng:
    def __init__(self, name, h, sem, selfdep):
        self.name, self.h, self.sem, self.selfdep = name, h, sem, selfdep
        self.key = "E_" + name
        self.count = 0
        self.clock = {}
        self.last = None
        self.nwait = 0


class Sched:
    def __init__(self, nc, stack):
        self.nc = nc
        self.stack = stack
        self.engs = {}
        for name, h, selfdep in [
            ("pe", nc.tensor, False),
            ("act", nc.scalar, True),
            ("dve", nc.vector, True),
            ("pool", nc.gpsimd, True),
            ("sp", nc.sync, True),
        ]:
            sem = stack.enter_context(nc.semaphore("s_" + name))
            self.engs[name] = Eng(name, h, sem, selfdep)
        self.dmas = []
        self.ndma = 0
        self.dma_pool = [stack.enter_context(nc.semaphore("d_%d" % i)) for i in range(NDMASEM)]
        self.all_sems = [e.sem for e in self.engs.values()] + self.dma_pool
        self.clear_sems()
        nc.all_engine_barrier()

    def clear_sems(self):
        for s in self.all_sems:
            self.nc.gpsimd.sem_clear(s)

    def _sync(self, e, deps):
        need = {}
        for t in deps:
            if t is None:
                continue
            if t.key == e.key and not e.selfdep:
                continue
            if e.clock.get(t.key, 0) >= t.val:
                continue
            cur = need.get(t.key)
            if cur is None or cur.val < t.val:
                need[t.key] = t
        for key, t in need.items():
            if e.clock.get(key, 0) >= t.val:
                continue
            e.h.wait_ge(t.sem, t.val)
            e.nwait += 1
            for k, v in t.clock.items():
                if e.clock.get(k, 0) < v:
                    e.clock[k] = v
            e.clock[key] = t.val

    @staticmethod
    def _deps(reads, writes, ekey=None):
        deps = []
        for r in reads:
            if r.w is not None:
                deps.append(r.w)
            if r.excl:
                deps.extend(t for k, t in r.r.items() if k != ekey)
        for w in writes:
            if w.w is not None:
                deps.append(w.w)
            deps.extend(w.r.values())
        return deps

    @staticmethod
    def _update(tok, reads, writes):
        for r in reads:
            r.r[tok.key] = tok
        for w in writes:
            w.w = tok
            w.r = {}

    def op(self, en, fn, reads=(), writes=()):
        e = self.engs[en]
        self._sync(e, self._deps(reads, writes, e.key))
        ins = fn(e.h)
        e.count += 1
        ins.then_inc(e.sem, 1)
        tok = Tok(e.sem, e.key, e.count, dict(e.clock))
        e.last = tok
        self._update(tok, reads, writes)
        return tok

    def dma(self, qn, out, in_, reads=(), writes=(), owner=None, **kw):
        e = self.engs[qn]
        self._sync(e, self._deps(reads, writes))
        ins = e.h.dma_start(out=out, in_=in_, **kw)
        if owner.dma is None:
            sem = self.dma_pool[len(self.dmas)]
            owner.dma = [sem, "D_%d" % len(self.dmas), 0, None]
            self.dmas.append(owner.dma)
        d = owner.dma
        d[2] += 16
        ins.then_inc(d[0], 16)
        tok = Tok(d[0], d[1], d[2], dict(e.clock))
        d[3] = tok
        self.ndma += 1
        self._update(tok, reads, writes)
        return tok

    def prewait(self, en, writes):
        e = self.engs[en]
        self._sync(e, self._deps([], writes, e.key))

    def all_toks(self):
        toks = [e.last for e in self.engs.values() if e.last is not None]
        toks += [d[3] for d in self.dmas if d[3] is not None]
        return toks

    def barrier(self):
        toks = self.all_toks()
        for e in self.engs.values():
            self._sync(e, toks)

    def finish(self):
        self._sync(self.engs["sp"], self.all_toks())

    def cleanup(self):
        self.nc.all_engine_barrier()
        self.clear_sems()


class Builder:
    def __init__(self, stages=("gla", "ffn0", "diff", "ffn1"), final=True):
        self.stages = stages
        self.final = final

    def sb(self, name, shape, dt, stack=None):
        stack = stack or self.st
        self._n += 1
        return stack.enter_context(self.nc.sbuf_tensor("%s_%d" % (name, self._n), shape, dt))

    def bank(self, pool=None):
        pool = pool or self.gen_banks
        i = pool[self._bk % len(pool)]
        self._bk += 1
        return self.banks[i], self.Rbk[i]

    def mm(self, out, lhsT, rhs, start, stop, reads, writes):
        self.S.op("pe", lambda h: h.matmul(out, lhsT=lhsT, rhs=rhs, start=start, stop=stop),
                  reads, writes)

    def tr(self, out, in_, reads, writes):
        self.S.op("pe", lambda h: h.transpose(out=out, in_=in_, identity=self.ident[:, :]),
                  list(reads) + [self.Rconst], writes)

    def act(self, out, in_, func, reads, writes, **kw):
        self.S.op("act", lambda h: h.activation(out=out, in_=in_, func=func, **kw), reads, writes)

    def open_ring(self, stack, nelem):
        self.ring = [self.sb("ring", [128, nelem], BF16, stack) for _ in range(NRING)]
        self.Rring = [Res("ring%d" % i) for i in range(NRING)]
        self.wbase = self.wpos
        self.wend = self.wstage_end.pop(0)
        if self.wbase == 0:
            gate = int(os.environ.get("XGATE", "-1"))
            if gate >= 0:
                self.S._sync(self.S.engs["pool"], [self.RX[gate].w])
        self._wissue(min(self.wbase + NRING - 1, self.wend))

    def _wissue(self, upto):
        while self.wissued <= upto:
            k = self.wissued
            ap, n = self.wseq[k]
            slot = (k - self.wbase) % NRING
            self.S.dma("pool", self.ring[slot][:, 0:n], ap, writes=[self.Rring[slot]],
                       owner=self.Rring[slot], max_dma_last_dim=2048)
            self.wissued += 1

    def wnext(self, hold=1):
        i = self.wpos
        self.wpos += 1
        self._wissue(min(i + NRING - hold, self.wend))
        slot = (i - self.wbase) % NRING
        return self.ring[slot], self.Rring[slot]

    def build(self):
        nc = bass.Bass("TRN2", target_bir_lowering=False)
        self.nc = nc
        self._n = 0
        self._bk = 0
        dr = {}

        def din(name, shape):
            dr[name] = nc.dram_tensor(name, shape, F32, kind="ExternalInput").ap()

        din("x", [SEQ, D])
        din("w_gla_h", [4, 128, 6144])
        din("w_gla_o", [4, 128, 2048])
        din("w_gla_a", [128, 128])
        din("w_gla_b", [33, 512])
        din("w_diff_h", [8, 128, 3072])
        din("w_diff_o", [8, 128, 1024])
        din("w_ffn_in", [2, 11, 128, 4096])
        din("w_ffn_out", [2, 4, 128, 5632])
        din("vecs", [1, V_TOT])
        out = nc.dram_tensor("out", [SEQ, D], F32, kind="ExternalOutput").ap()
        self.dr = dr
        self.out_ap = out
        self.final_done = set()
        self.last_stage = self.stages[-1] if self.stages else None

        wseq = []
        for stg in self.stages:
            if stg == "gla":
                for h in range(4):
                    wseq.append((dr["w_gla_h"][h], 6144))
                    wseq.append((dr["w_gla_o"][h], 2048))
            elif stg == "diff":
                for h in range(8):
                    wseq.append((dr["w_diff_h"][h], 3072))
                    wseq.append((dr["w_diff_o"][h], 1024))
            else:
                l = int(stg[-1])
                for hf in range(2):
                    for fb in range(11):
                        wseq.append((dr["w_ffn_in"][l, fb], 4096))
                    for nq in range(4):
                        wseq.append((dr["w_ffn_out"][l, nq], 5632))
        self.wseq = wseq
        self.wpos = 0
        self.wissued = 0
        self.wstage_end = []
        acc_ = 0
        for stg in self.stages:
            acc_ += {"gla": 8, "diff": 16}.get(stg, 30)
            self.wstage_end.append(acc_ - 1)

        with ExitStack() as st:
            self.st = st
            S = Sched(nc, st)
            self.S = S
            self.X = self.sb("X", [128, NT, D], F32)
            self.RX = [Res("X%d" % t) for t in range(NT)]
            self.ident = self.sb("ident", [128, 128], BF16)
            self.tri = self.sb("tri", [128, 128], BF16)
            self.U32 = self.sb("U32", [128, 128], F32)
            self.ones32 = self.sb("ones32", [128, 128], F32)
            self.Rconst = Res("const")
            self.gain = self.sb("gain", [128, D], F32)
            self.Rgain = Res("gain")
            self.ss = self.sb("ss", [128, NT], F32)
            self.lnss = self.sb("lnss", [128, NT], F32)
            self.rstd = self.sb("rstd", [128, NT], F32)
            self.Rss, self.Rln, self.Rrstd = Res("ss"), Res("lnss"), Res("rstd")
            self.banks = [st.enter_context(nc.psum_tensor("bk%d" % i, [128, 512], F32)) for i in range(8)]
            self.Rbk = [Res("bk%d" % i, excl=True) for i in range(8)]
            self.gen_banks = list(range(8))

            Rc = self.Rconst
            S.op("pool", lambda h: h.memset(self.ident[:, :], 1.0), writes=[Rc])
            S.op("pool", lambda h: h.affine_select(out=self.ident[:, :], in_=self.ident[:, :], pattern=[[1, 128]],
                                                    compare_op=ALU.is_equal, fill=0.0, base=0,
                                                    channel_multiplier=-1), reads=[Rc], writes=[Rc])
            S.op("pool", lambda h: h.memset(self.tri[:, :], 1.0), reads=[Rc], writes=[Rc])
            S.op("pool", lambda h: h.affine_select(out=self.tri[:, :], in_=self.tri[:, :], pattern=[[1, 128]],
                                                    compare_op=ALU.is_ge, fill=0.0, base=0,
                                                    channel_multiplier=-1), reads=[Rc], writes=[Rc])
            S.op("pool", lambda h: h.memset(self.U32[:, :], 1.0), reads=[Rc], writes=[Rc])
            S.op("pool", lambda h: h.affine_select(out=self.U32[:, :], in_=self.U32[:, :], pattern=[[1, 128]],
                                                    compare_op=ALU.is_ge, fill=0.0, base=0,
                                                    channel_multiplier=-1), reads=[Rc], writes=[Rc])
            S.op("pool", lambda h: h.memset(self.ones32[:, :], 1.0), reads=[Rc], writes=[Rc])

            for t in range(NT):
                S.dma("sp", self.X[:, t, :], dr["x"][t * 128:(t + 1) * 128, :], writes=[self.RX[t]],
                      owner=self.RX[t])

            for stg in self.stages:
                if stg == "gla":
                    self.gla_stage()
                elif stg == "diff":
                    self.diff_stage()
                else:
                    self.ffn_stage(int(stg[-1]))

            self.final_stage(out)
            S.finish()
            S.cleanup()
        self.stats = {k: (e.count, e.nwait) for k, e in S.engs.items()}
        self.stats["ndma"] = S.ndma
        return nc

    def norm_alloc(self, ls):
        junk = self.sb("junk", [128, D], BF16, ls)
        hbf = [self.sb("hbf", [128, D], BF16, ls) for _ in range(2)]
        return (junk, Res("junk"), hbf, [Res("hbf0"), Res("hbf1")])

    def norm_stats(self, gidx, tiles, tmp, li0=0, load_gain=True, skip_sq=False):
        S = self.S
        junk, Rjunk, _, _ = tmp
        vecs = self.dr["vecs"]
        if load_gain:
            S.dma("sp", self.gain[:, :], vecs[0:1, gidx * D:(gidx + 1) * D].broadcast_to([128, D]),
                  writes=[self.Rgain], owner=self.Rgain)
        n = len(tiles)
        for k, t in enumerate(tiles):
            li = li0 + k
            if not skip_sq:
                self.act(junk[:, :], self.X[:, t, :], AF.Square, [self.RX[t]], [Rjunk, self.Rss],
                         accum_out=self.ss[:, li:li + 1])
        self.act(self.lnss[:, li0:li0 + n], self.ss[:, li0:li0 + n], AF.Ln, [self.Rss], [self.Rln],
                 scale=1.0 / D, bias=EPS)
        self.act(self.rstd[:, li0:li0 + n], self.lnss[:, li0:li0 + n], AF.Exp, [self.Rln], [self.Rrstd], scale=-0.5)

    def norm_tile(self, li, t, hT, RhT, tmp, part=None):
        S = self.S
        _, _, hbf, Rhbf = tmp
        hb, Rhb = hbf[li % 2], Rhbf[li % 2]
        if part in (None, "a"):
            S.op("dve", lambda h: h.scalar_tensor_tensor(out=hb[:, :], in0=self.X[:, t, :],
                                                          scalar=self.rstd[:, li:li + 1], in1=self.gain[:, :],
                                                          op0=ALU.mult, op1=ALU.mult),
                 [self.RX[t], self.Rrstd, self.Rgain], [Rhb])
        if part == "a":
            return
        bk, Rb = self.bank()
        bkb = bk[:, :].bitcast(BF16)
        for c in range(8):
            self.tr(bkb[:, c * 128:(c + 1) * 128], hb[:, c * 128:(c + 1) * 128], [Rhb], [Rb])
        S.op("act", lambda h: h.copy(out=hT[:, :, li * 128:(li + 1) * 128],
                                      in_=bkb.rearrange("p (c t) -> p c t", c=8)), [Rb], [RhT[li]])

    def norm_stage(self, gidx, tiles, hT, RhT, ls, batch=16):
        tmp = self.norm_alloc(ls)
        skip_sq = getattr(self, "sq_ready", False) and len(tiles) == NT
        self.sq_ready = False
        for b0 in range(0, len(tiles), batch):
            self.norm_stats(gidx, tiles[b0:b0 + batch], tmp, li0=b0, load_gain=(b0 == 0), skip_sq=skip_sq)
            for k, t in enumerate(tiles[b0:b0 + batch]):
                self.norm_tile(b0 + k, t, hT, RhT, tmp)
        return tmp

    def ffn_stage(self, l):
        S = self.S
        with ExitStack() as ls:
            self.open_ring(ls, 5632)
            hTs = [self.sb("hTf", [128, 8, 1024], BF16, ls) for _ in range(2)]
            RhTs = [[Res("hTf%d_%d" % (h_, i)) for i in range(8)] for h_ in range(2)]
            aT = self.sb("aT", [128, NF, 1024], BF16, ls)
            RaT = [Res("aT0"), Res("aT1")]
            sg = [self.sb("sg", [128, 512], BF16, ls) for _ in range(2)]
            Rsg = [Res("sg0"), Res("sg1")]
            ntmp = self.norm_alloc(ls)
            early_final = self.final and self.last_stage == "ffn%d" % l
            si = self.stages.index("ffn%d" % l)
            early_next = (not early_final) and si + 1 < len(self.stages) and self.stages[si + 1] in ("gla", "diff")

            def early_sq(t):
                self.act(ntmp[0][:, :], self.X[:, t, :], AF.Square, [self.RX[t]], [ntmp[1], self.Rss],
                         accum_out=self.ss[:, t:t + 1])
            if early_final:
                self.final_alloc(ls)
            self.norm_stats(2 + l, list(range(0, 8)), ntmp, skip_sq=getattr(self, "sq_ready8", False))
            self.sq_ready8 = False
            for li in range(8):
                self.norm_tile(li, li, hTs[0], RhTs[0], ntmp)
            k = 0
            for hf in range(2):
                hT, RhT = hTs[hf], RhTs[hf]
                for fb in range(11):
                    if hf == 1 and early_next and fb == 1:
                        for t_ in range(8):
                            early_sq(t_)
                    if hf == 1 and early_final:
                        if fb == 1:
                            self.final_stats(list(range(0, 8)))
                        if 2 <= fb < 10:
                            self.final_tile(fb - 2)
                            self.final_done.add(fb - 2)
                    if hf == 0:
                        if fb == 1:
                            self.norm_stats(2 + l, list(range(8, 16)), ntmp)
                        if 3 <= fb < 11:
                            self.norm_tile(fb - 3, 8 + fb - 3, hTs[1], RhTs[1], ntmp, part="b")
                        if 2 <= fb < 10:
                            self.norm_tile(fb - 2, 8 + fb - 2, hTs[1], RhTs[1], ntmp, part="a")
                    W, RW = self.wnext()
                    Wv = W[:, 0:4096].rearrange("p (c g n) -> p c g n", c=8, g=2)
                    for tg in range(2):
                        rh = RhT[tg * 4:(tg + 1) * 4]
                        for j in range(2):
                            bg, Rbg = self.bank()
                            bu, Rbu = self.bank()
                            for c in range(8):
                                self.mm(bg[:, :], Wv[:, c, 0, j * 128:(j + 1) * 128],
                                        hT[:, c, tg * 512:(tg + 1) * 512], c == 0, c == 7, [RW] + rh, [Rbg])
                            for c in range(8):
                                self.mm(bu[:, :], Wv[:, c, 1, j * 128:(j + 1) * 128],
                                        hT[:, c, tg * 512:(tg + 1) * 512], c == 0, c == 7, [RW] + rh, [Rbu])
                            s_, Rs_ = sg[k % 2], Rsg[k % 2]
                            k += 1
                            self.act(s_[:, :], bg[:, :], AF.Silu, [Rbg], [Rs_])
                            S.op("dve", lambda h: h.tensor_tensor(out=aT[:, fb * 2 + j, tg * 512:(tg + 1) * 512],
                                                                   in0=bu[:, :], in1=s_[:, :], op=ALU.mult),
                                 [Rbu, Rs_], [RaT[tg]])
                if hf == 1 and (early_final or early_next):
                    Ws = []
                    for nq in range(4):
                        W, RW = self.wnext(hold=nq + 1)
                        Ws.append((W[:, 0:5632].rearrange("p (f n) -> p f n", f=NF), RW))
                    prev_t = None
                    for tl in range(8):
                        t = 8 * hf + tl
                        for nq in range(4):
                            Wv, RW = Ws[nq]
                            bk, Rb = self.bank()
                            for f in range(NF):
                                self.mm(bk[:, 0:256], aT[:, f, tl * 128:(tl + 1) * 128], Wv[:, f, :],
                                        f == 0, f == NF - 1, [RaT[tl // 4], RW], [Rb])
                            xs = self.X[:, t, nq * 256:(nq + 1) * 256]
                            S.op("dve", lambda h: h.tensor_tensor(out=xs, in0=bk[:, 0:256], in1=xs, op=ALU.add),
                                 [Rb, self.RX[t]], [self.RX[t]])
                        if prev_t is not None:
                            if early_final:
                                self.final_stats([prev_t], load_gain=False)
                                self.final_tile(prev_t)
                                self.final_done.add(prev_t)
                            else:
                                early_sq(prev_t)
                        prev_t = t
                    if early_final:
                        self.final_stats([prev_t], load_gain=False)
                        self.final_tile(prev_t)
                        self.final_done.add(prev_t)
                    else:
                        early_sq(prev_t)
                        self.sq_ready = True
                else:
                    for nq in range(4):
                        W, RW = self.wnext()
                        Wv = W[:, 0:5632].rearrange("p (f n) -> p f n", f=NF)
                        for tl in range(8):
                            t = 8 * hf + tl
                            bk, Rb = self.bank()
                            for f in range(NF):
                                self.mm(bk[:, 0:256], aT[:, f, tl * 128:(tl + 1) * 128], Wv[:, f, :],
                                        f == 0, f == NF - 1, [RaT[tl // 4], RW], [Rb])
                            xs = self.X[:, t, nq * 256:(nq + 1) * 256]
                            S.op("dve", lambda h: h.tensor_tensor(out=xs, in0=bk[:, 0:256], in1=xs, op=ALU.add),
                                 [Rb, self.RX[t]], [self.RX[t]])
            S.barrier()

    def gla_stage(self):
        S, nc, dr = self.S, self.nc, self.dr
        with ExitStack() as ls:
            self.open_ring(ls, 6144)
            hT = self.sb("hT", [128, 8, SEQ], BF16, ls)
            RhT = [Res("hT%d" % i) for i in range(NT)]
            gtmp = self.norm_stage(0, list(range(NT)), hT, RhT, ls)
            gi = self.stages.index("gla")
            gla_early = gi + 1 < len(self.stages) and self.stages[gi + 1].startswith("ffn")
            wa = self.sb("wa", [128, 128], BF16, ls)
            wb = self.sb("wb", [33, 512], BF16, ls)
            gnb = self.sb("gnb", [128, HV], F32, ls)
            Rwa, Rwb, Rgnb = Res("wa"), Res("wb"), Res("gnb")
            S.dma("pool", wa[:, :], dr["w_gla_a"][:, :], writes=[Rwa], owner=Rwa)
            S.dma("pool", wb[:, :], dr["w_gla_b"][:, :], writes=[Rwb], owner=Rwb)
            S.dma("sp", gnb[:, :], dr["vecs"][0:1, V_GLAGN:V_GLAGN + HV].broadcast_to([128, HV]),
                  writes=[Rgnb], owner=Rgnb)
            uaug = self.sb("uaug", [33, SEQ], BF16, ls)
            Ru = Res("uaug")
            S.op("dve", lambda h: h.memset(uaug[:, :], 0.0), writes=[Ru])
            S.op("dve", lambda h: h.memset(uaug[32:33, :], 1.0), reads=[Ru], writes=[Ru])
            for tg in range(4):
                bk, Rb = self.bank()
                for c in range(8):
                    self.mm(bk[0:16, :], wa[:, c * 16:(c + 1) * 16], hT[:, c, tg * 512:(tg + 1) * 512],
                            c == 0, c == 7, [Rwa] + RhT[tg * 4:(tg + 1) * 4], [Rb])
                S.op("act", lambda h: h.copy(out=uaug[0:16, tg * 512:(tg + 1) * 512], in_=bk[0:16, :]), [Rb], [Ru])

            sr = [self.sb("sr", [128, 2, SEQ], BF16, ls) for _ in range(2)]
            Rsr = [[Res("sr%d_%d" % (b_, i)) for i in range(4)] for b_ in range(2)]
            tmpA = self.sb("tmpA", [128, 512], F32, ls)
            tmpB = self.sb("tmpB", [128, 512], F32, ls)
            RtA, RtB = Res("tmpA"), Res("tmpB")
            dd = [self.sb("dd", [128, 4], F32, ls) for _ in range(2)]
            Rdd = [Res("dd0"), Res("dd1")]
            qs = [self.sb("qs", [128, 512], BF16, ls) for _ in range(2)]
            ks = [self.sb("ks", [128, 512], BF16, ls) for _ in range(2)]
            kh = [self.sb("kh", [128, 512], BF16, ls) for _ in range(2)]
            kht = [self.sb("kht", [128, 512], BF16, ls) for _ in range(2)]
            vsb = [self.sb("vsb", [128, 1024], BF16, ls) for _ in range(2)]
            Rqs, Rks, Rkh, Rkht = ([Res(n + "0"), Res(n + "1")] for n in ("qs", "ks", "kh", "kht"))
            Rv = [[Res("v%d_%d" % (b_, i)) for i in range(2)] for b_ in range(2)]
            at = [self.sb("at", [128, 128], BF16, ls) for _ in range(2)]
            Rat = [Res("at0"), Res("at1")]
            Sst = self.sb("Sst", [128, HV], F32, ls)
            RS = Res("S")
            Sbf = [self.sb("Sbf", [128, HV], BF16, ls) for _ in range(2)]
            RSbf = [Res("Sbf0"), Res("Sbf1")]
            sso = [self.sb("sso", [128, 4], F32, ls) for _ in range(4)]
            Rsso = [Res("sso%d" % i) for i in range(4)]
            go = [self.sb("go", [128, 2, 128], BF16, ls) for _ in range(4)]
            Rgo = [Res("go%d" % i) for i in range(4)]
            sq = [self.sb("sq", [128, HV], BF16, ls) for _ in range(4)]
            Rsq = [Res("sq%d" % i) for i in range(4)]
            gcol = self.sb("gcol", [128, 2], F32, ls)
            Rgcol = Res("gcol")
            for jv in range(2):
                S.dma("sp", gcol[:, jv:jv + 1],
                      dr["vecs"][0:1, V_GLAGN + jv * 128:V_GLAGN + (jv + 1) * 128].rearrange("o (p q) -> (o p) q", q=1),
                      writes=[Rgcol], owner=Rgcol)
            ysb = [self.sb("ysb", [128, 512], F32, ls) for _ in range(2)]
            Rysb = [Res("ysb0"), Res("ysb1")]
            ycnt = [0]
            onesb = self.sb("onesb", [128, 2], BF16, ls)
            S.op("dve", lambda h: h.memset(onesb[:, :], 1.0), reads=[self.Rconst], writes=[self.Rconst])

            step = [0]
            due = []

            def defer(k, fn):
                if os.environ.get('DBG_NODEFER'):
                    fn()
                    return
                due.append((step[0] + k, fn))

            def run_due(final=False):
                rest = []
                for d_, fn in due:
                    if final or d_ <= step[0]:
                        fn()
                    else:
                        rest.append((d_, fn))
                due[:] = rest

            kk = [0]
            cur = [0]

            def make_head(hd, Whv, RWh):
                hb = hd % 2
                def vproj(tg, i2):
                    b2 = tg % 2
                    bv, Rbv = self.bank()
                    for ii in range(2):
                        t = tg * 4 + i2 * 2 + ii
                        for c in range(8):
                            self.mm(bv[:, ii * 256:(ii + 1) * 256], hT[:, c, t * 128:(t + 1) * 128],
                                    Whv[:, c, 256:512], c == 0, c == 7, [RWh, RhT[t]], [Rbv])
                    S.op("act", lambda h: h.copy(out=vsb[b2][:, i2 * 512:(i2 + 1) * 512], in_=bv[:, :]),
                         [Rbv], [Rv[b2][i2]])

                def G1(tg):
                    bx, Rbx = self.bank()
                    for i in range(4):
                        t = tg * 4 + i
                        self.mm(bx[:, i * 128:(i + 1) * 128], uaug[0:33, t * 128:(t + 1) * 128],
                                wb[0:33, hd * 128:(hd + 1) * 128], True, True, [Ru, Rwb], [Rbx])
                    self.act(tmpA[:, :], bx[:, :], AF.Exp, [Rbx], [RtA], scale=-1.0)
                    self.act(tmpB[:, :], tmpA[:, :], AF.Ln, [RtA], [RtB], bias=1.0)
                    if not os.environ.get('DBG_VLATE'):
                        vproj(tg, 0)

                def G2(tg):
                    b2 = tg % 2
                    bc, Rbc = self.bank()
                    for i in range(4):
                        self.mm(bc[:, i * 128:(i + 1) * 128], tmpB[:, i * 128:(i + 1) * 128], self.U32[:, :],
                                True, True, [RtB, self.Rconst], [Rbc])
                    self.act(tmpA[:, :], bc[:, :], AF.Exp, [Rbc], [RtA], scale=-1.0 / 16)
                    self.act(tmpB[:, :], bc[:, :], AF.Exp, [Rbc], [RtB], scale=1.0 / 16)
                    S.op("dve", lambda h: h.tensor_copy(
                        out=dd[b2][:, :], in_=tmpA[:, :].rearrange("p (i t) -> p i t", i=4)[:, :, 127]),
                         [RtA], [Rdd[b2]])
                    if not os.environ.get('DBG_VLATE'):
                        vproj(tg, 1)

                def G3(tg):
                    b2 = tg % 2
                    rh = RhT[tg * 4:(tg + 1) * 4]
                    bq, Rbq = self.bank()
                    bkk, Rbkk = self.bank()
                    for c in range(8):
                        self.mm(bq[:, :], Whv[:, c, 0:128], hT[:, c, tg * 512:(tg + 1) * 512], c == 0, c == 7,
                                [RWh] + rh, [Rbq])
                    for c in range(8):
                        self.mm(bkk[:, :], Whv[:, c, 128:256], hT[:, c, tg * 512:(tg + 1) * 512], c == 0, c == 7,
                                [RWh] + rh, [Rbkk])
                    S.op("dve", lambda h: h.scalar_tensor_tensor(out=qs[b2][:, :], in0=bq[:, :], scalar=HK ** -0.5,
                                                                  in1=tmpA[:, :], op0=ALU.mult, op1=ALU.mult),
                         [Rbq, RtA], [Rqs[b2]])
                    S.op("dve", lambda h: h.tensor_tensor(out=ks[b2][:, :], in0=bkk[:, :], in1=tmpB[:, :],
                                                           op=ALU.mult), [Rbkk, RtB], [Rks[b2]])
                    for i in range(4):
                        S.op("dve", lambda h: h.scalar_tensor_tensor(
                            out=kh[b2][:, i * 128:(i + 1) * 128], in0=bkk[:, i * 128:(i + 1) * 128],
                            scalar=dd[b2][:, i:i + 1], in1=tmpB[:, i * 128:(i + 1) * 128],
                            op0=ALU.mult, op1=ALU.mult), [Rbkk, Rdd[b2], RtB], [Rkh[b2]])

                def G4(tg):
                    b2 = tg % 2
                    bt, Rbt = self.bank()
                    btb = bt[:, :].bitcast(BF16)
                    for i in range(4):
                        self.tr(btb[:, i * 128:(i + 1) * 128], kh[b2][:, i * 128:(i + 1) * 128], [Rkh[b2]], [Rbt])
                    S.op("act", lambda h: h.copy(out=kht[b2][:, :], in_=btb[:, 0:512]), [Rbt], [Rkht[b2]])
                    if os.environ.get('DBG_VLATE'):
                        vproj(tg, 0)
                        vproj(tg, 1)

                G = [G1, G2, G3, G4]

                def rpiece(tg, j):
                    rh = RhT[tg * 4:(tg + 1) * 4]
                    bk, Rb = self.bank()
                    for c in range(8):
                        self.mm(bk[:, :], Whv[:, c, 512 + j * 128:512 + (j + 1) * 128],
                                hT[:, c, tg * 512:(tg + 1) * 512], c == 0, c == 7, [RWh] + rh, [Rb])
                    self.act(sr[hb][:, j, tg * 512:(tg + 1) * 512], bk[:, :], AF.Silu, [Rb], [Rsr[hb][tg]])

                rp = [(lambda tg=tg, j=j: rpiece(tg, j)) for tg in range(4) for j in range(2)]
                return G, rp

            Wh, RWh = self.wnext(hold=2)
            heads = {0: make_head(0, Wh[:, 0:6144].rearrange("p (c n) -> p c n", c=8), RWh)}
            for pc in heads[0][1]:
                pc()
            for g_ in heads[0][0]:
                g_(0)
            for hd in range(4):
                hb = hd % 2
                Wo, RWo = self.wnext(hold=3)
                Wov = Wo[:, 0:2048].rearrange("p (j n) -> p j n", j=2)
                G = heads[hd][0]
                S.op("dve", lambda h: h.memset(Sst[:, :], 0.0), writes=[RS])
                S.op("dve", lambda h: h.memset(Sbf[cur[0]][:, :], 0.0), writes=[RSbf[cur[0]]])

                def chunk(tg, i, hb=hb, Wov=Wov, RWo=RWo, hd=hd):
                    t = tg * 4 + i
                    b2 = tg % 2
                    k4 = kk[0] % 4
                    p2 = kk[0] % 2
                    kk[0] += 1
                    q_, k_, kt_, v_, d_ = qs[b2], ks[b2], kht[b2], vsb[b2], dd[b2]
                    Rv_ = Rv[b2][i // 2]
                    ba, Rba = self.bank()
                    self.mm(ba[:, 0:128], k_[:, i * 128:(i + 1) * 128], q_[:, i * 128:(i + 1) * 128], True, True,
                            [Rks[b2], Rqs[b2]], [Rba])
                    a_ = at[p2]
                    if os.environ.get("ATMASK", "dve") == "pool":
                        S.op("act", lambda h: h.copy(out=a_[:, :], in_=ba[:, 0:128]), [Rba], [Rat[p2]])
                        S.op("pool", lambda h: h.tensor_tensor(out=a_[:, :], in0=a_[:, :], in1=self.tri[:, :],
                                                                op=ALU.mult), [Rat[p2], self.Rconst], [Rat[p2]])
                    else:
                        S.op("dve", lambda h: h.tensor_tensor(out=a_[:, :], in0=ba[:, 0:128], in1=self.tri[:, :],
                                                               op=ALU.mult), [Rba, self.Rconst], [Rat[p2]])
                    bs, Rbs = self.bank()
                    self.mm(bs[:, 0:256], kt_[:, i * 128:(i + 1) * 128], v_[:, i * 256:(i + 1) * 256], True, True,
                            [Rkht[b2], Rv_], [Rbs])
                    bo, Rbo = self.bank()
                    c_ = cur[0]
                    for jv in range(2):
                        self.mm(bo[:, jv * 128:(jv + 1) * 128], Sbf[c_][:, jv * 128:(jv + 1) * 128],
                                q_[:, i * 128:(i + 1) * 128], True, False, [Rqs[b2], RSbf[c_]], [Rbo])
                        self.mm(bo[:, jv * 128:(jv + 1) * 128], v_[:, i * 256 + jv * 128:i * 256 + (jv + 1) * 128],
                                a_[:, :], False, True, [Rat[p2], Rv_], [Rbo])
                    nxt = 1 - c_
                    S.op("dve", lambda h: h.scalar_tensor_tensor(out=Sst[:, :], in0=Sst[:, :],
                                                                  scalar=d_[:, i:i + 1], in1=bs[:, 0:256],
                                                                  op0=ALU.mult, op1=ALU.add),
                         [RS, Rdd[b2], Rbs], [RS])
                    S.op(os.environ.get("SBF_ENG", "pool"), lambda h: (h.copy if hasattr(h, "copy") else h.tensor_copy)(
                        out=Sbf[nxt][:, :], in_=Sst[:, :]), [RS], [RSbf[nxt]])
                    cur[0] = nxt
                    s_, Rs_ = sso[k4], Rsso[k4]
                    self.act(sq[k4][:, :], bo[:, 0:256], AF.Square, [Rbo], [Rsq[k4]])
                    for jv in range(2):
                        S.op("dve", lambda h: h.scalar_tensor_tensor(
                            out=go[k4][:, jv, :], in0=bo[:, jv * 128:(jv + 1) * 128], scalar=gcol[:, jv:jv + 1],
                            in1=sr[hb][:, jv, t * 128:(t + 1) * 128], op0=ALU.mult, op1=ALU.mult),
                             [Rbo, Rgcol, Rsr[hb][tg]], [Rgo[k4]])

                    def n1():
                        bq2, Rbq2 = self.bank()
                        for jv in range(2):
                            self.mm(bq2[:, 0:2], sq[k4][:, jv * 128:(jv + 1) * 128], onesb[:, 0:2], jv == 0, jv == 1,
                                    [Rsq[k4], self.Rconst], [Rbq2])
                        self.act(s_[:, 1:2], bq2[:, 0:1], AF.Ln, [Rbq2], [Rs_], scale=1.0 / HV, bias=EPS)
                        self.act(s_[:, 2:3], s_[:, 1:2], AF.Exp, [Rs_], [Rs_], scale=-0.5)

                    def n4():
                        for nh in range(2):
                            by, Rby = self.bank()
                            for j in range(2):
                                self.mm(by[:, :], go[k4][:, j, :], Wov[:, j, nh * 512:(nh + 1) * 512], j == 0, j == 1,
                                        [Rgo[k4], RWo], [Rby])
                            xs = self.X[:, t, nh * 512:(nh + 1) * 512]
                            if os.environ.get("GLA_XADD", "dve") == "pool":
                                yk = ycnt[0] % 2
                                ycnt[0] += 1
                                self.act(ysb[yk][:, :], by[:, :], AF.Copy, [Rby, Rs_], [Rysb[yk]], scale=s_[:, 2:3])
                                S.op("pool", lambda h: h.tensor_tensor(out=xs, in0=xs, in1=ysb[yk][:, :], op=ALU.add),
                                     [Rysb[yk], self.RX[t]], [self.RX[t]])
                            else:
                                S.op("dve", lambda h: h.scalar_tensor_tensor(out=xs, in0=by[:, :], scalar=s_[:, 2:3],
                                                                              in1=xs, op0=ALU.mult, op1=ALU.add),
                                     [Rby, Rs_, self.RX[t]], [self.RX[t]])
                        if gla_early and hd == 3 and t < 8:
                            self.act(gtmp[0][:, :], self.X[:, t, :], AF.Square, [self.RX[t]], [gtmp[1], self.Rss],
                                     accum_out=self.ss[:, t:t + 1])
                            self.sq_ready8 = True

                    defer(1, n1)
                    defer(2, n4)

                nG, nR = None, []
                for tg in range(4):
                    if tg == 3 and hd + 1 < 4:
                        Wn, RWn = self.wnext(hold=2)
                        heads[hd + 1] = make_head(hd + 1, Wn[:, 0:6144].rearrange("p (c n) -> p c n", c=8), RWn)
                        nG, nR = heads[hd + 1][0], list(heads[hd + 1][1])
                    for i in range(4):
                        if tg < 3:
                            G[i](tg + 1)
                        elif nG is not None:
                            for _ in range(2):
                                if nR:
                                    nR.pop(0)()
                            nG[i](0)
                        chunk(tg, i)
                        step[0] += 1
                        run_due()
            run_due(final=True)
            S.barrier()

    def diff_stage(self):
        S, nc, dr = self.S, self.nc, self.dr
        vecs = dr["vecs"]
        with ExitStack() as ls:
            self.open_ring(ls, 3072)
            hT = self.sb("hT", [128, 8, SEQ], BF16, ls)
            RhT = [Res("hT%d" % i) for i in range(NT)]
            with ExitStack() as ns:
                self.norm_stage(1, list(range(NT)), hT, RhT, ns)
                S.barrier()
            lamv = self.sb("lamv", [1, 256], F32, ls)
            lt = self.sb("lt", [1, 128 + 8], F32, ls)
            Rlam = Res("lam")
            S.dma("sp", lamv[:, :], vecs[0:1, V_LAM:V_LAM + 256], writes=[Rlam], owner=Rlam)
            S.op("dve", lambda h: h.tensor_tensor(out=lt[:, 0:128], in0=lamv[:, 0:128], in1=lamv[:, 128:256],
                                                   op=ALU.mult), [Rlam], [Rlam])
            S.op("dve", lambda h: h.tensor_reduce(out=lt[:, 128:130],
                                                   in_=lt[:, 0:128].rearrange("p (a d) -> p a d", a=2),
                                                   axis=AX.X, op=ALU.add), [Rlam], [Rlam])
            self.act(lt[:, 130:132], lt[:, 128:130], AF.Exp, [Rlam], [Rlam])
            S.op("dve", lambda h: h.tensor_tensor(out=lt[:, 132:133], in0=lt[:, 130:131], in1=lt[:, 131:132],
                                                   op=ALU.subtract), [Rlam], [Rlam])
            for cc in (134, 135):
                S.op("dve", lambda h: h.tensor_scalar(out=lt[:, cc:cc + 1], in0=lt[:, 132:133],
                                                       scalar1=LAMBDA_INIT, scalar2=-1.0, op0=ALU.add, op1=ALU.mult),
                     [Rlam], [Rlam])
            bk, Rb = self.bank()
            self.mm(bk[:, 0:2], self.ones32[0:1, :], lt[0:1, 134:136], True, True, [self.Rconst, Rlam], [Rb])
            neglam = self.sb("neglam", [128, 2], F32, ls)
            Rnl = Res("neglam")
            S.op("dve", lambda h: h.tensor_copy(out=neglam[:, :], in_=bk[:, 0:2]), [Rb], [Rnl])
            gn2 = self.sb("gn2", [128, 128], F32, ls)
            Rgn2 = Res("gn2")
            S.dma("sp", gn2[:, :], vecs[0:1, V_DIFFGN:V_DIFFGN + 128].broadcast_to([128, 128]), writes=[Rgn2],
                  owner=Rgn2)
            S.op("dve", lambda h: h.tensor_scalar(out=gn2[:, :], in0=gn2[:, :], scalar1=1.0 - LAMBDA_INIT,
                                                   scalar2=None, op0=ALU.mult), [Rgn2], [Rgn2])

            qz = [self.sb("qz", [128, 8, 2, 256], BF16, ls) for _ in range(2)]
            kT = [self.sb("kT", [128, SEQ], BF16, ls) for _ in range(2)]
            va = [self.sb("va", [128, NT, 130], BF16, ls) for _ in range(2)]
            vb = [self.sb("vb", [128, NT, 130], BF16, ls) for _ in range(2)]
            Rq, Rk, Rva, Rvb = ([Res(n + "0"), Res(n + "1")] for n in ("qz", "kT", "va", "vb"))
            for b_ in range(2):
                S.op("dve", lambda h: h.memset(va[b_][:, :, 128:130], 1.0), writes=[Rva[b_]])
                S.op("dve", lambda h: h.memset(vb[b_][:, :, 128:130], 1.0), writes=[Rvb[b_]])
                S.op("dve", lambda h: h.memset(qz[b_][64:128, :, 0, :], 0.0), writes=[Rq[b_]])
                S.op("dve", lambda h: h.memset(qz[b_][0:64, :, 1, :], 0.0), reads=[Rq[b_]], writes=[Rq[b_]])
            NPT = 2
            pT = [self.sb("pT", [128, NT, 512], BF16, ls) for _ in range(NPT)]
            RpT = [[Res("pT%d_%d" % (b_, i)) for i in range(NT)] for b_ in range(NPT)]
            rl = [self.sb("rl", [128, 2], F32, ls) for _ in range(4)]
            Rrl = [Res("rl%d" % i) for i in range(4)]
            t2 = [self.sb("t2", [128, 128], F32, ls) for _ in range(2)]
            Rt2 = [Res("t20"), Res("t21")]
            of = [self.sb("of", [128, 128], F32, ls) for _ in range(4)]
            Rof = [Res("of%d" % i) for i in range(4)]
            junk3 = self.sb("junk3", [128, 128], BF16, ls)
            Rj3 = Res("junk3")
            sso = [self.sb("ssd", [128, 8], F32, ls) for _ in range(2)]
            Rsso = [Res("ssd0"), Res("ssd1")]
            on = [self.sb("ond", [128, 128], BF16, ls) for _ in range(4)]
            Ron = [Res("ond%d" % i) for i in range(4)]
            oT = [self.sb("oTd", [128, 2, 128], BF16, ls) for _ in range(2)]
            RoT = [Res("oTd0"), Res("oTd1")]

            oTprev = self.gain[:, :].bitcast(BF16)
            RoTp = [self.Rgain] * 8
            genA = [0, 1, 2, 3, 4, 5]
            uu = 0
            step = [0]
            due = []

            def defer(k, fn):
                due.append((step[0] + k, fn))

            def run_due(final=False):
                rest = []
                for d_, fn in due:
                    if final or d_ <= step[0]:
                        fn()
                    else:
                        rest.append((d_, fn))
                due[:] = rest

            def proj_pieces(hd, W, RW):
                hb = hd % 2
                Wv = W[:, 0:3072].rearrange("p (c n) -> p c n", c=8)
                q_, k_, v_, v1_ = qz[hb], kT[hb], va[hb], vb[hb]
                pcs = []

                def pq(tg):
                    rh = RhT[tg * 4:(tg + 1) * 4]
                    bq, Rbq = self.bank(genA)
                    for c in range(8):
                        self.mm(bq[:, :], Wv[:, c, 0:128], hT[:, c, tg * 512:(tg + 1) * 512], c == 0, c == 7,
                                [RW] + rh, [Rbq])
                    self.act(q_[0:64, 2 * tg:2 * tg + 2, 0, :], bq[0:64, :].rearrange("p (g n) -> p g n", g=2),
                             AF.Copy, [Rbq], [Rq[hb]], scale=0.125)
                    S.op("dve", lambda h: h.tensor_scalar(
                        out=q_[64:128, 2 * tg:2 * tg + 2, 1, :], in0=bq[64:128, :].rearrange("p (g n) -> p g n", g=2),
                        scalar1=0.125, scalar2=None, op0=ALU.mult), [Rbq], [Rq[hb]])

                def pk(tg):
                    rh = RhT[tg * 4:(tg + 1) * 4]
                    bkk, Rbkk = self.bank(genA)
                    for c in range(8):
                        self.mm(bkk[:, :], Wv[:, c, 128:256], hT[:, c, tg * 512:(tg + 1) * 512], c == 0, c == 7,
                                [RW] + rh, [Rbkk])
                    S.op("dve", lambda h: h.tensor_copy(out=k_[:, tg * 512:(tg + 1) * 512], in_=bkk[:, :]),
                         [Rbkk], [Rk[hb]])

                def pv(t4):
                    bv, Rbv = self.bank(genA)
                    for ii in range(4):
                        t = t4 * 4 + ii
                        for c in range(8):
                            self.mm(bv[:, ii * 128:(ii + 1) * 128], hT[:, c, t * 128:(t + 1) * 128],
                                    Wv[:, c, 256:384], c == 0, c == 7, [RW, RhT[t]], [Rbv])
                    bv3 = bv[:, :].rearrange("p (i n) -> p i n", i=4)
                    S.op("dve", lambda h: h.tensor_copy(out=v_[:, t4 * 4:(t4 + 1) * 4, 0:128], in_=bv3), [Rbv], [Rva[hb]])
                    S.op("dve", lambda h: h.tensor_scalar(out=v1_[:, t4 * 4:(t4 + 1) * 4, 0:128], in0=bv3,
                                                           scalar1=neglam[:, 0:1], scalar2=None, op0=ALU.mult),
                         [Rbv, Rnl], [Rvb[hb]])

                for tg in range(4):
                    pcs.append(lambda tg=tg: pq(tg))
                    pcs.append(lambda tg=tg: pk(tg))
                    pcs.append(lambda tg=tg: pv(tg))
                return pcs

            nheads = int(os.environ.get('DBG_HEADS', '8'))
            di_ = self.stages.index("diff")
            diff_early = di_ + 1 < len(self.stages) and self.stages[di_ + 1].startswith("ffn")
            pieces = []
            for hd in range(nheads):
                hb = hd % 2
                if hd == 0:
                    W, RW = self.wnext(hold=2)
                    for pc in proj_pieces(0, W, RW):
                        pc()
                while pieces:
                    pieces.pop(0)()
                run_due(final=True)
                if hd % 2 == 1:
                    prevWo = (Wo, RWo)
                    Wo, RWo = self.wnext(hold=3)
                else:
                    prevWo = (None, None)
                    Wo, RWo = self.wnext(hold=2)
                q_, k_, v_, v1_ = qz[hb], kT[hb], va[hb], vb[hb]

                units = list(range(int(os.environ.get('DBG_G2', '8'))))

                def st_parts(g2, ub, hb=hb, q_=q_, k_=k_):
                    p_ = pT[ub]
                    parts = []
                    qrhs = q_[:, g2, :, :].rearrange("p c n -> p (c n)")

                    def one(j):
                        bs_, Rbs_ = self.bank(genA)
                        self.mm(bs_[:, :], k_[:, j * 128:(j + 1) * 128], qrhs, True, True, [Rk[hb], Rq[hb]], [Rbs_])
                        self.act(p_[:, j, :], bs_[:, :], AF.Exp, [Rbs_], [RpT[ub][j]])
                        if j >= 2 * g2:
                            qi = j - 2 * g2
                            for c in range(2):
                                blk = p_[:, j, c * 256 + qi * 128:c * 256 + (qi + 1) * 128]
                                S.op(os.environ.get("MASK_ENG", "pool"),
                                     lambda h: h.tensor_tensor(out=blk, in0=blk, in1=self.tri[:, :], op=ALU.mult),
                                     [RpT[ub][j], self.Rconst], [RpT[ub][j]])

                    for j in range(2 * g2 + 2):
                        parts.append(lambda j=j: one(j))
                    return parts

                def chain1(g2):
                    for qi in range(2):
                        i = 2 * g2 + qi
                        bi = 6 + (i % 2)
                        acc, Racc = self.banks[bi], self.Rbk[bi]
                        r_, Rr_ = rl[i % 4], Rrl[i % 4]
                        S.op("dve", lambda h: h.reciprocal(
                            out=r_[:, 0:2], in_=acc[:, 0:260].rearrange("p (c n) -> p c n", c=2)[:, :, 128]),
                             [Racc], [Rr_])
                        S.op("dve", lambda h: h.tensor_scalar(out=t2[qi][:, :], in0=acc[:, 130:258],
                                                               scalar1=r_[:, 1:2], scalar2=None, op0=ALU.mult),
                             [Racc, Rr_], [Rt2[qi]])
                        S.op("dve", lambda h: h.scalar_tensor_tensor(out=of[i % 4][:, :], in0=acc[:, 0:128],
                                                                      scalar=r_[:, 0:1], in1=t2[qi][:, :],
                                                                      op0=ALU.mult, op1=ALU.add),
                             [Racc, Rr_, Rt2[qi]], [Rof[i % 4]])

                def chain2(g2):
                    s_, Rs_ = sso[g2 % 2], Rsso[g2 % 2]
                    for qi in range(2):
                        i = 2 * g2 + qi
                        self.act(junk3[:, :], of[i % 4][:, :], AF.Square, [Rof[i % 4]], [Rj3, Rs_],
                                 accum_out=s_[:, qi:qi + 1])
                    self.act(s_[:, 2:4], s_[:, 0:2], AF.Ln, [Rs_], [Rs_], scale=1.0 / 128, bias=SUBLN_EPS)
                    self.act(s_[:, 4:6], s_[:, 2:4], AF.Exp, [Rs_], [Rs_], scale=-0.5)

                def chain3(g2):
                    s_, Rs_ = sso[g2 % 2], Rsso[g2 % 2]
                    for qi in range(2):
                        i = 2 * g2 + qi
                        if os.environ.get("CH3_ENG", "pool") == "pool":
                            S.op("pool", lambda h: h.tensor_tensor(out=of[i % 4][:, :], in0=of[i % 4][:, :],
                                                                    in1=gn2[:, :], op=ALU.mult),
                                 [Rof[i % 4], Rgn2], [Rof[i % 4]])
                            S.op("pool", lambda h: h.tensor_scalar(out=on[i % 4][:, :], in0=of[i % 4][:, :],
                                                                    scalar1=s_[:, 4 + qi:5 + qi], scalar2=1.0,
                                                                    op0=ALU.mult, op1=ALU.mult),
                                 [Rof[i % 4], Rs_], [Ron[i % 4]])
                        else:
                            S.op("dve", lambda h: h.scalar_tensor_tensor(out=on[i % 4][:, :], in0=of[i % 4][:, :],
                                                                          scalar=s_[:, 4 + qi:5 + qi], in1=gn2[:, :],
                                                                          op0=ALU.mult, op1=ALU.mult),
                                 [Rof[i % 4], Rs_, Rgn2], [Ron[i % 4]])

                even = (hd % 2 == 0)

                def epi_a(g2, even=even):
                    btr, Rbtr = self.bank(genA)
                    btrb = btr[:, :].bitcast(BF16)
                    for qi in range(2):
                        i = 2 * g2 + qi
                        self.tr(btrb[:, qi * 128:(qi + 1) * 128], on[i % 4][:, :], [Ron[i % 4]], [Rbtr])
                    if even:
                        S.op("act", lambda h: h.copy(out=oTprev[:, g2 * 256:(g2 + 1) * 256], in_=btrb[:, 0:256]),
                             [Rbtr], [RoTp[g2]])
                    else:
                        ot_, Rot_ = oT[g2 % 2], RoT[g2 % 2]
                        S.op("act", lambda h: h.copy(out=ot_[:, :, :],
                                                      in_=btrb[:, 0:256].rearrange("p (q t) -> p q t", q=2)),
                             [Rbtr], [Rot_])

                def epi_b(g2, qi, Wo=Wo, RWo=RWo, even=even, Wop=prevWo[0], RWop=prevWo[1],
                          lasthead=(hd == nheads - 1)):
                    if even:
                        return
                    ot_, Rot_ = oT[g2 % 2], RoT[g2 % 2]
                    i = 2 * g2 + qi
                    for nh in range(2):
                        by, Rby = self.bank(genA)
                        self.mm(by[:, :], oTprev[:, i * 128:(i + 1) * 128], Wop[:, nh * 512:(nh + 1) * 512], True, False,
                                [RoTp[g2], RWop], [Rby])
                        self.mm(by[:, :], ot_[:, qi, :], Wo[:, nh * 512:(nh + 1) * 512], False, True,
                                [Rot_, RWo], [Rby])
                        xs = self.X[:, i, nh * 512:(nh + 1) * 512]
                        S.op("dve", lambda h: h.tensor_tensor(out=xs, in0=by[:, :], in1=xs, op=ALU.add),
                             [Rby, self.RX[i]], [self.RX[i]])
                    if diff_early and lasthead and i < 8:
                        self.act(hT[:, 0, 0:1024], self.X[:, i, :], AF.Square,
                                 [self.RX[i]], RhT[0:8] + [self.Rss], accum_out=self.ss[:, i:i + 1])
                        self.sq_ready8 = True

                def pv_parts(g2, ub, hb=hb, v_=v_, v1_=v1_):
                    p_ = pT[ub]
                    parts = []
                    for c in range(2):
                        vv, Rvv = (v_, Rva[hb]) if c == 0 else (v1_, Rvb[hb])
                        for qi in range(2):
                            i = 2 * g2 + qi
                            bi = 6 + (i % 2)
                            acc, Racc = self.banks[bi], self.Rbk[bi]
                            for j in range(i + 1):
                                parts.append(lambda acc=acc, Racc=Racc, j=j, i=i, qi=qi, c=c, vv=vv, Rvv=Rvv: self.mm(
                                    acc[:, c * 130:c * 130 + 129],
                                    p_[:, j, c * 256 + qi * 128:c * 256 + (qi + 1) * 128], vv[:, j, 0:129],
                                    j == 0, j == i,
                                    (RpT[ub][0:2 * g2 + 2] if os.environ.get("PV_COARSE") else [RpT[ub][j]]) + [Rvv],
                                    [Racc]))
                    return parts

                def pv_tail(g2):
                    chain1(g2)
                    defer(1, lambda g2=g2: chain2(g2))
                    defer(1, lambda g2=g2: chain3(g2))
                    defer(2, lambda g2=g2: epi_a(g2))
                    defer(2, lambda g2=g2: epi_b(g2, 0))
                    defer(3, lambda g2=g2: epi_b(g2, 1))

                for pc in st_parts(units[0], uu % NPT):
                    pc()
                for n_ in range(len(units)):
                    if n_ == 1 and hd + 1 < nheads:
                        Wn, RWn = self.wnext(hold=4 if hd % 2 == 1 else 2)
                        pieces = proj_pieces(hd + 1, Wn, RWn)
                    sts = st_parts(units[n_ + 1], (uu + 1) % NPT) if n_ + 1 < len(units) else []
                    pvs = pv_parts(units[n_], uu % NPT)
                    per = -(-len(pvs) // max(1, len(sts)))
                    for st_ in sts:
                        st_()
                        for _ in range(per):
                            if pvs:
                                pvs.pop(0)()
                    while pvs:
                        pvs.pop(0)()
                    pv_tail(units[n_])
                    for _ in range(2):
                        if n_ >= 1 and pieces:
                            pieces.pop(0)()
                    uu += 1
                    step[0] += 1
                    run_due()
            run_due(final=True)
            S.barrier()

    def final_alloc(self, stack):
        self.ob = [self.sb("ob", [128, D], F32, stack) for _ in range(2)]
        self.Rob = [Res("ob0"), Res("ob1")]
        self.junkf = self.sb("junkf", [128, D], BF16, stack)
        self.Rjunkf = Res("junkf")

    def final_stats(self, tiles, load_gain=True):
        S = self.S
        if load_gain:
            S.dma("sp", self.gain[:, :], self.dr["vecs"][0:1, 4 * D:5 * D].broadcast_to([128, D]),
                  writes=[self.Rgain], owner=self.Rgain)
        lo, hi = tiles[0], tiles[-1] + 1
        for t in tiles:
            self.act(self.junkf[:, :], self.X[:, t, :], AF.Square, [self.RX[t]], [self.Rjunkf, self.Rss],
                     accum_out=self.ss[:, t:t + 1])
        self.act(self.lnss[:, lo:hi], self.ss[:, lo:hi], AF.Ln, [self.Rss], [self.Rln], scale=1.0 / D, bias=EPS)
        self.act(self.rstd[:, lo:hi], self.lnss[:, lo:hi], AF.Exp, [self.Rln], [self.Rrstd], scale=-0.5)

    def final_tile(self, t):
        S = self.S
        o_, Ro_ = self.ob[t % 2], self.Rob[t % 2]
        if self.final:
            S.op("dve", lambda h: h.scalar_tensor_tensor(out=o_[:, :], in0=self.X[:, t, :],
                                                          scalar=self.rstd[:, t:t + 1], in1=self.gain[:, :],
                                                          op0=ALU.mult, op1=ALU.mult),
                 [self.RX[t], self.Rrstd, self.Rgain], [Ro_])
        else:
            S.op("dve", lambda h: h.tensor_copy(out=o_[:, :], in_=self.X[:, t, :]), [self.RX[t]], [Ro_])
        S.dma("sp", self.out_ap[t * 128:(t + 1) * 128, :], o_[:, :], reads=[Ro_], owner=Ro_)

    def final_stage(self, out):
        S = self.S
        with ExitStack() as ls:
            self.final_alloc(ls)
            tiles = [t for t in range(NT) if t not in self.final_done]
            if self.final and tiles:
                self.final_stats(tiles)
            for t in tiles:
                self.final_tile(t)
            S.finish()


def prep_weights(inp):
    f = lambda a: np.ascontiguousarray(np.asarray(a, dtype=np.float32))
    w = {}
    gw = f(inp["gla_w_in"])[0].reshape(8, 128, 3072)
    heads = []
    for h in range(4):
        cols = np.concatenate([gw[:, :, h * 128:(h + 1) * 128],
                               gw[:, :, 512 + h * 128:512 + (h + 1) * 128],
                               gw[:, :, 1024 + h * 256:1024 + (h + 1) * 256],
                               gw[:, :, 2048 + h * 256:2048 + (h + 1) * 256]], axis=2)
        heads.append(cols.transpose(1, 0, 2).reshape(128, 6144))
    w["w_gla_h"] = f(np.stack(heads))
    go = f(inp["gla_w_out"])[0].reshape(4, 2, 128, 1024)
    w["w_gla_o"] = f(go.transpose(0, 2, 1, 3).reshape(4, 128, 2048))
    ga = f(inp["gla_w_gate_a"])[0].reshape(8, 128, 16)
    w["w_gla_a"] = f(ga.transpose(1, 0, 2).reshape(128, 128))
    wbm = np.zeros((33, 512), np.float32)
    wbm[0:16] = f(inp["gla_w_gate_b"])[0]
    wbm[32] = f(inp["gla_b_gate"])[0]
    w["w_gla_b"] = wbm
    dw = f(inp["diff_w_in"])[0].reshape(8, 128, 3072)
    heads = []
    for h in range(8):
        cols = np.concatenate([dw[:, :, h * 128:(h + 1) * 128],
                               dw[:, :, 1024 + h * 128:1024 + (h + 1) * 128],
                               dw[:, :, 2048 + h * 128:2048 + (h + 1) * 128]], axis=2)
        heads.append(cols.transpose(1, 0, 2).reshape(128, 3072))
    w["w_diff_h"] = f(np.stack(heads))
    w["w_diff_o"] = f(f(inp["diff_w_out"])[0].reshape(8, 128, 1024))
    fi = f(inp["ffn_w_in"]).reshape(2, 8, 128, 2, 11, 256)
    w["w_ffn_in"] = f(fi.transpose(0, 4, 2, 1, 3, 5).reshape(2, 11, 128, 4096))
    fo = f(inp["ffn_w_out"]).reshape(2, NF, 128, 4, 256)
    w["w_ffn_out"] = f(fo.transpose(0, 3, 2, 1, 4).reshape(2, 4, 128, 5632))
    vec = np.zeros((1, V_TOT), np.float32)
    vec[0, 0:2 * D] = f(inp["norm_mixer"]).reshape(-1)
    vec[0, 2 * D:4 * D] = f(inp["norm_ffn"]).reshape(-1)
    vec[0, 4 * D:5 * D] = f(inp["norm_final"]).reshape(-1)
    vec[0, V_GLAGN:V_GLAGN + 256] = f(inp["gla_norm"]).reshape(-1)
    vec[0, V_DIFFGN:V_DIFFGN + 128] = f(inp["diff_norm"]).reshape(-1)
    vec[0, V_LAM:V_LAM + 64] = f(inp["diff_lam_q1"]).reshape(-1)
    vec[0, V_LAM + 64:V_LAM + 128] = f(inp["diff_lam_q2"]).reshape(-1)
    vec[0, V_LAM + 128:V_LAM + 192] = f(inp["diff_lam_k1"]).reshape(-1)
    vec[0, V_LAM + 192:V_LAM + 256] = f(inp["diff_lam_k2"]).reshape(-1)
    w["vecs"] = vec
    return w


def run(inputs, stages=("gla", "ffn0", "diff", "ffn1"), final=True, cores=8, trace=False):
    b = Builder(stages, final)
    nc = b.build()
    w = prep_weights(inputs)
    x = np.asarray(inputs["x"], dtype=np.float32)
    in_maps = []
    for i in range(cores):
        m = dict(w)
        m["x"] = np.ascontiguousarray(x[i])
        in_maps.append(m)
    res = run_bass_kernel_spmd(nc, in_maps, core_ids=list(range(cores)), trace=trace)
    outs = np.stack([np.asarray(r["out"], dtype=np.float32) for r in res.results], axis=0)
    return outs, res, b


def kernel(**inputs):
    outs, _, _ = run(inputs)
    return outs
```

```python
import math
import os
import numpy as np
import concourse.bass as bass
import concourse.mybir as mybir
from concourse.bass_utils import run_bass_kernel_spmd
from contextlib import ExitStack

F32 = mybir.dt.float32
BF16 = mybir.dt.bfloat16
AF = mybir.ActivationFunctionType
ALU = mybir.AluOpType
AX = mybir.AxisListType

D = 1024
SEQ = 2048
NT = 16
DFF = 2816
NF = 22
HK = 128
HV = 256
EPS = 1e-6
SUBLN_EPS = 1e-5
LAMBDA_INIT = 0.8 - 0.6 * math.exp(-0.3 * 1)
RING_N = 6144
NRING = 4
NDMASEM = 48

V_NORM = 0
V_GLAGN = 5 * 1024
V_DIFFGN = V_GLAGN + 256
V_LAM = V_DIFFGN + 128
V_TOT = V_LAM + 256


class Tok:
    __slots__ = ("sem", "key", "val", "clock")

    def __init__(self, sem, key, val, clock):
        self.sem, self.key, self.val, self.clock = sem, key, val, clock


class Res:
    __slots__ = ("name", "w", "r", "dma", "excl")

    def __init__(self, name, excl=False):
        self.name = name
        self.w = None
        self.r = {}
        self.dma = None
        self.excl = excl


class Eng:
    def __init__(self, name, h, sem, selfdep):
        self.name, self.h, self.sem, self.selfdep = name, h, sem, selfdep
        self.key = "E_" + name
        self.count = 0
        self.clock = {}
        self.last = None
        self.nwait = 0


class Sched:
    def __init__(self, nc, stack):
        self.nc = nc
        self.stack = stack
        self.engs = {}
        for name, h, selfdep in [
            ("pe", nc.tensor, False),
            ("act", nc.scalar, True),
            ("dve", nc.vector, True),
            ("pool", nc.gpsimd, True),
            ("sp", nc.sync, True),
        ]:
            sem = stack.enter_context(nc.semaphore("s_" + name))
            self.engs[name] = Eng(name, h, sem, selfdep)
        self.dmas = []
        self.ndma = 0
        self.dma_pool = [stack.enter_context(nc.semaphore("d_%d" % i)) for i in range(NDMASEM)]
        self.all_sems = [e.sem for e in self.engs.values()] + self.dma_pool
        self.clear_sems()
        nc.all_engine_barrier()

    def clear_sems(self):
        for s in self.all_sems:
            self.nc.gpsimd.sem_clear(s)

    def _sync(self, e, deps):
        need = {}
        for t in deps:
            if t is None:
                continue
            if t.key == e.key and not e.selfdep:
                continue
            if e.clock.get(t.key, 0) >= t.val:
                continue
            cur = need.get(t.key)
            if cur is None or cur.val < t.val:
                need[t.key] = t
        for key, t in need.items():
            if e.clock.get(key, 0) >= t.val:
                continue
            e.h.wait_ge(t.sem, t.val)
            e.nwait += 1
            for k, v in t.clock.items():
                if e.clock.get(k, 0) < v:
                    e.clock[k] = v
            e.clock[key] = t.val

    @staticmethod
    def _deps(reads, writes, ekey=None):
        deps = []
        for r in reads:
            if r.w is not None:
                deps.append(r.w)
            if r.excl:
                deps.extend(t for k, t in r.r.items() if k != ekey)
        for w in writes:
            if w.w is not None:
                deps.append(w.w)
            deps.extend(w.r.values())
        return deps

    @staticmethod
    def _update(tok, reads, writes):
        for r in reads:
            r.r[tok.key] = tok
        for w in writes:
            w.w = tok
            w.r = {}

    def op(self, en, fn, reads=(), writes=()):
        e = self.engs[en]
        self._sync(e, self._deps(reads, writes, e.key))
        ins = fn(e.h)
        e.count += 1
        ins.then_inc(e.sem, 1)
        tok = Tok(e.sem, e.key, e.count, dict(e.clock))
        e.last = tok
        self._update(tok, reads, writes)
        return tok

    def dma(self, qn, out, in_, reads=(), writes=(), owner=None, **kw):
        e = self.engs[qn]
        self._sync(e, self._deps(reads, writes))
        ins = e.h.dma_start(out=out, in_=in_, **kw)
        if owner.dma is None:
            sem = self.dma_pool[len(self.dmas)]
            owner.dma = [sem, "D_%d" % len(self.dmas), 0, None]
            self.dmas.append(owner.dma)
        d = owner.dma
        d[2] += 16
        ins.then_inc(d[0], 16)
        tok = Tok(d[0], d[1], d[2], dict(e.clock))
        d[3] = tok
        self.ndma += 1
        self._update(tok, reads, writes)
        return tok

    def prewait(self, en, writes):
        e = self.engs[en]
        self._sync(e, self._deps([], writes, e.key))

    def all_toks(self):
        toks = [e.last for e in self.engs.values() if e.last is not None]
        toks += [d[3] for d in self.dmas if d[3] is not None]
        return toks

    def barrier(self):
        toks = self.all_toks()
        for e in self.engs.values():
            self._sync(e, toks)

    def finish(self):
        self._sync(self.engs["sp"], self.all_toks())

    def cleanup(self):
        self.nc.all_engine_barrier()
        self.clear_sems()


class Builder:
    def __init__(self, stages=("gla", "ffn0", "diff", "ffn1"), final=True):
        self.stages = stages
        self.final = final

    def sb(self, name, shape, dt, stack=None):
        stack = stack or self.st
        self._n += 1
        return stack.enter_context(self.nc.sbuf_tensor("%s_%d" % (name, self._n), shape, dt))

    def bank(self, pool=None):
        pool = pool or self.gen_banks
        i = pool[self._bk % len(pool)]
        self._bk += 1
        return self.banks[i], self.Rbk[i]

    def mm(self, out, lhsT, rhs, start, stop, reads, writes):
        self.S.op("pe", lambda h: h.matmul(out, lhsT=lhsT, rhs=rhs, start=start, stop=stop),
                  reads, writes)

    def tr(self, out, in_, reads, writes):
        self.S.op("pe", lambda h: h.transpose(out=out, in_=in_, identity=self.ident[:, :]),
                  list(reads) + [self.Rconst], writes)

    def act(self, out, in_, func, reads, writes, **kw):
        self.S.op("act", lambda h: h.activation(out=out, in_=in_, func=func, **kw), reads, writes)

    def open_ring(self, stack, nelem):
        self.ring = [self.sb("ring", [128, nelem], BF16, stack) for _ in range(NRING)]
        self.Rring = [Res("ring%d" % i) for i in range(NRING)]
        self.wbase = self.wpos
        self.wend = self.wstage_end.pop(0)
        if self.wbase == 0:
            gate = int(os.environ.get("XGATE", "-1"))
            if gate >= 0:
                self.S._sync(self.S.engs["pool"], [self.RX[gate].w])
        self._wissue(min(self.wbase + NRING - 1, self.wend))

    def _wissue(self, upto):
        while self.wissued <= upto:
            k = self.wissued
            ap, n = self.wseq[k]
            slot = (k - self.wbase) % NRING
            self.S.dma("pool", self.ring[slot][:, 0:n], ap, writes=[self.Rring[slot]],
                       owner=self.Rring[slot], max_dma_last_dim=2048)
            self.wissued += 1

    def wnext(self, hold=1):
        i = self.wpos
        self.wpos += 1
        self._wissue(min(i + NRING - hold, self.wend))
        slot = (i - self.wbase) % NRING
        return self.ring[slot], self.Rring[slot]

    def build(self):
        nc = bass.Bass("TRN2", target_bir_lowering=False)
        self.nc = nc
        self._n = 0
        self._bk = 0
        dr = {}

        def din(name, shape):
            dr[name] = nc.dram_tensor(name, shape, F32, kind="ExternalInput").ap()

        din("x", [SEQ, D])
        din("w_gla_h", [4, 128, 6144])
        din("w_gla_o", [4, 128, 2048])
        din("w_gla_a", [128, 128])
        din("w_gla_b", [128, 512])
        din("w_diff_h", [8, 128, 3072])
        din("w_diff_o", [8, 128, 1024])
        din("w_ffn_in", [2, 11, 128, 4096])
        din("w_ffn_out", [2, 4, 128, 5632])
        din("vecs", [1, V_TOT])
        out = nc.dram_tensor("out", [SEQ, D], F32, kind="ExternalOutput").ap()
        self.dr = dr
        self.out_ap = out
        self.final_done = set()
        self.last_stage = self.stages[-1] if self.stages else None

        wseq = []
        for stg in self.stages:
            if stg == "gla":
                for h in range(4):
                    wseq.append((dr["w_gla_h"][h], 6144))
                    wseq.append((dr["w_gla_o"][h], 2048))
            elif stg == "diff":
                for h in range(8):
                    wseq.append((dr["w_diff_h"][h], 3072))
                    wseq.append((dr["w_diff_o"][h], 1024))
            else:
                l = int(stg[-1])
                for hf in range(2):
                    for fb in range(11):
                        wseq.append((dr["w_ffn_in"][l, fb], 4096))
                    for nq in range(4):
                        wseq.append((dr["w_ffn_out"][l, nq], 5632))
        self.wseq = wseq
        self.wpos = 0
        self.wissued = 0
        self.wstage_end = []
        acc_ = 0
        for stg in self.stages:
            acc_ += {"gla": 8, "diff": 16}.get(stg, 30)
            self.wstage_end.append(acc_ - 1)

        with ExitStack() as st:
            self.st = st
            S = Sched(nc, st)
            self.S = S
            self.X = self.sb("X", [128, NT, D], F32)
            self.RX = [Res("X%d" % t) for t in range(NT)]
            self.ident = self.sb("ident", [128, 128], BF16)
            self.tri = self.sb("tri", [128, 128], BF16)
            self.U32 = self.sb("U32", [128, 128], F32)
            self.ones32 = self.sb("ones32", [128, 128], F32)
            self.Rconst = Res("const")
            self.gain = self.sb("gain", [128, D], F32)
            self.Rgain = Res("gain")
            self.ss = self.sb("ss", [128, NT], F32)
            self.lnss = self.sb("lnss", [128, NT], F32)
            self.rstd = self.sb("rstd", [128, NT], F32)
            self.Rss, self.Rln, self.Rrstd = Res("ss"), Res("lnss"), Res("rstd")
            self.banks = [st.enter_context(nc.psum_tensor("bk%d" % i, [128, 512], F32)) for i in range(8)]
            self.Rbk = [Res("bk%d" % i, excl=True) for i in range(8)]
            self.gen_banks = list(range(8))

            Rc = self.Rconst
            S.op("pool", lambda h: h.memset(self.ident[:, :], 1.0), writes=[Rc])
            S.op("pool", lambda h: h.affine_select(out=self.ident[:, :], in_=self.ident[:, :], pattern=[[1, 128]],
                                                    compare_op=ALU.is_equal, fill=0.0, base=0,
                                                    channel_multiplier=-1), reads=[Rc], writes=[Rc])
            S.op("pool", lambda h: h.memset(self.tri[:, :], 1.0), reads=[Rc], writes=[Rc])
            S.op("pool", lambda h: h.affine_select(out=self.tri[:, :], in_=self.tri[:, :], pattern=[[1, 128]],
                                                    compare_op=ALU.is_ge, fill=0.0, base=0,
                                                    channel_multiplier=-1), reads=[Rc], writes=[Rc])
            S.op("pool", lambda h: h.memset(self.U32[:, :], 1.0), reads=[Rc], writes=[Rc])
            S.op("pool", lambda h: h.affine_select(out=self.U32[:, :], in_=self.U32[:, :], pattern=[[1, 128]],
                                                    compare_op=ALU.is_ge, fill=0.0, base=0,
                                                    channel_multiplier=-1), reads=[Rc], writes=[Rc])
            S.op("pool", lambda h: h.memset(self.ones32[:, :], 1.0), reads=[Rc], writes=[Rc])

            for t in range(NT):
                S.dma("sp", self.X[:, t, :], dr["x"][t * 128:(t + 1) * 128, :], writes=[self.RX[t]],
                      owner=self.RX[t])

            for stg in self.stages:
                if stg == "gla":
                    self.gla_stage()
                elif stg == "diff":
                    self.diff_stage()
                else:
                    self.ffn_stage(int(stg[-1]))

            self.final_stage(out)
            S.finish()
            S.cleanup()
        self.stats = {k: (e.count, e.nwait) for k, e in S.engs.items()}
        self.stats["ndma"] = S.ndma
        return nc

    def norm_alloc(self, ls):
        junk = self.sb("junk", [128, D], BF16, ls)
        hbf = [self.sb("hbf", [128, D], BF16, ls) for _ in range(2)]
        return (junk, Res("junk"), hbf, [Res("hbf0"), Res("hbf1")])

    def norm_stats(self, gidx, tiles, tmp, li0=0, load_gain=True, skip_sq=False):
        S = self.S
        junk, Rjunk, _, _ = tmp
        vecs = self.dr["vecs"]
        if load_gain:
            S.dma("sp", self.gain[:, :], vecs[0:1, gidx * D:(gidx + 1) * D].broadcast_to([128, D]),
                  writes=[self.Rgain], owner=self.Rgain)
        n = len(tiles)
        for k, t in enumerate(tiles):
            li = li0 + k
            if not skip_sq:
                self.act(junk[:, :], self.X[:, t, :], AF.Square, [self.RX[t]], [Rjunk, self.Rss],
                         accum_out=self.ss[:, li:li + 1])
        self.act(self.lnss[:, li0:li0 + n], self.ss[:, li0:li0 + n], AF.Ln, [self.Rss], [self.Rln],
                 scale=1.0 / D, bias=EPS)
        self.act(self.rstd[:, li0:li0 + n], self.lnss[:, li0:li0 + n], AF.Exp, [self.Rln], [self.Rrstd], scale=-0.5)

    def norm_tile(self, li, t, hT, RhT, tmp, part=None):
        S = self.S
        _, _, hbf, Rhbf = tmp
        hb, Rhb = hbf[li % 2], Rhbf[li % 2]
        if part in (None, "a"):
            S.op("dve", lambda h: h.scalar_tensor_tensor(out=hb[:, :], in0=self.X[:, t, :],
                                                          scalar=self.rstd[:, li:li + 1], in1=self.gain[:, :],
                                                          op0=ALU.mult, op1=ALU.mult),
                 [self.RX[t], self.Rrstd, self.Rgain], [Rhb])
        if part == "a":
            return
        bk, Rb = self.bank()
        bkb = bk[:, :].bitcast(BF16)
        for c in range(8):
            self.tr(bkb[:, c * 128:(c + 1) * 128], hb[:, c * 128:(c + 1) * 128], [Rhb], [Rb])
        S.op("act", lambda h: h.copy(out=hT[:, :, li * 128:(li + 1) * 128],
                                      in_=bkb.rearrange("p (c t) -> p c t", c=8)), [Rb], [RhT[li]])

    def norm_stage(self, gidx, tiles, hT, RhT, ls, batch=16):
        tmp = self.norm_alloc(ls)
        skip_sq = getattr(self, "sq_ready", False) and len(tiles) == NT
        self.sq_ready = False
        for b0 in range(0, len(tiles), batch):
            self.norm_stats(gidx, tiles[b0:b0 + batch], tmp, li0=b0, load_gain=(b0 == 0), skip_sq=skip_sq)
            for k, t in enumerate(tiles[b0:b0 + batch]):
                self.norm_tile(b0 + k, t, hT, RhT, tmp)

    def ffn_stage(self, l):
        S = self.S
        with ExitStack() as ls:
            self.open_ring(ls, 5632)
            hTs = [self.sb("hTf", [128, 8, 1024], BF16, ls) for _ in range(2)]
            RhTs = [[Res("hTf%d_%d" % (h_, i)) for i in range(8)] for h_ in range(2)]
            aT = self.sb("aT", [128, NF, 1024], BF16, ls)
            RaT = [Res("aT0"), Res("aT1")]
            sg = [self.sb("sg", [128, 512], BF16, ls) for _ in range(2)]
            Rsg = [Res("sg0"), Res("sg1")]
            ntmp = self.norm_alloc(ls)
            early_final = self.final and self.last_stage == "ffn%d" % l
            si = self.stages.index("ffn%d" % l)
            early_next = (not early_final) and si + 1 < len(self.stages) and self.stages[si + 1] in ("gla", "diff")

            def early_sq(t):
                self.act(ntmp[0][:, :], self.X[:, t, :], AF.Square, [self.RX[t]], [ntmp[1], self.Rss],
                         accum_out=self.ss[:, t:t + 1])
            if early_final:
                self.final_alloc(ls)
            self.norm_stats(2 + l, list(range(0, 8)), ntmp)
            for li in range(8):
                self.norm_tile(li, li, hTs[0], RhTs[0], ntmp)
            k = 0
            for hf in range(2):
                hT, RhT = hTs[hf], RhTs[hf]
                for fb in range(11):
                    if hf == 1 and early_next and fb == 1:
                        for t_ in range(8):
                            early_sq(t_)
                    if hf == 1 and early_final:
                        if fb == 1:
                            self.final_stats(list(range(0, 8)))
                        if 2 <= fb < 10:
                            self.final_tile(fb - 2)
                            self.final_done.add(fb - 2)
                    if hf == 0:
                        if fb == 1:
                            self.norm_stats(2 + l, list(range(8, 16)), ntmp)
                        if 3 <= fb < 11:
                            self.norm_tile(fb - 3, 8 + fb - 3, hTs[1], RhTs[1], ntmp, part="b")
                        if 2 <= fb < 10:
                            self.norm_tile(fb - 2, 8 + fb - 2, hTs[1], RhTs[1], ntmp, part="a")
                    W, RW = self.wnext()
                    Wv = W[:, 0:4096].rearrange("p (c g n) -> p c g n", c=8, g=2)
                    for tg in range(2):
                        rh = RhT[tg * 4:(tg + 1) * 4]
                        for j in range(2):
                            bg, Rbg = self.bank()
                            bu, Rbu = self.bank()
                            for c in range(8):
                                self.mm(bg[:, :], Wv[:, c, 0, j * 128:(j + 1) * 128],
                                        hT[:, c, tg * 512:(tg + 1) * 512], c == 0, c == 7, [RW] + rh, [Rbg])
                            for c in range(8):
                                self.mm(bu[:, :], Wv[:, c, 1, j * 128:(j + 1) * 128],
                                        hT[:, c, tg * 512:(tg + 1) * 512], c == 0, c == 7, [RW] + rh, [Rbu])
                            s_, Rs_ = sg[k % 2], Rsg[k % 2]
                            k += 1
                            self.act(s_[:, :], bg[:, :], AF.Silu, [Rbg], [Rs_])
                            S.op("dve", lambda h: h.tensor_tensor(out=aT[:, fb * 2 + j, tg * 512:(tg + 1) * 512],
                                                                   in0=bu[:, :], in1=s_[:, :], op=ALU.mult),
                                 [Rbu, Rs_], [RaT[tg]])
                if hf == 1 and (early_final or early_next):
                    Ws = []
                    for nq in range(4):
                        W, RW = self.wnext(hold=nq + 1)
                        Ws.append((W[:, 0:5632].rearrange("p (f n) -> p f n", f=NF), RW))
                    prev_t = None
                    for tl in range(8):
                        t = 8 * hf + tl
                        for nq in range(4):
                            Wv, RW = Ws[nq]
                            bk, Rb = self.bank()
                            for f in range(NF):
                                self.mm(bk[:, 0:256], aT[:, f, tl * 128:(tl + 1) * 128], Wv[:, f, :],
                                        f == 0, f == NF - 1, [RaT[tl // 4], RW], [Rb])
                            xs = self.X[:, t, nq * 256:(nq + 1) * 256]
                            S.op("dve", lambda h: h.tensor_tensor(out=xs, in0=bk[:, 0:256], in1=xs, op=ALU.add),
                                 [Rb, self.RX[t]], [self.RX[t]])
                        if prev_t is not None:
                            if early_final:
                                self.final_stats([prev_t], load_gain=False)
                                self.final_tile(prev_t)
                                self.final_done.add(prev_t)
                            else:
                                early_sq(prev_t)
                        prev_t = t
                    if early_final:
                        self.final_stats([prev_t], load_gain=False)
                        self.final_tile(prev_t)
                        self.final_done.add(prev_t)
                    else:
                        early_sq(prev_t)
                        self.sq_ready = True
                else:
                    for nq in range(4):
                        W, RW = self.wnext()
                        Wv = W[:, 0:5632].rearrange("p (f n) -> p f n", f=NF)
                        for tl in range(8):
                            t = 8 * hf + tl
                            bk, Rb = self.bank()
                            for f in range(NF):
                                self.mm(bk[:, 0:256], aT[:, f, tl * 128:(tl + 1) * 128], Wv[:, f, :],
                                        f == 0, f == NF - 1, [RaT[tl // 4], RW], [Rb])
                            xs = self.X[:, t, nq * 256:(nq + 1) * 256]
                            S.op("dve", lambda h: h.tensor_tensor(out=xs, in0=bk[:, 0:256], in1=xs, op=ALU.add),
                                 [Rb, self.RX[t]], [self.RX[t]])
            S.barrier()

    def gla_stage(self):
        S, nc, dr = self.S, self.nc, self.dr
        with ExitStack() as ls:
            self.open_ring(ls, 6144)
            hT = self.sb("hT", [128, 8, SEQ], BF16, ls)
            RhT = [Res("hT%d" % i) for i in range(NT)]
            self.norm_stage(0, list(range(NT)), hT, RhT, ls)
            wa = self.sb("wa", [128, 128], BF16, ls)
            wb = self.sb("wb", [128, 512], BF16, ls)
            gnb = self.sb("gnb", [128, HV], F32, ls)
            Rwa, Rwb, Rgnb = Res("wa"), Res("wb"), Res("gnb")
            S.dma("pool", wa[:, :], dr["w_gla_a"][:, :], writes=[Rwa], owner=Rwa)
            S.dma("pool", wb[:, :], dr["w_gla_b"][:, :], writes=[Rwb], owner=Rwb)
            S.dma("sp", gnb[:, :], dr["vecs"][0:1, V_GLAGN:V_GLAGN + HV].broadcast_to([128, HV]),
                  writes=[Rgnb], owner=Rgnb)
            uaug = self.sb("uaug", [128, SEQ], BF16, ls)
            Ru = Res("uaug")
            S.op("dve", lambda h: h.memset(uaug[:, :], 0.0), writes=[Ru])
            S.op("dve", lambda h: h.memset(uaug[32:33, :], 1.0), reads=[Ru], writes=[Ru])
            for tg in range(4):
                bk, Rb = self.bank()
                for c in range(8):
                    self.mm(bk[0:16, :], wa[:, c * 16:(c + 1) * 16], hT[:, c, tg * 512:(tg + 1) * 512],
                            c == 0, c == 7, [Rwa] + RhT[tg * 4:(tg + 1) * 4], [Rb])
                S.op("act", lambda h: h.copy(out=uaug[0:16, tg * 512:(tg + 1) * 512], in_=bk[0:16, :]), [Rb], [Ru])

            sr = [self.sb("sr", [128, 2, SEQ], BF16, ls) for _ in range(2)]
            Rsr = [[Res("sr%d_%d" % (b_, i)) for i in range(4)] for b_ in range(2)]
            tmpA = self.sb("tmpA", [128, 512], F32, ls)
            tmpB = self.sb("tmpB", [128, 512], F32, ls)
            RtA, RtB = Res("tmpA"), Res("tmpB")
            dd = [self.sb("dd", [128, 4], F32, ls) for _ in range(2)]
            Rdd = [Res("dd0"), Res("dd1")]
            qs = [self.sb("qs", [128, 512], BF16, ls) for _ in range(2)]
            ks = [self.sb("ks", [128, 512], BF16, ls) for _ in range(2)]
            kh = [self.sb("kh", [128, 512], BF16, ls) for _ in range(2)]
            kht = [self.sb("kht", [128, 512], BF16, ls) for _ in range(2)]
            vsb = [self.sb("vsb", [128, 1024], BF16, ls) for _ in range(2)]
            Rqs, Rks, Rkh, Rkht = ([Res(n + "0"), Res(n + "1")] for n in ("qs", "ks", "kh", "kht"))
            Rv = [[Res("v%d_%d" % (b_, i)) for i in range(2)] for b_ in range(2)]
            at = [self.sb("at", [128, 128], BF16, ls) for _ in range(2)]
            Rat = [Res("at0"), Res("at1")]
            Sst = self.sb("Sst", [128, HV], F32, ls)
            RS = Res("S")
            Sbf = [self.sb("Sbf", [128, HV], BF16, ls) for _ in range(2)]
            RSbf = [Res("Sbf0"), Res("Sbf1")]
            sso = [self.sb("sso", [128, 4], F32, ls) for _ in range(4)]
            Rsso = [Res("sso%d" % i) for i in range(4)]
            go = [self.sb("go", [128, 2, 128], BF16, ls) for _ in range(4)]
            Rgo = [Res("go%d" % i) for i in range(4)]
            sq = [self.sb("sq", [128, HV], BF16, ls) for _ in range(4)]
            Rsq = [Res("sq%d" % i) for i in range(4)]
            gcol = self.sb("gcol", [128, 2], F32, ls)
            Rgcol = Res("gcol")
            for jv in range(2):
                S.dma("sp", gcol[:, jv:jv + 1],
                      dr["vecs"][0:1, V_GLAGN + jv * 128:V_GLAGN + (jv + 1) * 128].rearrange("o (p q) -> (o p) q", q=1),
                      writes=[Rgcol], owner=Rgcol)
            ysb = [self.sb("ysb", [128, 512], F32, ls) for _ in range(2)]
            Rysb = [Res("ysb0"), Res("ysb1")]
            ycnt = [0]
            onesb = self.sb("onesb", [128, 2], BF16, ls)
            S.op("dve", lambda h: h.memset(onesb[:, :], 1.0), reads=[self.Rconst], writes=[self.Rconst])

            step = [0]
            due = []

            def defer(k, fn):
                if os.environ.get('DBG_NODEFER'):
                    fn()
                    return
                due.append((step[0] + k, fn))

            def run_due(final=False):
                rest = []
                for d_, fn in due:
                    if final or d_ <= step[0]:
                        fn()
                    else:
                        rest.append((d_, fn))
                due[:] = rest

            kk = [0]
            cur = [0]

            def make_head(hd, Whv, RWh):
                hb = hd % 2
                def vproj(tg, i2):
                    b2 = tg % 2
                    bv, Rbv = self.bank()
                    for ii in range(2):
                        t = tg * 4 + i2 * 2 + ii
                        for c in range(8):
                            self.mm(bv[:, ii * 256:(ii + 1) * 256], hT[:, c, t * 128:(t + 1) * 128],
                                    Whv[:, c, 256:512], c == 0, c == 7, [RWh, RhT[t]], [Rbv])
                    S.op("act", lambda h: h.copy(out=vsb[b2][:, i2 * 512:(i2 + 1) * 512], in_=bv[:, :]),
                         [Rbv], [Rv[b2][i2]])

                def G1(tg):
                    bx, Rbx = self.bank()
                    for i in range(4):
                        t = tg * 4 + i
                        self.mm(bx[:, i * 128:(i + 1) * 128], uaug[:, t * 128:(t + 1) * 128],
                                wb[:, hd * 128:(hd + 1) * 128], True, True, [Ru, Rwb], [Rbx])
                    self.act(tmpA[:, :], bx[:, :], AF.Exp, [Rbx], [RtA], scale=-1.0)
                    self.act(tmpB[:, :], tmpA[:, :], AF.Ln, [RtA], [RtB], bias=1.0)
                    if not os.environ.get('DBG_VLATE'):
                        vproj(tg, 0)

                def G2(tg):
                    b2 = tg % 2
                    bc, Rbc = self.bank()
                    for i in range(4):
                        self.mm(bc[:, i * 128:(i + 1) * 128], tmpB[:, i * 128:(i + 1) * 128], self.U32[:, :],
                                True, True, [RtB, self.Rconst], [Rbc])
                    self.act(tmpA[:, :], bc[:, :], AF.Exp, [Rbc], [RtA], scale=-1.0 / 16)
                    self.act(tmpB[:, :], bc[:, :], AF.Exp, [Rbc], [RtB], scale=1.0 / 16)
                    S.op("dve", lambda h: h.tensor_copy(
                        out=dd[b2][:, :], in_=tmpA[:, :].rearrange("p (i t) -> p i t", i=4)[:, :, 127]),
                         [RtA], [Rdd[b2]])
                    if not os.environ.get('DBG_VLATE'):
                        vproj(tg, 1)

                def G3(tg):
                    b2 = tg % 2
                    rh = RhT[tg * 4:(tg + 1) * 4]
                    bq, Rbq = self.bank()
                    bkk, Rbkk = self.bank()
                    for c in range(8):
                        self.mm(bq[:, :], Whv[:, c, 0:128], hT[:, c, tg * 512:(tg + 1) * 512], c == 0, c == 7,
                                [RWh] + rh, [Rbq])
                    for c in range(8):
                        self.mm(bkk[:, :], Whv[:, c, 128:256], hT[:, c, tg * 512:(tg + 1) * 512], c == 0, c == 7,
                                [RWh] + rh, [Rbkk])
                    S.op("dve", lambda h: h.scalar_tensor_tensor(out=qs[b2][:, :], in0=bq[:, :], scalar=HK ** -0.5,
                                                                  in1=tmpA[:, :], op0=ALU.mult, op1=ALU.mult),
                         [Rbq, RtA], [Rqs[b2]])
                    S.op("dve", lambda h: h.tensor_tensor(out=ks[b2][:, :], in0=bkk[:, :], in1=tmpB[:, :],
                                                           op=ALU.mult), [Rbkk, RtB], [Rks[b2]])
                    for i in range(4):
                        S.op("pool", lambda h: h.tensor_scalar(
                            out=kh[b2][:, i * 128:(i + 1) * 128], in0=ks[b2][:, i * 128:(i + 1) * 128],
                            scalar1=dd[b2][:, i:i + 1], scalar2=1.0, op0=ALU.mult, op1=ALU.mult),
                             [Rks[b2], Rdd[b2]], [Rkh[b2]])

                def G4(tg):
                    b2 = tg % 2
                    bt, Rbt = self.bank()
                    btb = bt[:, :].bitcast(BF16)
                    for i in range(4):
                        self.tr(btb[:, i * 128:(i + 1) * 128], kh[b2][:, i * 128:(i + 1) * 128], [Rkh[b2]], [Rbt])
                    S.op("act", lambda h: h.copy(out=kht[b2][:, :], in_=btb[:, 0:512]), [Rbt], [Rkht[b2]])
                    if os.environ.get('DBG_VLATE'):
                        vproj(tg, 0)
                        vproj(tg, 1)

                G = [G1, G2, G3, G4]

                def rpiece(tg, j):
                    rh = RhT[tg * 4:(tg + 1) * 4]
                    bk, Rb = self.bank()
                    for c in range(8):
                        self.mm(bk[:, :], Whv[:, c, 512 + j * 128:512 + (j + 1) * 128],
                                hT[:, c, tg * 512:(tg + 1) * 512], c == 0, c == 7, [RWh] + rh, [Rb])
                    self.act(sr[hb][:, j, tg * 512:(tg + 1) * 512], bk[:, :], AF.Silu, [Rb], [Rsr[hb][tg]])

                rp = [(lambda tg=tg, j=j: rpiece(tg, j)) for tg in range(4) for j in range(2)]
                return G, rp

            Wh, RWh = self.wnext(hold=2)
            heads = {0: make_head(0, Wh[:, 0:6144].rearrange("p (c n) -> p c n", c=8), RWh)}
            for pc in heads[0][1]:
                pc()
            for g_ in heads[0][0]:
                g_(0)
            for hd in range(4):
                hb = hd % 2
                Wo, RWo = self.wnext(hold=3)
                Wov = Wo[:, 0:2048].rearrange("p (j n) -> p j n", j=2)
                G = heads[hd][0]
                S.op("dve", lambda h: h.memset(Sst[:, :], 0.0), writes=[RS])
                S.op("dve", lambda h: h.memset(Sbf[cur[0]][:, :], 0.0), writes=[RSbf[cur[0]]])

                def chunk(tg, i, hb=hb, Wov=Wov, RWo=RWo):
                    t = tg * 4 + i
                    b2 = tg % 2
                    k4 = kk[0] % 4
                    p2 = kk[0] % 2
                    kk[0] += 1
                    q_, k_, kt_, v_, d_ = qs[b2], ks[b2], kht[b2], vsb[b2], dd[b2]
                    Rv_ = Rv[b2][i // 2]
                    ba, Rba = self.bank()
                    self.mm(ba[:, 0:128], k_[:, i * 128:(i + 1) * 128], q_[:, i * 128:(i + 1) * 128], True, True,
                            [Rks[b2], Rqs[b2]], [Rba])
                    a_ = at[p2]
                    if os.environ.get("ATMASK", "dve") == "pool":
                        S.op("act", lambda h: h.copy(out=a_[:, :], in_=ba[:, 0:128]), [Rba], [Rat[p2]])
                        S.op("pool", lambda h: h.tensor_tensor(out=a_[:, :], in0=a_[:, :], in1=self.tri[:, :],
                                                                op=ALU.mult), [Rat[p2], self.Rconst], [Rat[p2]])
                    else:
                        S.op("dve", lambda h: h.tensor_tensor(out=a_[:, :], in0=ba[:, 0:128], in1=self.tri[:, :],
                                                               op=ALU.mult), [Rba, self.Rconst], [Rat[p2]])
                    bs, Rbs = self.bank()
                    self.mm(bs[:, 0:256], kt_[:, i * 128:(i + 1) * 128], v_[:, i * 256:(i + 1) * 256], True, True,
                            [Rkht[b2], Rv_], [Rbs])
                    bo, Rbo = self.bank()
                    c_ = cur[0]
                    for jv in range(2):
                        self.mm(bo[:, jv * 128:(jv + 1) * 128], Sbf[c_][:, jv * 128:(jv + 1) * 128],
                                q_[:, i * 128:(i + 1) * 128], True, False, [Rqs[b2], RSbf[c_]], [Rbo])
                        self.mm(bo[:, jv * 128:(jv + 1) * 128], v_[:, i * 256 + jv * 128:i * 256 + (jv + 1) * 128],
                                a_[:, :], False, True, [Rat[p2], Rv_], [Rbo])
                    nxt = 1 - c_
                    S.op("dve", lambda h: h.scalar_tensor_tensor(out=Sst[:, :], in0=Sst[:, :],
                                                                  scalar=d_[:, i:i + 1], in1=bs[:, 0:256],
                                                                  op0=ALU.mult, op1=ALU.add),
                         [RS, Rdd[b2], Rbs], [RS])
                    S.op(os.environ.get("SBF_ENG", "pool"), lambda h: (h.copy if hasattr(h, "copy") else h.tensor_copy)(
                        out=Sbf[nxt][:, :], in_=Sst[:, :]), [RS], [RSbf[nxt]])
                    cur[0] = nxt
                    s_, Rs_ = sso[k4], Rsso[k4]
                    self.act(sq[k4][:, :], bo[:, 0:256], AF.Square, [Rbo], [Rsq[k4]])
                    for jv in range(2):
                        S.op("dve", lambda h: h.scalar_tensor_tensor(
                            out=go[k4][:, jv, :], in0=bo[:, jv * 128:(jv + 1) * 128], scalar=gcol[:, jv:jv + 1],
                            in1=sr[hb][:, jv, t * 128:(t + 1) * 128], op0=ALU.mult, op1=ALU.mult),
                             [Rbo, Rgcol, Rsr[hb][tg]], [Rgo[k4]])

                    def n1():
                        bq2, Rbq2 = self.bank()
                        for jv in range(2):
                            self.mm(bq2[:, 0:2], sq[k4][:, jv * 128:(jv + 1) * 128], onesb[:, 0:2], jv == 0, jv == 1,
                                    [Rsq[k4], self.Rconst], [Rbq2])
                        self.act(s_[:, 1:2], bq2[:, 0:1], AF.Ln, [Rbq2], [Rs_], scale=1.0 / HV, bias=EPS)
                        self.act(s_[:, 2:3], s_[:, 1:2], AF.Exp, [Rs_], [Rs_], scale=-0.5)

                    def n4():
                        for nh in range(2):
                            by, Rby = self.bank()
                            for j in range(2):
                                self.mm(by[:, :], go[k4][:, j, :], Wov[:, j, nh * 512:(nh + 1) * 512], j == 0, j == 1,
                                        [Rgo[k4], RWo], [Rby])
                            xs = self.X[:, t, nh * 512:(nh + 1) * 512]
                            if os.environ.get("GLA_XADD", "dve") == "pool":
                                yk = ycnt[0] % 2
                                ycnt[0] += 1
                                self.act(ysb[yk][:, :], by[:, :], AF.Copy, [Rby, Rs_], [Rysb[yk]], scale=s_[:, 2:3])
                                S.op("pool", lambda h: h.tensor_tensor(out=xs, in0=xs, in1=ysb[yk][:, :], op=ALU.add),
                                     [Rysb[yk], self.RX[t]], [self.RX[t]])
                            else:
                                S.op("dve", lambda h: h.scalar_tensor_tensor(out=xs, in0=by[:, :], scalar=s_[:, 2:3],
                                                                              in1=xs, op0=ALU.mult, op1=ALU.add),
                                     [Rby, Rs_, self.RX[t]], [self.RX[t]])

                    defer(1, n1)
                    defer(2, n4)

                nG, nR = None, []
                for tg in range(4):
                    if tg == 3 and hd + 1 < 4:
                        Wn, RWn = self.wnext(hold=2)
                        heads[hd + 1] = make_head(hd + 1, Wn[:, 0:6144].rearrange("p (c n) -> p c n", c=8), RWn)
                        nG, nR = heads[hd + 1][0], list(heads[hd + 1][1])
                    for i in range(4):
                        if tg < 3:
                            G[i](tg + 1)
                        elif nG is not None:
                            for _ in range(2):
                                if nR:
                                    nR.pop(0)()
                            nG[i](0)
                        chunk(tg, i)
                        step[0] += 1
                        run_due()
            run_due(final=True)
            S.barrier()

    def diff_stage(self):
        S, nc, dr = self.S, self.nc, self.dr
        vecs = dr["vecs"]
        with ExitStack() as ls:
            self.open_ring(ls, 3072)
            hT = self.sb("hT", [128, 8, SEQ], BF16, ls)
            RhT = [Res("hT%d" % i) for i in range(NT)]
            with ExitStack() as ns:
                self.norm_stage(1, list(range(NT)), hT, RhT, ns)
                S.barrier()
            lamv = self.sb("lamv", [1, 256], F32, ls)
            lt = self.sb("lt", [1, 128 + 8], F32, ls)
            Rlam = Res("lam")
            S.dma("sp", lamv[:, :], vecs[0:1, V_LAM:V_LAM + 256], writes=[Rlam], owner=Rlam)
            S.op("dve", lambda h: h.tensor_tensor(out=lt[:, 0:128], in0=lamv[:, 0:128], in1=lamv[:, 128:256],
                                                   op=ALU.mult), [Rlam], [Rlam])
            S.op("dve", lambda h: h.tensor_reduce(out=lt[:, 128:130],
                                                   in_=lt[:, 0:128].rearrange("p (a d) -> p a d", a=2),
                                                   axis=AX.X, op=ALU.add), [Rlam], [Rlam])
            self.act(lt[:, 130:132], lt[:, 128:130], AF.Exp, [Rlam], [Rlam])
            S.op("dve", lambda h: h.tensor_tensor(out=lt[:, 132:133], in0=lt[:, 130:131], in1=lt[:, 131:132],
                                                   op=ALU.subtract), [Rlam], [Rlam])
            for cc in (134, 135):
                S.op("dve", lambda h: h.tensor_scalar(out=lt[:, cc:cc + 1], in0=lt[:, 132:133],
                                                       scalar1=LAMBDA_INIT, scalar2=-1.0, op0=ALU.add, op1=ALU.mult),
                     [Rlam], [Rlam])
            bk, Rb = self.bank()
            self.mm(bk[:, 0:2], self.ones32[0:1, :], lt[0:1, 134:136], True, True, [self.Rconst, Rlam], [Rb])
            neglam = self.sb("neglam", [128, 2], F32, ls)
            Rnl = Res("neglam")
            S.op("dve", lambda h: h.tensor_copy(out=neglam[:, :], in_=bk[:, 0:2]), [Rb], [Rnl])
            gn2 = self.sb("gn2", [128, 128], F32, ls)
            Rgn2 = Res("gn2")
            S.dma("sp", gn2[:, :], vecs[0:1, V_DIFFGN:V_DIFFGN + 128].broadcast_to([128, 128]), writes=[Rgn2],
                  owner=Rgn2)
            S.op("dve", lambda h: h.tensor_scalar(out=gn2[:, :], in0=gn2[:, :], scalar1=1.0 - LAMBDA_INIT,
                                                   scalar2=None, op0=ALU.mult), [Rgn2], [Rgn2])

            qz = [self.sb("qz", [128, 8, 2, 256], BF16, ls) for _ in range(2)]
            kT = [self.sb("kT", [128, SEQ], BF16, ls) for _ in range(2)]
            va = [self.sb("va", [128, NT, 130], BF16, ls) for _ in range(2)]
            vb = [self.sb("vb", [128, NT, 130], BF16, ls) for _ in range(2)]
            Rq, Rk, Rva, Rvb = ([Res(n + "0"), Res(n + "1")] for n in ("qz", "kT", "va", "vb"))
            for b_ in range(2):
                S.op("dve", lambda h: h.memset(va[b_][:, :, 128:130], 1.0), writes=[Rva[b_]])
                S.op("dve", lambda h: h.memset(vb[b_][:, :, 128:130], 1.0), writes=[Rvb[b_]])
                S.op("dve", lambda h: h.memset(qz[b_][64:128, :, 0, :], 0.0), writes=[Rq[b_]])
                S.op("dve", lambda h: h.memset(qz[b_][0:64, :, 1, :], 0.0), reads=[Rq[b_]], writes=[Rq[b_]])
            NPT = 2
            pT = [self.sb("pT", [128, NT, 512], BF16, ls) for _ in range(NPT)]
            RpT = [[Res("pT%d_%d" % (b_, i)) for i in range(NT)] for b_ in range(NPT)]
            rl = [self.sb("rl", [128, 2], F32, ls) for _ in range(4)]
            Rrl = [Res("rl%d" % i) for i in range(4)]
            t2 = [self.sb("t2", [128, 128], F32, ls) for _ in range(2)]
            Rt2 = [Res("t20"), Res("t21")]
            of = [self.sb("of", [128, 128], F32, ls) for _ in range(4)]
            Rof = [Res("of%d" % i) for i in range(4)]
            junk3 = self.sb("junk3", [128, 128], BF16, ls)
            Rj3 = Res("junk3")
            sso = [self.sb("ssd", [128, 8], F32, ls) for _ in range(2)]
            Rsso = [Res("ssd0"), Res("ssd1")]
            on = [self.sb("ond", [128, 128], BF16, ls) for _ in range(4)]
            Ron = [Res("ond%d" % i) for i in range(4)]
            oT = [self.sb("oTd", [128, 2, 128], BF16, ls) for _ in range(2)]
            RoT = [Res("oTd0"), Res("oTd1")]

            oTprev = self.gain[:, :].bitcast(BF16)
            RoTp = [self.Rgain] * 8
            genA = [0, 1, 2, 3, 4, 5]
            uu = 0
            step = [0]
            due = []

            def defer(k, fn):
                due.append((step[0] + k, fn))

            def run_due(final=False):
                rest = []
                for d_, fn in due:
                    if final or d_ <= step[0]:
                        fn()
                    else:
                        rest.append((d_, fn))
                due[:] = rest

            def proj_pieces(hd, W, RW):
                hb = hd % 2
                Wv = W[:, 0:3072].rearrange("p (c n) -> p c n", c=8)
                q_, k_, v_, v1_ = qz[hb], kT[hb], va[hb], vb[hb]
                pcs = []

                def pq(tg):
                    rh = RhT[tg * 4:(tg + 1) * 4]
                    bq, Rbq = self.bank(genA)
                    for c in range(8):
                        self.mm(bq[:, :], Wv[:, c, 0:128], hT[:, c, tg * 512:(tg + 1) * 512], c == 0, c == 7,
                                [RW] + rh, [Rbq])
                    self.act(q_[0:64, 2 * tg:2 * tg + 2, 0, :], bq[0:64, :].rearrange("p (g n) -> p g n", g=2),
                             AF.Copy, [Rbq], [Rq[hb]], scale=0.125)
                    S.op("dve", lambda h: h.tensor_scalar(
                        out=q_[64:128, 2 * tg:2 * tg + 2, 1, :], in0=bq[64:128, :].rearrange("p (g n) -> p g n", g=2),
                        scalar1=0.125, scalar2=None, op0=ALU.mult), [Rbq], [Rq[hb]])

                def pk(tg):
                    rh = RhT[tg * 4:(tg + 1) * 4]
                    bkk, Rbkk = self.bank(genA)
                    for c in range(8):
                        self.mm(bkk[:, :], Wv[:, c, 128:256], hT[:, c, tg * 512:(tg + 1) * 512], c == 0, c == 7,
                                [RW] + rh, [Rbkk])
                    S.op("dve", lambda h: h.tensor_copy(out=k_[:, tg * 512:(tg + 1) * 512], in_=bkk[:, :]),
                         [Rbkk], [Rk[hb]])

                def pv(t4):
                    bv, Rbv = self.bank(genA)
                    for ii in range(4):
                        t = t4 * 4 + ii
                        for c in range(8):
                            self.mm(bv[:, ii * 128:(ii + 1) * 128], hT[:, c, t * 128:(t + 1) * 128],
                                    Wv[:, c, 256:384], c == 0, c == 7, [RW, RhT[t]], [Rbv])
                    bv3 = bv[:, :].rearrange("p (i n) -> p i n", i=4)
                    S.op("dve", lambda h: h.tensor_copy(out=v_[:, t4 * 4:(t4 + 1) * 4, 0:128], in_=bv3), [Rbv], [Rva[hb]])
                    S.op("dve", lambda h: h.tensor_scalar(out=v1_[:, t4 * 4:(t4 + 1) * 4, 0:128], in0=bv3,
                                                           scalar1=neglam[:, 0:1], scalar2=None, op0=ALU.mult),
                         [Rbv, Rnl], [Rvb[hb]])

                for tg in range(4):
                    pcs.append(lambda tg=tg: pq(tg))
                    pcs.append(lambda tg=tg: pk(tg))
                    pcs.append(lambda tg=tg: pv(tg))
                return pcs

            nheads = int(os.environ.get('DBG_HEADS', '8'))
            pieces = []
            for hd in range(nheads):
                hb = hd % 2
                if hd == 0:
                    W, RW = self.wnext(hold=2)
                    for pc in proj_pieces(0, W, RW):
                        pc()
                while pieces:
                    pieces.pop(0)()
                run_due(final=True)
                if hd % 2 == 1:
                    prevWo = (Wo, RWo)
                    Wo, RWo = self.wnext(hold=3)
                else:
                    prevWo = (None, None)
                    Wo, RWo = self.wnext(hold=2)
                q_, k_, v_, v1_ = qz[hb], kT[hb], va[hb], vb[hb]

                units = list(range(int(os.environ.get('DBG_G2', '8'))))

                def st_parts(g2, ub, hb=hb, q_=q_, k_=k_):
                    p_ = pT[ub]
                    parts = []
                    qrhs = q_[:, g2, :, :].rearrange("p c n -> p (c n)")

                    def one(j):
                        bs_, Rbs_ = self.bank(genA)
                        self.mm(bs_[:, :], k_[:, j * 128:(j + 1) * 128], qrhs, True, True, [Rk[hb], Rq[hb]], [Rbs_])
                        self.act(p_[:, j, :], bs_[:, :], AF.Exp, [Rbs_], [RpT[ub][j]])
                        if j >= 2 * g2:
                            qi = j - 2 * g2
                            for c in range(2):
                                blk = p_[:, j, c * 256 + qi * 128:c * 256 + (qi + 1) * 128]
                                S.op(os.environ.get("MASK_ENG", "pool"),
                                     lambda h: h.tensor_tensor(out=blk, in0=blk, in1=self.tri[:, :], op=ALU.mult),
                                     [RpT[ub][j], self.Rconst], [RpT[ub][j]])

                    for j in range(2 * g2 + 2):
                        parts.append(lambda j=j: one(j))
                    return parts

                def chain1(g2):
                    for qi in range(2):
                        i = 2 * g2 + qi
                        bi = 6 + (i % 2)
                        acc, Racc = self.banks[bi], self.Rbk[bi]
                        r_, Rr_ = rl[i % 4], Rrl[i % 4]
                        S.op("dve", lambda h: h.reciprocal(
                            out=r_[:, 0:2], in_=acc[:, 0:260].rearrange("p (c n) -> p c n", c=2)[:, :, 128]),
                             [Racc], [Rr_])
                        S.op("dve", lambda h: h.tensor_scalar(out=t2[qi][:, :], in0=acc[:, 130:258],
                                                               scalar1=r_[:, 1:2], scalar2=None, op0=ALU.mult),
                             [Racc, Rr_], [Rt2[qi]])
                        S.op("dve", lambda h: h.scalar_tensor_tensor(out=of[i % 4][:, :], in0=acc[:, 0:128],
                                                                      scalar=r_[:, 0:1], in1=t2[qi][:, :],
                                                                      op0=ALU.mult, op1=ALU.add),
                             [Racc, Rr_, Rt2[qi]], [Rof[i % 4]])

                def chain2(g2):
                    s_, Rs_ = sso[g2 % 2], Rsso[g2 % 2]
                    for qi in range(2):
                        i = 2 * g2 + qi
                        self.act(junk3[:, :], of[i % 4][:, :], AF.Square, [Rof[i % 4]], [Rj3, Rs_],
                                 accum_out=s_[:, qi:qi + 1])
                    self.act(s_[:, 2:4], s_[:, 0:2], AF.Ln, [Rs_], [Rs_], scale=1.0 / 128, bias=SUBLN_EPS)
                    self.act(s_[:, 4:6], s_[:, 2:4], AF.Exp, [Rs_], [Rs_], scale=-0.5)

                def chain3(g2):
                    s_, Rs_ = sso[g2 % 2], Rsso[g2 % 2]
                    for qi in range(2):
                        i = 2 * g2 + qi
                        if os.environ.get("CH3_ENG", "pool") == "pool":
                            S.op("pool", lambda h: h.tensor_tensor(out=of[i % 4][:, :], in0=of[i % 4][:, :],
                                                                    in1=gn2[:, :], op=ALU.mult),
                                 [Rof[i % 4], Rgn2], [Rof[i % 4]])
                            S.op("pool", lambda h: h.tensor_scalar(out=on[i % 4][:, :], in0=of[i % 4][:, :],
                                                                    scalar1=s_[:, 4 + qi:5 + qi], scalar2=1.0,
                                                                    op0=ALU.mult, op1=ALU.mult),
                                 [Rof[i % 4], Rs_], [Ron[i % 4]])
                        else:
                            S.op("dve", lambda h: h.scalar_tensor_tensor(out=on[i % 4][:, :], in0=of[i % 4][:, :],
                                                                          scalar=s_[:, 4 + qi:5 + qi], in1=gn2[:, :],
                                                                          op0=ALU.mult, op1=ALU.mult),
                                 [Rof[i % 4], Rs_, Rgn2], [Ron[i % 4]])

                even = (hd % 2 == 0)

                def epi_a(g2, even=even):
                    btr, Rbtr = self.bank(genA)
                    btrb = btr[:, :].bitcast(BF16)
                    for qi in range(2):
                        i = 2 * g2 + qi
                        self.tr(btrb[:, qi * 128:(qi + 1) * 128], on[i % 4][:, :], [Ron[i % 4]], [Rbtr])
                    if even:
                        S.op("act", lambda h: h.copy(out=oTprev[:, g2 * 256:(g2 + 1) * 256], in_=btrb[:, 0:256]),
                             [Rbtr], [RoTp[g2]])
                    else:
                        ot_, Rot_ = oT[g2 % 2], RoT[g2 % 2]
                        S.op("act", lambda h: h.copy(out=ot_[:, :, :],
                                                      in_=btrb[:, 0:256].rearrange("p (q t) -> p q t", q=2)),
                             [Rbtr], [Rot_])

                def epi_b(g2, qi, Wo=Wo, RWo=RWo, even=even, Wop=prevWo[0], RWop=prevWo[1]):
                    if even:
                        return
                    ot_, Rot_ = oT[g2 % 2], RoT[g2 % 2]
                    i = 2 * g2 + qi
                    for nh in range(2):
                        by, Rby = self.bank(genA)
                        self.mm(by[:, :], oTprev[:, i * 128:(i + 1) * 128], Wop[:, nh * 512:(nh + 1) * 512], True, False,
                                [RoTp[g2], RWop], [Rby])
                        self.mm(by[:, :], ot_[:, qi, :], Wo[:, nh * 512:(nh + 1) * 512], False, True,
                                [Rot_, RWo], [Rby])
                        xs = self.X[:, i, nh * 512:(nh + 1) * 512]
                        S.op("dve", lambda h: h.tensor_tensor(out=xs, in0=by[:, :], in1=xs, op=ALU.add),
                             [Rby, self.RX[i]], [self.RX[i]])

                def pv_parts(g2, ub, hb=hb, v_=v_, v1_=v1_):
                    p_ = pT[ub]
                    parts = []
                    for c in range(2):
                        vv, Rvv = (v_, Rva[hb]) if c == 0 else (v1_, Rvb[hb])
                        for qi in range(2):
                            i = 2 * g2 + qi
                            bi = 6 + (i % 2)
                            acc, Racc = self.banks[bi], self.Rbk[bi]
                            for j in range(i + 1):
                                parts.append(lambda acc=acc, Racc=Racc, j=j, i=i, qi=qi, c=c, vv=vv, Rvv=Rvv: self.mm(
                                    acc[:, c * 130:c * 130 + 129],
                                    p_[:, j, c * 256 + qi * 128:c * 256 + (qi + 1) * 128], vv[:, j, 0:129],
                                    j == 0, j == i,
                                    (RpT[ub][0:2 * g2 + 2] if os.environ.get("PV_COARSE") else [RpT[ub][j]]) + [Rvv],
                                    [Racc]))
                    return parts

                def pv_tail(g2):
                    chain1(g2)
                    defer(1, lambda g2=g2: chain2(g2))
                    defer(1, lambda g2=g2: chain3(g2))
                    defer(2, lambda g2=g2: epi_a(g2))
                    defer(2, lambda g2=g2: epi_b(g2, 0))
                    defer(3, lambda g2=g2: epi_b(g2, 1))

                for pc in st_parts(units[0], uu % NPT):
                    pc()
                for n_ in range(len(units)):
                    if n_ == 1 and hd + 1 < nheads:
                        Wn, RWn = self.wnext(hold=4 if hd % 2 == 1 else 2)
                        pieces = proj_pieces(hd + 1, Wn, RWn)
                    sts = st_parts(units[n_ + 1], (uu + 1) % NPT) if n_ + 1 < len(units) else []
                    pvs = pv_parts(units[n_], uu % NPT)
                    per = -(-len(pvs) // max(1, len(sts)))
                    for st_ in sts:
                        st_()
                        for _ in range(per):
                            if pvs:
                                pvs.pop(0)()
                    while pvs:
                        pvs.pop(0)()
                    pv_tail(units[n_])
                    for _ in range(2):
                        if n_ >= 1 and pieces:
                            pieces.pop(0)()
                    uu += 1
                    step[0] += 1
                    run_due()
            run_due(final=True)
            S.barrier()

    def final_alloc(self, stack):
        self.ob = [self.sb("ob", [128, D], F32, stack) for _ in range(2)]
        self.Rob = [Res("ob0"), Res("ob1")]
        self.junkf = self.sb("junkf", [128, D], BF16, stack)
        self.Rjunkf = Res("junkf")

    def final_stats(self, tiles, load_gain=True):
        S = self.S
        if load_gain:
            S.dma("sp", self.gain[:, :], self.dr["vecs"][0:1, 4 * D:5 * D].broadcast_to([128, D]),
                  writes=[self.Rgain], owner=self.Rgain)
        lo, hi = tiles[0], tiles[-1] + 1
        for t in tiles:
            self.act(self.junkf[:, :], self.X[:, t, :], AF.Square, [self.RX[t]], [self.Rjunkf, self.Rss],
                     accum_out=self.ss[:, t:t + 1])
        self.act(self.lnss[:, lo:hi], self.ss[:, lo:hi], AF.Ln, [self.Rss], [self.Rln], scale=1.0 / D, bias=EPS)
        self.act(self.rstd[:, lo:hi], self.lnss[:, lo:hi], AF.Exp, [self.Rln], [self.Rrstd], scale=-0.5)

    def final_tile(self, t):
        S = self.S
        o_, Ro_ = self.ob[t % 2], self.Rob[t % 2]
        if self.final:
            S.op("dve", lambda h: h.scalar_tensor_tensor(out=o_[:, :], in0=self.X[:, t, :],
                                                          scalar=self.rstd[:, t:t + 1], in1=self.gain[:, :],
                                                          op0=ALU.mult, op1=ALU.mult),
                 [self.RX[t], self.Rrstd, self.Rgain], [Ro_])
        else:
            S.op("dve", lambda h: h.tensor_copy(out=o_[:, :], in_=self.X[:, t, :]), [self.RX[t]], [Ro_])
        S.dma("sp", self.out_ap[t * 128:(t + 1) * 128, :], o_[:, :], reads=[Ro_], owner=Ro_)

    def final_stage(self, out):
        S = self.S
        with ExitStack() as ls:
            self.final_alloc(ls)
            tiles = [t for t in range(NT) if t not in self.final_done]
            if self.final and tiles:
                self.final_stats(tiles)
            for t in tiles:
                self.final_tile(t)
            S.finish()


def prep_weights(inp):
    f = lambda a: np.ascontiguousarray(np.asarray(a, dtype=np.float32))
    w = {}
    gw = f(inp["gla_w_in"])[0].reshape(8, 128, 3072)
    heads = []
    for h in range(4):
        cols = np.concatenate([gw[:, :, h * 128:(h + 1) * 128],
                               gw[:, :, 512 + h * 128:512 + (h + 1) * 128],
                               gw[:, :, 1024 + h * 256:1024 + (h + 1) * 256],
                               gw[:, :, 2048 + h * 256:2048 + (h + 1) * 256]], axis=2)
        heads.append(cols.transpose(1, 0, 2).reshape(128, 6144))
    w["w_gla_h"] = f(np.stack(heads))
    go = f(inp["gla_w_out"])[0].reshape(4, 2, 128, 1024)
    w["w_gla_o"] = f(go.transpose(0, 2, 1, 3).reshape(4, 128, 2048))
    ga = f(inp["gla_w_gate_a"])[0].reshape(8, 128, 16)
    w["w_gla_a"] = f(ga.transpose(1, 0, 2).reshape(128, 128))
    wbm = np.zeros((128, 512), np.float32)
    wbm[0:16] = f(inp["gla_w_gate_b"])[0]
    wbm[32] = f(inp["gla_b_gate"])[0]
    w["w_gla_b"] = wbm
    dw = f(inp["diff_w_in"])[0].reshape(8, 128, 3072)
    heads = []
    for h in range(8):
        cols = np.concatenate([dw[:, :, h * 128:(h + 1) * 128],
                               dw[:, :, 1024 + h * 128:1024 + (h + 1) * 128],
                               dw[:, :, 2048 + h * 128:2048 + (h + 1) * 128]], axis=2)
        heads.append(cols.transpose(1, 0, 2).reshape(128, 3072))
    w["w_diff_h"] = f(np.stack(heads))
    w["w_diff_o"] = f(f(inp["diff_w_out"])[0].reshape(8, 128, 1024))
    fi = f(inp["ffn_w_in"]).reshape(2, 8, 128, 2, 11, 256)
    w["w_ffn_in"] = f(fi.transpose(0, 4, 2, 1, 3, 5).reshape(2, 11, 128, 4096))
    fo = f(inp["ffn_w_out"]).reshape(2, NF, 128, 4, 256)
    w["w_ffn_out"] = f(fo.transpose(0, 3, 2, 1, 4).reshape(2, 4, 128, 5632))
    vec = np.zeros((1, V_TOT), np.float32)
    vec[0, 0:2 * D] = f(inp["norm_mixer"]).reshape(-1)
    vec[0, 2 * D:4 * D] = f(inp["norm_ffn"]).reshape(-1)
    vec[0, 4 * D:5 * D] = f(inp["norm_final"]).reshape(-1)
    vec[0, V_GLAGN:V_GLAGN + 256] = f(inp["gla_norm"]).reshape(-1)
    vec[0, V_DIFFGN:V_DIFFGN + 128] = f(inp["diff_norm"]).reshape(-1)
    vec[0, V_LAM:V_LAM + 64] = f(inp["diff_lam_q1"]).reshape(-1)
    vec[0, V_LAM + 64:V_LAM + 128] = f(inp["diff_lam_q2"]).reshape(-1)
    vec[0, V_LAM + 128:V_LAM + 192] = f(inp["diff_lam_k1"]).reshape(-1)
    vec[0, V_LAM + 192:V_LAM + 256] = f(inp["diff_lam_k2"]).reshape(-1)
    w["vecs"] = vec
    return w


def run(inputs, stages=("gla", "ffn0", "diff", "ffn1"), final=True, cores=8, trace=False):
    b = Builder(stages, final)
    nc = b.build()
    w = prep_weights(inputs)
    x = np.asarray(inputs["x"], dtype=np.float32)
    in_maps = []
    for i in range(cores):
        m = dict(w)
        m["x"] = np.ascontiguousarray(x[i])
        in_maps.append(m)
    res = run_bass_kernel_spmd(nc, in_maps, core_ids=list(range(cores)), trace=trace)
    outs = np.stack([np.asarray(r["out"], dtype=np.float32) for r in res.results], axis=0)
    return outs, res, b


def kernel(**inputs):
    outs, _, _ = run(inputs)
    return outs
```

```python
import math
import os
import numpy as np
import concourse.bass as bass
import concourse.mybir as mybir
from concourse.bass_utils import run_bass_kernel_spmd
from contextlib import ExitStack

F32 = mybir.dt.float32
BF16 = mybir.dt.bfloat16
AF = mybir.ActivationFunctionType
ALU = mybir.AluOpType
AX = mybir.AxisListType

D = 1024
SEQ = 2048
NT = 16
DFF = 2816
NF = 22
HK = 128
HV = 256
EPS = 1e-6
SUBLN_EPS = 1e-5
LAMBDA_INIT = 0.8 - 0.6 * math.exp(-0.3 * 1)
RING_N = 6144
NRING = 4
NDMASEM = 48

V_NORM = 0
V_GLAGN = 5 * 1024
V_DIFFGN = V_GLAGN + 256
V_LAM = V_DIFFGN + 128
V_TOT = V_LAM + 256


class Tok:
    __slots__ = ("sem", "key", "val", "clock")

    def __init__(self, sem, key, val, clock):
        self.sem, self.key, self.val, self.clock = sem, key, val, clock


class Res:
    __slots__ = ("name", "w", "r", "dma", "excl")

    def __init__(self, name, excl=False):
        self.name = name
        self.w = None
        self.r = {}
        self.dma = None
        self.excl = excl


class Eng:
    def __init__(self, name, h, sem, selfdep):
        self.name, self.h, self.sem, self.selfdep = name, h, sem, selfdep
        self.key = "E_" + name
        self.count = 0
        self.clock = {}
        self.last = None
        self.nwait = 0


class Sched:
    def __init__(self, nc, stack):
        self.nc = nc
        self.stack = stack
        self.engs = {}
        for name, h, selfdep in [
            ("pe", nc.tensor, False),
            ("act", nc.scalar, True),
            ("dve", nc.vector, True),
            ("pool", nc.gpsimd, True),
            ("sp", nc.sync, True),
        ]:
            sem = stack.enter_context(nc.semaphore("s_" + name))
            self.engs[name] = Eng(name, h, sem, selfdep)
        self.dmas = []
        self.ndma = 0
        self.dma_pool = [stack.enter_context(nc.semaphore("d_%d" % i)) for i in range(NDMASEM)]
        self.all_sems = [e.sem for e in self.engs.values()] + self.dma_pool
        self.clear_sems()
        nc.all_engine_barrier()

    def clear_sems(self):
        for s in self.all_sems:
            self.nc.gpsimd.sem_clear(s)

    def _sync(self, e, deps):
        need = {}
        for t in deps:
            if t is None:
                continue
            if t.key == e.key and not e.selfdep:
                continue
            if e.clock.get(t.key, 0) >= t.val:
                continue
            cur = need.get(t.key)
            if cur is None or cur.val < t.val:
                need[t.key] = t
        for key, t in need.items():
            if e.clock.get(key, 0) >= t.val:
                continue
            e.h.wait_ge(t.sem, t.val)
            e.nwait += 1
            for k, v in t.clock.items():
                if e.clock.get(k, 0) < v:
                    e.clock[k] = v
            e.clock[key] = t.val

    @staticmethod
    def _deps(reads, writes, ekey=None):
        deps = []
        for r in reads:
            if r.w is not None:
                deps.append(r.w)
            if r.excl:
                deps.extend(t for k, t in r.r.items() if k != ekey)
        for w in writes:
            if w.w is not None:
                deps.append(w.w)
            deps.extend(w.r.values())
        return deps

    @staticmethod
    def _update(tok, reads, writes):
        for r in reads:
            r.r[tok.key] = tok
        for w in writes:
            w.w = tok
            w.r = {}

    def op(self, en, fn, reads=(), writes=()):
        e = self.engs[en]
        self._sync(e, self._deps(reads, writes, e.key))
        ins = fn(e.h)
        e.count += 1
        ins.then_inc(e.sem, 1)
        tok = Tok(e.sem, e.key, e.count, dict(e.clock))
        e.last = tok
        self._update(tok, reads, writes)
        return tok

    def dma(self, qn, out, in_, reads=(), writes=(), owner=None, **kw):
        e = self.engs[qn]
        self._sync(e, self._deps(reads, writes))
        ins = e.h.dma_start(out=out, in_=in_, **kw)
        if owner.dma is None:
            sem = self.dma_pool[len(self.dmas)]
            owner.dma = [sem, "D_%d" % len(self.dmas), 0, None]
            self.dmas.append(owner.dma)
        d = owner.dma
        d[2] += 16
        ins.then_inc(d[0], 16)
        tok = Tok(d[0], d[1], d[2], dict(e.clock))
        d[3] = tok
        self.ndma += 1
        self._update(tok, reads, writes)
        return tok

    def prewait(self, en, writes):
        e = self.engs[en]
        self._sync(e, self._deps([], writes, e.key))

    def all_toks(self):
        toks = [e.last for e in self.engs.values() if e.last is not None]
        toks += [d[3] for d in self.dmas if d[3] is not None]
        return toks

    def barrier(self):
        toks = self.all_toks()
        for e in self.engs.values():
            self._sync(e, toks)

    def finish(self):
        self._sync(self.engs["sp"], self.all_toks())

    def cleanup(self):
        self.nc.all_engine_barrier()
        self.clear_sems()


class Builder:
    def __init__(self, stages=("gla", "ffn0", "diff", "ffn1"), final=True):
        self.stages = stages
        self.final = final

    def sb(self, name, shape, dt, stack=None):
        stack = stack or self.st
        self._n += 1
        return stack.enter_context(self.nc.sbuf_tensor("%s_%d" % (name, self._n), shape, dt))

    def bank(self, pool=None):
        pool = pool or self.gen_banks
        i = pool[self._bk % len(pool)]
        self._bk += 1
        return self.banks[i], self.Rbk[i]

    def mm(self, out, lhsT, rhs, start, stop, reads, writes):
        self.S.op("pe", lambda h: h.matmul(out, lhsT=lhsT, rhs=rhs, start=start, stop=stop),
                  reads, writes)

    def tr(self, out, in_, reads, writes):
        self.S.op("pe", lambda h: h.transpose(out=out, in_=in_, identity=self.ident[:, :]),
                  list(reads) + [self.Rconst], writes)

    def act(self, out, in_, func, reads, writes, **kw):
        self.S.op("act", lambda h: h.activation(out=out, in_=in_, func=func, **kw), reads, writes)

    def open_ring(self, stack, nelem):
        self.ring = [self.sb("ring", [128, nelem], BF16, stack) for _ in range(NRING)]
        self.Rring = [Res("ring%d" % i) for i in range(NRING)]
        self.wbase = self.wpos
        self.wend = self.wstage_end.pop(0)
        if self.wbase == 0:
            gate = int(os.environ.get("XGATE", "-1"))
            if gate >= 0:
                self.S._sync(self.S.engs["pool"], [self.RX[gate].w])
        self._wissue(min(self.wbase + NRING - 1, self.wend))

    def _wissue(self, upto):
        while self.wissued <= upto:
            k = self.wissued
            ap, n = self.wseq[k]
            slot = (k - self.wbase) % NRING
            self.S.dma("pool", self.ring[slot][:, 0:n], ap, writes=[self.Rring[slot]],
                       owner=self.Rring[slot], max_dma_last_dim=2048)
            self.wissued += 1

    def wnext(self, hold=1):
        i = self.wpos
        self.wpos += 1
        self._wissue(min(i + NRING - hold, self.wend))
        slot = (i - self.wbase) % NRING
        return self.ring[slot], self.Rring[slot]

    def build(self):
        nc = bass.Bass("TRN2", target_bir_lowering=False)
        self.nc = nc
        self._n = 0
        self._bk = 0
        dr = {}

        def din(name, shape):
            dr[name] = nc.dram_tensor(name, shape, F32, kind="ExternalInput").ap()

        din("x", [SEQ, D])
        din("w_gla_h", [4, 128, 6144])
        din("w_gla_o", [4, 128, 2048])
        din("w_gla_a", [128, 128])
        din("w_gla_b", [128, 512])
        din("w_diff_h", [8, 128, 3072])
        din("w_diff_o", [8, 128, 1024])
        din("w_ffn_in", [2, 11, 128, 4096])
        din("w_ffn_out", [2, 4, 128, 5632])
        din("vecs", [1, V_TOT])
        out = nc.dram_tensor("out", [SEQ, D], F32, kind="ExternalOutput").ap()
        self.dr = dr
        self.out_ap = out
        self.final_done = set()
        self.last_stage = self.stages[-1] if self.stages else None

        wseq = []
        for stg in self.stages:
            if stg == "gla":
                for h in range(4):
                    wseq.append((dr["w_gla_h"][h], 6144))
                    wseq.append((dr["w_gla_o"][h], 2048))
            elif stg == "diff":
                for h in range(8):
                    wseq.append((dr["w_diff_h"][h], 3072))
                    wseq.append((dr["w_diff_o"][h], 1024))
            else:
                l = int(stg[-1])
                for hf in range(2):
                    for fb in range(11):
                        wseq.append((dr["w_ffn_in"][l, fb], 4096))
                    for nq in range(4):
                        wseq.append((dr["w_ffn_out"][l, nq], 5632))
        self.wseq = wseq
        self.wpos = 0
        self.wissued = 0
        self.wstage_end = []
        acc_ = 0
        for stg in self.stages:
            acc_ += {"gla": 8, "diff": 16}.get(stg, 30)
            self.wstage_end.append(acc_ - 1)

        with ExitStack() as st:
            self.st = st
            S = Sched(nc, st)
            self.S = S
            self.X = self.sb("X", [128, NT, D], F32)
            self.RX = [Res("X%d" % t) for t in range(NT)]
            self.ident = self.sb("ident", [128, 128], BF16)
            self.tri = self.sb("tri", [128, 128], BF16)
            self.U32 = self.sb("U32", [128, 128], F32)
            self.ones32 = self.sb("ones32", [128, 128], F32)
            self.Rconst = Res("const")
            self.gain = self.sb("gain", [128, D], F32)
            self.Rgain = Res("gain")
            self.ss = self.sb("ss", [128, NT], F32)
            self.lnss = self.sb("lnss", [128, NT], F32)
            self.rstd = self.sb("rstd", [128, NT], F32)
            self.Rss, self.Rln, self.Rrstd = Res("ss"), Res("lnss"), Res("rstd")
            self.banks = [st.enter_context(nc.psum_tensor("bk%d" % i, [128, 512], F32)) for i in range(8)]
            self.Rbk = [Res("bk%d" % i, excl=True) for i in range(8)]
            self.gen_banks = list(range(8))

            Rc = self.Rconst
            S.op("pool", lambda h: h.memset(self.ident[:, :], 1.0), writes=[Rc])
            S.op("pool", lambda h: h.affine_select(out=self.ident[:, :], in_=self.ident[:, :], pattern=[[1, 128]],
                                                    compare_op=ALU.is_equal, fill=0.0, base=0,
                                                    channel_multiplier=-1), reads=[Rc], writes=[Rc])
            S.op("pool", lambda h: h.memset(self.tri[:, :], 1.0), reads=[Rc], writes=[Rc])
            S.op("pool", lambda h: h.affine_select(out=self.tri[:, :], in_=self.tri[:, :], pattern=[[1, 128]],
                                                    compare_op=ALU.is_ge, fill=0.0, base=0,
                                                    channel_multiplier=-1), reads=[Rc], writes=[Rc])
            S.op("pool", lambda h: h.memset(self.U32[:, :], 1.0), reads=[Rc], writes=[Rc])
            S.op("pool", lambda h: h.affine_select(out=self.U32[:, :], in_=self.U32[:, :], pattern=[[1, 128]],
                                                    compare_op=ALU.is_ge, fill=0.0, base=0,
                                                    channel_multiplier=-1), reads=[Rc], writes=[Rc])
            S.op("pool", lambda h: h.memset(self.ones32[:, :], 1.0), reads=[Rc], writes=[Rc])

            for t in range(NT):
                S.dma("sp", self.X[:, t, :], dr["x"][t * 128:(t + 1) * 128, :], writes=[self.RX[t]],
                      owner=self.RX[t])

            for stg in self.stages:
                if stg == "gla":
                    self.gla_stage()
                elif stg == "diff":
                    self.diff_stage()
                else:
                    self.ffn_stage(int(stg[-1]))

            self.final_stage(out)
            S.finish()
            S.cleanup()
        self.stats = {k: (e.count, e.nwait) for k, e in S.engs.items()}
        self.stats["ndma"] = S.ndma
        return nc

    def norm_alloc(self, ls):
        junk = self.sb("junk", [128, D], BF16, ls)
        hbf = [self.sb("hbf", [128, D], BF16, ls) for _ in range(2)]
        return (junk, Res("junk"), hbf, [Res("hbf0"), Res("hbf1")])

    def norm_stats(self, gidx, tiles, tmp, li0=0, load_gain=True, skip_sq=False):
        S = self.S
        junk, Rjunk, _, _ = tmp
        vecs = self.dr["vecs"]
        if load_gain:
            S.dma("sp", self.gain[:, :], vecs[0:1, gidx * D:(gidx + 1) * D].broadcast_to([128, D]),
                  writes=[self.Rgain], owner=self.Rgain)
        n = len(tiles)
        for k, t in enumerate(tiles):
            li = li0 + k
            if not skip_sq:
                self.act(junk[:, :], self.X[:, t, :], AF.Square, [self.RX[t]], [Rjunk, self.Rss],
                         accum_out=self.ss[:, li:li + 1])
        self.act(self.lnss[:, li0:li0 + n], self.ss[:, li0:li0 + n], AF.Ln, [self.Rss], [self.Rln],
                 scale=1.0 / D, bias=EPS)
        self.act(self.rstd[:, li0:li0 + n], self.lnss[:, li0:li0 + n], AF.Exp, [self.Rln], [self.Rrstd], scale=-0.5)

    def norm_tile(self, li, t, hT, RhT, tmp, part=None):
        S = self.S
        _, _, hbf, Rhbf = tmp
        hb, Rhb = hbf[li % 2], Rhbf[li % 2]
        if part in (None, "a"):
            S.op("dve", lambda h: h.scalar_tensor_tensor(out=hb[:, :], in0=self.X[:, t, :],
                                                          scalar=self.rstd[:, li:li + 1], in1=self.gain[:, :],
                                                          op0=ALU.mult, op1=ALU.mult),
                 [self.RX[t], self.Rrstd, self.Rgain], [Rhb])
        if part == "a":
            return
        bk, Rb = self.bank()
        bkb = bk[:, :].bitcast(BF16)
        for c in range(8):
            self.tr(bkb[:, c * 128:(c + 1) * 128], hb[:, c * 128:(c + 1) * 128], [Rhb], [Rb])
        S.op("act", lambda h: h.copy(out=hT[:, :, li * 128:(li + 1) * 128],
                                      in_=bkb.rearrange("p (c t) -> p c t", c=8)), [Rb], [RhT[li]])

    def norm_stage(self, gidx, tiles, hT, RhT, ls, batch=16):
        tmp = self.norm_alloc(ls)
        skip_sq = getattr(self, "sq_ready", False) and len(tiles) == NT
        self.sq_ready = False
        for b0 in range(0, len(tiles), batch):
            self.norm_stats(gidx, tiles[b0:b0 + batch], tmp, li0=b0, load_gain=(b0 == 0), skip_sq=skip_sq)
            for k, t in enumerate(tiles[b0:b0 + batch]):
                self.norm_tile(b0 + k, t, hT, RhT, tmp)

    def ffn_stage(self, l):
        S = self.S
        with ExitStack() as ls:
            self.open_ring(ls, 5632)
            hTs = [self.sb("hTf", [128, 8, 1024], BF16, ls) for _ in range(2)]
            RhTs = [[Res("hTf%d_%d" % (h_, i)) for i in range(8)] for h_ in range(2)]
            aT = self.sb("aT", [128, NF, 1024], BF16, ls)
            RaT = [Res("aT0"), Res("aT1")]
            sg = [self.sb("sg", [128, 512], BF16, ls) for _ in range(2)]
            Rsg = [Res("sg0"), Res("sg1")]
            ntmp = self.norm_alloc(ls)
            early_final = self.final and self.last_stage == "ffn%d" % l
            si = self.stages.index("ffn%d" % l)
            early_next = (not early_final) and si + 1 < len(self.stages) and self.stages[si + 1] in ("gla", "diff")

            def early_sq(t):
                self.act(ntmp[0][:, :], self.X[:, t, :], AF.Square, [self.RX[t]], [ntmp[1], self.Rss],
                         accum_out=self.ss[:, t:t + 1])
            if early_final:
                self.final_alloc(ls)
            self.norm_stats(2 + l, list(range(0, 8)), ntmp)
            for li in range(8):
                self.norm_tile(li, li, hTs[0], RhTs[0], ntmp)
            k = 0
            for hf in range(2):
                hT, RhT = hTs[hf], RhTs[hf]
                for fb in range(11):
                    if hf == 1 and early_next and fb == 1:
                        for t_ in range(8):
                            early_sq(t_)
                    if hf == 1 and early_final:
                        if fb == 1:
                            self.final_stats(list(range(0, 8)))
                        if 2 <= fb < 10:
                            self.final_tile(fb - 2)
                            self.final_done.add(fb - 2)
                    if hf == 0:
                        if fb == 1:
                            self.norm_stats(2 + l, list(range(8, 16)), ntmp)
                        if 3 <= fb < 11:
                            self.norm_tile(fb - 3, 8 + fb - 3, hTs[1], RhTs[1], ntmp, part="b")
                        if 2 <= fb < 10:
                            self.norm_tile(fb - 2, 8 + fb - 2, hTs[1], RhTs[1], ntmp, part="a")
                    W, RW = self.wnext()
                    Wv = W[:, 0:4096].rearrange("p (c g n) -> p c g n", c=8, g=2)
                    for tg in range(2):
                        rh = RhT[tg * 4:(tg + 1) * 4]
                        for j in range(2):
                            bg, Rbg = self.bank()
                            bu, Rbu = self.bank()
                            for c in range(8):
                                self.mm(bg[:, :], Wv[:, c, 0, j * 128:(j + 1) * 128],
                                        hT[:, c, tg * 512:(tg + 1) * 512], c == 0, c == 7, [RW] + rh, [Rbg])
                            for c in range(8):
                                self.mm(bu[:, :], Wv[:, c, 1, j * 128:(j + 1) * 128],
                                        hT[:, c, tg * 512:(tg + 1) * 512], c == 0, c == 7, [RW] + rh, [Rbu])
                            s_, Rs_ = sg[k % 2], Rsg[k % 2]
                            k += 1
                            self.act(s_[:, :], bg[:, :], AF.Silu, [Rbg], [Rs_])
                            S.op("dve", lambda h: h.tensor_tensor(out=aT[:, fb * 2 + j, tg * 512:(tg + 1) * 512],
                                                                   in0=bu[:, :], in1=s_[:, :], op=ALU.mult),
                                 [Rbu, Rs_], [RaT[tg]])
                if hf == 1 and (early_final or early_next):
                    Ws = []
                    for nq in range(4):
                        W, RW = self.wnext(hold=nq + 1)
                        Ws.append((W[:, 0:5632].rearrange("p (f n) -> p f n", f=NF), RW))
                    prev_t = None
                    for tl in range(8):
                        t = 8 * hf + tl
                        for nq in range(4):
                            Wv, RW = Ws[nq]
                            bk, Rb = self.bank()
                            for f in range(NF):
                                self.mm(bk[:, 0:256], aT[:, f, tl * 128:(tl + 1) * 128], Wv[:, f, :],
                                        f == 0, f == NF - 1, [RaT[tl // 4], RW], [Rb])
                            xs = self.X[:, t, nq * 256:(nq + 1) * 256]
                            S.op("dve", lambda h: h.tensor_tensor(out=xs, in0=bk[:, 0:256], in1=xs, op=ALU.add),
                                 [Rb, self.RX[t]], [self.RX[t]])
                        if prev_t is not None:
                            if early_final:
                                self.final_stats([prev_t], load_gain=False)
                                self.final_tile(prev_t)
                                self.final_done.add(prev_t)
                            else:
                                early_sq(prev_t)
                        prev_t = t
                    if early_final:
                        self.final_stats([prev_t], load_gain=False)
                        self.final_tile(prev_t)
                        self.final_done.add(prev_t)
                    else:
                        early_sq(prev_t)
                        self.sq_ready = True
                else:
                    for nq in range(4):
                        W, RW = self.wnext()
                        Wv = W[:, 0:5632].rearrange("p (f n) -> p f n", f=NF)
                        for tl in range(8):
                            t = 8 * hf + tl
                            bk, Rb = self.bank()
                            for f in range(NF):
                                self.mm(bk[:, 0:256], aT[:, f, tl * 128:(tl + 1) * 128], Wv[:, f, :],
                                        f == 0, f == NF - 1, [RaT[tl // 4], RW], [Rb])
                            xs = self.X[:, t, nq * 256:(nq + 1) * 256]
                            S.op("dve", lambda h: h.tensor_tensor(out=xs, in0=bk[:, 0:256], in1=xs, op=ALU.add),
                                 [Rb, self.RX[t]], [self.RX[t]])
            S.barrier()

    def gla_stage(self):
        S, nc, dr = self.S, self.nc, self.dr
        with ExitStack() as ls:
            self.open_ring(ls, 6144)
            hT = self.sb("hT", [128, 8, SEQ], BF16, ls)
            RhT = [Res("hT%d" % i) for i in range(NT)]
            self.norm_stage(0, list(range(NT)), hT, RhT, ls)
            wa = self.sb("wa", [128, 128], BF16, ls)
            wb = self.sb("wb", [128, 512], BF16, ls)
            gnb = self.sb("gnb", [128, HV], F32, ls)
            Rwa, Rwb, Rgnb = Res("wa"), Res("wb"), Res("gnb")
            S.dma("pool", wa[:, :], dr["w_gla_a"][:, :], writes=[Rwa], owner=Rwa)
            S.dma("pool", wb[:, :], dr["w_gla_b"][:, :], writes=[Rwb], owner=Rwb)
            S.dma("sp", gnb[:, :], dr["vecs"][0:1, V_GLAGN:V_GLAGN + HV].broadcast_to([128, HV]),
                  writes=[Rgnb], owner=Rgnb)
            uaug = self.sb("uaug", [128, SEQ], BF16, ls)
            Ru = Res("uaug")
            S.op("dve", lambda h: h.memset(uaug[:, :], 0.0), writes=[Ru])
            S.op("dve", lambda h: h.memset(uaug[32:33, :], 1.0), reads=[Ru], writes=[Ru])
            for tg in range(4):
                bk, Rb = self.bank()
                for c in range(8):
                    self.mm(bk[0:16, :], wa[:, c * 16:(c + 1) * 16], hT[:, c, tg * 512:(tg + 1) * 512],
                            c == 0, c == 7, [Rwa] + RhT[tg * 4:(tg + 1) * 4], [Rb])
                S.op("act", lambda h: h.copy(out=uaug[0:16, tg * 512:(tg + 1) * 512], in_=bk[0:16, :]), [Rb], [Ru])

            sr = [self.sb("sr", [128, 2, SEQ], BF16, ls) for _ in range(2)]
            Rsr = [[Res("sr%d_%d" % (b_, i)) for i in range(4)] for b_ in range(2)]
            tmpA = self.sb("tmpA", [128, 512], F32, ls)
            tmpB = self.sb("tmpB", [128, 512], F32, ls)
            RtA, RtB = Res("tmpA"), Res("tmpB")
            dd = [self.sb("dd", [128, 4], F32, ls) for _ in range(2)]
            Rdd = [Res("dd0"), Res("dd1")]
            qs = [self.sb("qs", [128, 512], BF16, ls) for _ in range(2)]
            ks = [self.sb("ks", [128, 512], BF16, ls) for _ in range(2)]
            kh = [self.sb("kh", [128, 512], BF16, ls) for _ in range(2)]
            kht = [self.sb("kht", [128, 512], BF16, ls) for _ in range(2)]
            vsb = [self.sb("vsb", [128, 1024], BF16, ls) for _ in range(2)]
            Rqs, Rks, Rkh, Rkht = ([Res(n + "0"), Res(n + "1")] for n in ("qs", "ks", "kh", "kht"))
            Rv = [[Res("v%d_%d" % (b_, i)) for i in range(2)] for b_ in range(2)]
            at = [self.sb("at", [128, 128], BF16, ls) for _ in range(2)]
            Rat = [Res("at0"), Res("at1")]
            Sst = self.sb("Sst", [128, HV], F32, ls)
            RS = Res("S")
            Sbf = [self.sb("Sbf", [128, HV], BF16, ls) for _ in range(2)]
            RSbf = [Res("Sbf0"), Res("Sbf1")]
            sso = [self.sb("sso", [128, 4], F32, ls) for _ in range(4)]
            Rsso = [Res("sso%d" % i) for i in range(4)]
            go = [self.sb("go", [128, 2, 128], BF16, ls) for _ in range(4)]
            Rgo = [Res("go%d" % i) for i in range(4)]
            sq = [self.sb("sq", [128, HV], BF16, ls) for _ in range(4)]
            Rsq = [Res("sq%d" % i) for i in range(4)]
            gcol = self.sb("gcol", [128, 2], F32, ls)
            Rgcol = Res("gcol")
            for jv in range(2):
                S.dma("sp", gcol[:, jv:jv + 1],
                      dr["vecs"][0:1, V_GLAGN + jv * 128:V_GLAGN + (jv + 1) * 128].rearrange("o (p q) -> (o p) q", q=1),
                      writes=[Rgcol], owner=Rgcol)
            ysb = [self.sb("ysb", [128, 512], F32, ls) for _ in range(2)]
            Rysb = [Res("ysb0"), Res("ysb1")]
            ycnt = [0]
            onesb = self.sb("onesb", [128, 2], BF16, ls)
            S.op("dve", lambda h: h.memset(onesb[:, :], 1.0), reads=[self.Rconst], writes=[self.Rconst])

            step = [0]
            due = []

            def defer(k, fn):
                if os.environ.get('DBG_NODEFER'):
                    fn()
                    return
                due.append((step[0] + k, fn))

            def run_due(final=False):
                rest = []
                for d_, fn in due:
                    if final or d_ <= step[0]:
                        fn()
                    else:
                        rest.append((d_, fn))
                due[:] = rest

            kk = [0]
            cur = [0]

            def make_head(hd, Whv, RWh):
                hb = hd % 2
                def vproj(tg, i2):
                    b2 = tg % 2
                    bv, Rbv = self.bank()
                    for ii in range(2):
                        t = tg * 4 + i2 * 2 + ii
                        for c in range(8):
                            self.mm(bv[:, ii * 256:(ii + 1) * 256], hT[:, c, t * 128:(t + 1) * 128],
                                    Whv[:, c, 256:512], c == 0, c == 7, [RWh, RhT[t]], [Rbv])
                    S.op("act", lambda h: h.copy(out=vsb[b2][:, i2 * 512:(i2 + 1) * 512], in_=bv[:, :]),
                         [Rbv], [Rv[b2][i2]])

                def G1(tg):
                    bx, Rbx = self.bank()
                    for i in range(4):
                        t = tg * 4 + i
                        self.mm(bx[:, i * 128:(i + 1) * 128], uaug[:, t * 128:(t + 1) * 128],
                                wb[:, hd * 128:(hd + 1) * 128], True, True, [Ru, Rwb], [Rbx])
                    self.act(tmpA[:, :], bx[:, :], AF.Exp, [Rbx], [RtA], scale=-1.0)
                    self.act(tmpB[:, :], tmpA[:, :], AF.Ln, [RtA], [RtB], bias=1.0)
                    if not os.environ.get('DBG_VLATE'):
                        vproj(tg, 0)

                def G2(tg):
                    b2 = tg % 2
                    bc, Rbc = self.bank()
                    for i in range(4):
                        self.mm(bc[:, i * 128:(i + 1) * 128], tmpB[:, i * 128:(i + 1) * 128], self.U32[:, :],
                                True, True, [RtB, self.Rconst], [Rbc])
                    self.act(tmpA[:, :], bc[:, :], AF.Exp, [Rbc], [RtA], scale=-1.0 / 16)
                    self.act(tmpB[:, :], bc[:, :], AF.Exp, [Rbc], [RtB], scale=1.0 / 16)
                    S.op("dve", lambda h: h.tensor_copy(
                        out=dd[b2][:, :], in_=tmpA[:, :].rearrange("p (i t) -> p i t", i=4)[:, :, 127]),
                         [RtA], [Rdd[b2]])
                    if not os.environ.get('DBG_VLATE'):
                        vproj(tg, 1)

                def G3(tg):
                    b2 = tg % 2
                    rh = RhT[tg * 4:(tg + 1) * 4]
                    bq, Rbq = self.bank()
                    bkk, Rbkk = self.bank()
                    for c in range(8):
                        self.mm(bq[:, :], Whv[:, c, 0:128], hT[:, c, tg * 512:(tg + 1) * 512], c == 0, c == 7,
                                [RWh] + rh, [Rbq])
                    for c in range(8):
                        self.mm(bkk[:, :], Whv[:, c, 128:256], hT[:, c, tg * 512:(tg + 1) * 512], c == 0, c == 7,
                                [RWh] + rh, [Rbkk])
                    S.op("dve", lambda h: h.scalar_tensor_tensor(out=qs[b2][:, :], in0=bq[:, :], scalar=HK ** -0.5,
                                                                  in1=tmpA[:, :], op0=ALU.mult, op1=ALU.mult),
                         [Rbq, RtA], [Rqs[b2]])
                    S.op("dve", lambda h: h.tensor_tensor(out=ks[b2][:, :], in0=bkk[:, :], in1=tmpB[:, :],
                                                           op=ALU.mult), [Rbkk, RtB], [Rks[b2]])
                    for i in range(4):
                        S.op("pool", lambda h: h.tensor_scalar(
                            out=kh[b2][:, i * 128:(i + 1) * 128], in0=ks[b2][:, i * 128:(i + 1) * 128],
                            scalar1=dd[b2][:, i:i + 1], scalar2=1.0, op0=ALU.mult, op1=ALU.mult),
                             [Rks[b2], Rdd[b2]], [Rkh[b2]])

                def G4(tg):
                    b2 = tg % 2
                    bt, Rbt = self.bank()
                    btb = bt[:, :].bitcast(BF16)
                    for i in range(4):
                        self.tr(btb[:, i * 128:(i + 1) * 128], kh[b2][:, i * 128:(i + 1) * 128], [Rkh[b2]], [Rbt])
                    S.op("act", lambda h: h.copy(out=kht[b2][:, :], in_=btb[:, 0:512]), [Rbt], [Rkht[b2]])
                    if os.environ.get('DBG_VLATE'):
                        vproj(tg, 0)
                        vproj(tg, 1)

                G = [G1, G2, G3, G4]

                def rpiece(tg, j):
                    rh = RhT[tg * 4:(tg + 1) * 4]
                    bk, Rb = self.bank()
                    for c in range(8):
                        self.mm(bk[:, :], Whv[:, c, 512 + j * 128:512 + (j + 1) * 128],
                                hT[:, c, tg * 512:(tg + 1) * 512], c == 0, c == 7, [RWh] + rh, [Rb])
                    self.act(sr[hb][:, j, tg * 512:(tg + 1) * 512], bk[:, :], AF.Silu, [Rb], [Rsr[hb][tg]])
                    S.op("pool", lambda h: h.tensor_scalar(out=sr[hb][:, j, tg * 512:(tg + 1) * 512],
                                                            in0=sr[hb][:, j, tg * 512:(tg + 1) * 512],
                                                            scalar1=gcol[:, j:j + 1], scalar2=1.0,
                                                            op0=ALU.mult, op1=ALU.mult),
                         [Rsr[hb][tg], Rgcol], [Rsr[hb][tg]])

                rp = [(lambda tg=tg, j=j: rpiece(tg, j)) for tg in range(4) for j in range(2)]
                return G, rp

            Wh, RWh = self.wnext(hold=2)
            heads = {0: make_head(0, Wh[:, 0:6144].rearrange("p (c n) -> p c n", c=8), RWh)}
            for pc in heads[0][1]:
                pc()
            for g_ in heads[0][0]:
                g_(0)
            for hd in range(4):
                hb = hd % 2
                Wo, RWo = self.wnext(hold=3)
                Wov = Wo[:, 0:2048].rearrange("p (j n) -> p j n", j=2)
                G = heads[hd][0]
                S.op("dve", lambda h: h.memset(Sst[:, :], 0.0), writes=[RS])
                S.op("dve", lambda h: h.memset(Sbf[cur[0]][:, :], 0.0), writes=[RSbf[cur[0]]])

                def chunk(tg, i, hb=hb, Wov=Wov, RWo=RWo):
                    t = tg * 4 + i
                    b2 = tg % 2
                    k4 = kk[0] % 4
                    p2 = kk[0] % 2
                    kk[0] += 1
                    q_, k_, kt_, v_, d_ = qs[b2], ks[b2], kht[b2], vsb[b2], dd[b2]
                    Rv_ = Rv[b2][i // 2]
                    ba, Rba = self.bank()
                    self.mm(ba[:, 0:128], k_[:, i * 128:(i + 1) * 128], q_[:, i * 128:(i + 1) * 128], True, True,
                            [Rks[b2], Rqs[b2]], [Rba])
                    a_ = at[p2]
                    if os.environ.get("ATMASK", "dve") == "pool":
                        S.op("act", lambda h: h.copy(out=a_[:, :], in_=ba[:, 0:128]), [Rba], [Rat[p2]])
                        S.op("pool", lambda h: h.tensor_tensor(out=a_[:, :], in0=a_[:, :], in1=self.tri[:, :],
                                                                op=ALU.mult), [Rat[p2], self.Rconst], [Rat[p2]])
                    else:
                        S.op("dve", lambda h: h.tensor_tensor(out=a_[:, :], in0=ba[:, 0:128], in1=self.tri[:, :],
                                                               op=ALU.mult), [Rba, self.Rconst], [Rat[p2]])
                    bs, Rbs = self.bank()
                    self.mm(bs[:, 0:256], kt_[:, i * 128:(i + 1) * 128], v_[:, i * 256:(i + 1) * 256], True, True,
                            [Rkht[b2], Rv_], [Rbs])
                    bo, Rbo = self.bank()
                    c_ = cur[0]
                    for jv in range(2):
                        self.mm(bo[:, jv * 128:(jv + 1) * 128], Sbf[c_][:, jv * 128:(jv + 1) * 128],
                                q_[:, i * 128:(i + 1) * 128], True, False, [Rqs[b2], RSbf[c_]], [Rbo])
                        self.mm(bo[:, jv * 128:(jv + 1) * 128], v_[:, i * 256 + jv * 128:i * 256 + (jv + 1) * 128],
                                a_[:, :], False, True, [Rat[p2], Rv_], [Rbo])
                    nxt = 1 - c_
                    S.op("dve", lambda h: h.scalar_tensor_tensor(out=Sst[:, :], in0=Sst[:, :],
                                                                  scalar=d_[:, i:i + 1], in1=bs[:, 0:256],
                                                                  op0=ALU.mult, op1=ALU.add),
                         [RS, Rdd[b2], Rbs], [RS])
                    S.op(os.environ.get("SBF_ENG", "pool"), lambda h: (h.copy if hasattr(h, "copy") else h.tensor_copy)(
                        out=Sbf[nxt][:, :], in_=Sst[:, :]), [RS], [RSbf[nxt]])
                    cur[0] = nxt
                    s_, Rs_ = sso[k4], Rsso[k4]
                    self.act(sq[k4][:, :], bo[:, 0:256], AF.Square, [Rbo], [Rsq[k4]])
                    S.op("dve", lambda h: h.tensor_tensor(
                        out=go[k4][:, :, :], in0=bo[:, 0:256].rearrange("p (j t) -> p j t", j=2),
                        in1=sr[hb][:, :, t * 128:(t + 1) * 128], op=ALU.mult), [Rbo, Rsr[hb][tg]], [Rgo[k4]])

                    def n1():
                        bq2, Rbq2 = self.bank()
                        for jv in range(2):
                            self.mm(bq2[:, 0:2], sq[k4][:, jv * 128:(jv + 1) * 128], onesb[:, 0:2], jv == 0, jv == 1,
                                    [Rsq[k4], self.Rconst], [Rbq2])
                        self.act(s_[:, 1:2], bq2[:, 0:1], AF.Ln, [Rbq2], [Rs_], scale=1.0 / HV, bias=EPS)
                        self.act(s_[:, 2:3], s_[:, 1:2], AF.Exp, [Rs_], [Rs_], scale=-0.5)

                    def n4():
                        for nh in range(2):
                            by, Rby = self.bank()
                            for j in range(2):
                                self.mm(by[:, :], go[k4][:, j, :], Wov[:, j, nh * 512:(nh + 1) * 512], j == 0, j == 1,
                                        [Rgo[k4], RWo], [Rby])
                            xs = self.X[:, t, nh * 512:(nh + 1) * 512]
                            if os.environ.get("GLA_XADD", "dve") == "pool":
                                yk = ycnt[0] % 2
                                ycnt[0] += 1
                                self.act(ysb[yk][:, :], by[:, :], AF.Copy, [Rby, Rs_], [Rysb[yk]], scale=s_[:, 2:3])
                                S.op("pool", lambda h: h.tensor_tensor(out=xs, in0=xs, in1=ysb[yk][:, :], op=ALU.add),
                                     [Rysb[yk], self.RX[t]], [self.RX[t]])
                            else:
                                S.op("dve", lambda h: h.scalar_tensor_tensor(out=xs, in0=by[:, :], scalar=s_[:, 2:3],
                                                                              in1=xs, op0=ALU.mult, op1=ALU.add),
                                     [Rby, Rs_, self.RX[t]], [self.RX[t]])

                    defer(1, n1)
                    defer(2, n4)

                nG, nR = None, []
                for tg in range(4):
                    if tg == 3 and hd + 1 < 4:
                        Wn, RWn = self.wnext(hold=2)
                        heads[hd + 1] = make_head(hd + 1, Wn[:, 0:6144].rearrange("p (c n) -> p c n", c=8), RWn)
                        nG, nR = heads[hd + 1][0], list(heads[hd + 1][1])
                    for i in range(4):
                        if tg < 3:
                            G[i](tg + 1)
                        elif nG is not None:
                            for _ in range(2):
                                if nR:
                                    nR.pop(0)()
                            nG[i](0)
                        chunk(tg, i)
                        step[0] += 1
                        run_due()
            run_due(final=True)
            S.barrier()

    def diff_stage(self):
        S, nc, dr = self.S, self.nc, self.dr
        vecs = dr["vecs"]
        with ExitStack() as ls:
            self.open_ring(ls, 3072)
            hT = self.sb("hT", [128, 8, SEQ], BF16, ls)
            RhT = [Res("hT%d" % i) for i in range(NT)]
            with ExitStack() as ns:
                self.norm_stage(1, list(range(NT)), hT, RhT, ns)
                S.barrier()
            lamv = self.sb("lamv", [1, 256], F32, ls)
            lt = self.sb("lt", [1, 128 + 8], F32, ls)
            Rlam = Res("lam")
            S.dma("sp", lamv[:, :], vecs[0:1, V_LAM:V_LAM + 256], writes=[Rlam], owner=Rlam)
            S.op("dve", lambda h: h.tensor_tensor(out=lt[:, 0:128], in0=lamv[:, 0:128], in1=lamv[:, 128:256],
                                                   op=ALU.mult), [Rlam], [Rlam])
            S.op("dve", lambda h: h.tensor_reduce(out=lt[:, 128:130],
                                                   in_=lt[:, 0:128].rearrange("p (a d) -> p a d", a=2),
                                                   axis=AX.X, op=ALU.add), [Rlam], [Rlam])
            self.act(lt[:, 130:132], lt[:, 128:130], AF.Exp, [Rlam], [Rlam])
            S.op("dve", lambda h: h.tensor_tensor(out=lt[:, 132:133], in0=lt[:, 130:131], in1=lt[:, 131:132],
                                                   op=ALU.subtract), [Rlam], [Rlam])
            for cc in (134, 135):
                S.op("dve", lambda h: h.tensor_scalar(out=lt[:, cc:cc + 1], in0=lt[:, 132:133],
                                                       scalar1=LAMBDA_INIT, scalar2=-1.0, op0=ALU.add, op1=ALU.mult),
                     [Rlam], [Rlam])
            bk, Rb = self.bank()
            self.mm(bk[:, 0:2], self.ones32[0:1, :], lt[0:1, 134:136], True, True, [self.Rconst, Rlam], [Rb])
            neglam = self.sb("neglam", [128, 2], F32, ls)
            Rnl = Res("neglam")
            S.op("dve", lambda h: h.tensor_copy(out=neglam[:, :], in_=bk[:, 0:2]), [Rb], [Rnl])
            gn2 = self.sb("gn2", [128, 128], F32, ls)
            Rgn2 = Res("gn2")
            S.dma("sp", gn2[:, :], vecs[0:1, V_DIFFGN:V_DIFFGN + 128].broadcast_to([128, 128]), writes=[Rgn2],
                  owner=Rgn2)
            S.op("dve", lambda h: h.tensor_scalar(out=gn2[:, :], in0=gn2[:, :], scalar1=1.0 - LAMBDA_INIT,
                                                   scalar2=None, op0=ALU.mult), [Rgn2], [Rgn2])

            qz = [self.sb("qz", [128, 8, 2, 256], BF16, ls) for _ in range(2)]
            kT = [self.sb("kT", [128, SEQ], BF16, ls) for _ in range(2)]
            va = [self.sb("va", [128, NT, 130], BF16, ls) for _ in range(2)]
            vb = [self.sb("vb", [128, NT, 130], BF16, ls) for _ in range(2)]
            Rq, Rk, Rva, Rvb = ([Res(n + "0"), Res(n + "1")] for n in ("qz", "kT", "va", "vb"))
            for b_ in range(2):
                S.op("dve", lambda h: h.memset(va[b_][:, :, 128:130], 1.0), writes=[Rva[b_]])
                S.op("dve", lambda h: h.memset(vb[b_][:, :, 128:130], 1.0), writes=[Rvb[b_]])
                S.op("dve", lambda h: h.memset(qz[b_][64:128, :, 0, :], 0.0), writes=[Rq[b_]])
                S.op("dve", lambda h: h.memset(qz[b_][0:64, :, 1, :], 0.0), reads=[Rq[b_]], writes=[Rq[b_]])
            NPT = 2
            pT = [self.sb("pT", [128, NT, 512], BF16, ls) for _ in range(NPT)]
            RpT = [[Res("pT%d_%d" % (b_, i)) for i in range(NT)] for b_ in range(NPT)]
            rl = [self.sb("rl", [128, 2], F32, ls) for _ in range(4)]
            Rrl = [Res("rl%d" % i) for i in range(4)]
            t2 = [self.sb("t2", [128, 128], F32, ls) for _ in range(2)]
            Rt2 = [Res("t20"), Res("t21")]
            of = [self.sb("of", [128, 128], F32, ls) for _ in range(4)]
            Rof = [Res("of%d" % i) for i in range(4)]
            junk3 = self.sb("junk3", [128, 128], BF16, ls)
            Rj3 = Res("junk3")
            sso = [self.sb("ssd", [128, 8], F32, ls) for _ in range(2)]
            Rsso = [Res("ssd0"), Res("ssd1")]
            on = [self.sb("ond", [128, 128], BF16, ls) for _ in range(4)]
            Ron = [Res("ond%d" % i) for i in range(4)]
            oT = [self.sb("oTd", [128, 2, 128], BF16, ls) for _ in range(2)]
            RoT = [Res("oTd0"), Res("oTd1")]

            oTprev = self.gain[:, :].bitcast(BF16)
            RoTp = [self.Rgain] * 8
            genA = [0, 1, 2, 3, 4, 5]
            uu = 0
            step = [0]
            due = []

            def defer(k, fn):
                due.append((step[0] + k, fn))

            def run_due(final=False):
                rest = []
                for d_, fn in due:
                    if final or d_ <= step[0]:
                        fn()
                    else:
                        rest.append((d_, fn))
                due[:] = rest

            def proj_pieces(hd, W, RW):
                hb = hd % 2
                Wv = W[:, 0:3072].rearrange("p (c n) -> p c n", c=8)
                q_, k_, v_, v1_ = qz[hb], kT[hb], va[hb], vb[hb]
                pcs = []

                def pq(tg):
                    rh = RhT[tg * 4:(tg + 1) * 4]
                    bq, Rbq = self.bank(genA)
                    for c in range(8):
                        self.mm(bq[:, :], Wv[:, c, 0:128], hT[:, c, tg * 512:(tg + 1) * 512], c == 0, c == 7,
                                [RW] + rh, [Rbq])
                    self.act(q_[0:64, 2 * tg:2 * tg + 2, 0, :], bq[0:64, :].rearrange("p (g n) -> p g n", g=2),
                             AF.Copy, [Rbq], [Rq[hb]], scale=0.125)
                    S.op("dve", lambda h: h.tensor_scalar(
                        out=q_[64:128, 2 * tg:2 * tg + 2, 1, :], in0=bq[64:128, :].rearrange("p (g n) -> p g n", g=2),
                        scalar1=0.125, scalar2=None, op0=ALU.mult), [Rbq], [Rq[hb]])

                def pk(tg):
                    rh = RhT[tg * 4:(tg + 1) * 4]
                    bkk, Rbkk = self.bank(genA)
                    for c in range(8):
                        self.mm(bkk[:, :], Wv[:, c, 128:256], hT[:, c, tg * 512:(tg + 1) * 512], c == 0, c == 7,
                                [RW] + rh, [Rbkk])
                    S.op("dve", lambda h: h.tensor_copy(out=k_[:, tg * 512:(tg + 1) * 512], in_=bkk[:, :]),
                         [Rbkk], [Rk[hb]])

                def pv(t4):
                    bv, Rbv = self.bank(genA)
                    for ii in range(4):
                        t = t4 * 4 + ii
                        for c in range(8):
                            self.mm(bv[:, ii * 128:(ii + 1) * 128], hT[:, c, t * 128:(t + 1) * 128],
                                    Wv[:, c, 256:384], c == 0, c == 7, [RW, RhT[t]], [Rbv])
                    bv3 = bv[:, :].rearrange("p (i n) -> p i n", i=4)
                    S.op("dve", lambda h: h.tensor_copy(out=v_[:, t4 * 4:(t4 + 1) * 4, 0:128], in_=bv3), [Rbv], [Rva[hb]])
                    S.op("dve", lambda h: h.tensor_scalar(out=v1_[:, t4 * 4:(t4 + 1) * 4, 0:128], in0=bv3,
                                                           scalar1=neglam[:, 0:1], scalar2=None, op0=ALU.mult),
                         [Rbv, Rnl], [Rvb[hb]])

                for tg in range(4):
                    pcs.append(lambda tg=tg: pq(tg))
                    pcs.append(lambda tg=tg: pk(tg))
                    pcs.append(lambda tg=tg: pv(tg))
                return pcs

            nheads = int(os.environ.get('DBG_HEADS', '8'))
            pieces = []
            for hd in range(nheads):
                hb = hd % 2
                if hd == 0:
                    W, RW = self.wnext(hold=2)
                    for pc in proj_pieces(0, W, RW):
                        pc()
                while pieces:
                    pieces.pop(0)()
                run_due(final=True)
                if hd % 2 == 1:
                    prevWo = (Wo, RWo)
                    Wo, RWo = self.wnext(hold=3)
                else:
                    prevWo = (None, None)
                    Wo, RWo = self.wnext(hold=2)
                q_, k_, v_, v1_ = qz[hb], kT[hb], va[hb], vb[hb]

                units = list(range(int(os.environ.get('DBG_G2', '8'))))

                def st_parts(g2, ub, hb=hb, q_=q_, k_=k_):
                    p_ = pT[ub]
                    parts = []
                    qrhs = q_[:, g2, :, :].rearrange("p c n -> p (c n)")

                    def one(j):
                        bs_, Rbs_ = self.bank(genA)
                        self.mm(bs_[:, :], k_[:, j * 128:(j + 1) * 128], qrhs, True, True, [Rk[hb], Rq[hb]], [Rbs_])
                        self.act(p_[:, j, :], bs_[:, :], AF.Exp, [Rbs_], [RpT[ub][j]])
                        if j >= 2 * g2:
                            qi = j - 2 * g2
                            for c in range(2):
                                blk = p_[:, j, c * 256 + qi * 128:c * 256 + (qi + 1) * 128]
                                S.op(os.environ.get("MASK_ENG", "pool"),
                                     lambda h: h.tensor_tensor(out=blk, in0=blk, in1=self.tri[:, :], op=ALU.mult),
                                     [RpT[ub][j], self.Rconst], [RpT[ub][j]])

                    for j in range(2 * g2 + 2):
                        parts.append(lambda j=j: one(j))
                    return parts

                def chain1(g2):
                    for qi in range(2):
                        i = 2 * g2 + qi
                        bi = 6 + (i % 2)
                        acc, Racc = self.banks[bi], self.Rbk[bi]
                        r_, Rr_ = rl[i % 4], Rrl[i % 4]
                        S.op("dve", lambda h: h.reciprocal(
                            out=r_[:, 0:2], in_=acc[:, 0:260].rearrange("p (c n) -> p c n", c=2)[:, :, 128]),
                             [Racc], [Rr_])
                        S.op("dve", lambda h: h.tensor_scalar(out=t2[qi][:, :], in0=acc[:, 130:258],
                                                               scalar1=r_[:, 1:2], scalar2=None, op0=ALU.mult),
                             [Racc, Rr_], [Rt2[qi]])
                        S.op("dve", lambda h: h.scalar_tensor_tensor(out=of[i % 4][:, :], in0=acc[:, 0:128],
                                                                      scalar=r_[:, 0:1], in1=t2[qi][:, :],
                                                                      op0=ALU.mult, op1=ALU.add),
                             [Racc, Rr_, Rt2[qi]], [Rof[i % 4]])

                def chain2(g2):
                    s_, Rs_ = sso[g2 % 2], Rsso[g2 % 2]
                    for qi in range(2):
                        i = 2 * g2 + qi
                        self.act(junk3[:, :], of[i % 4][:, :], AF.Square, [Rof[i % 4]], [Rj3, Rs_],
                                 accum_out=s_[:, qi:qi + 1])
                    self.act(s_[:, 2:4], s_[:, 0:2], AF.Ln, [Rs_], [Rs_], scale=1.0 / 128, bias=SUBLN_EPS)
                    self.act(s_[:, 4:6], s_[:, 2:4], AF.Exp, [Rs_], [Rs_], scale=-0.5)

                def chain3(g2):
                    s_, Rs_ = sso[g2 % 2], Rsso[g2 % 2]
                    for qi in range(2):
                        i = 2 * g2 + qi
                        if os.environ.get("CH3_ENG", "pool") == "pool":
                            S.op("pool", lambda h: h.tensor_tensor(out=of[i % 4][:, :], in0=of[i % 4][:, :],
                                                                    in1=gn2[:, :], op=ALU.mult),
                                 [Rof[i % 4], Rgn2], [Rof[i % 4]])
                            S.op("pool", lambda h: h.tensor_scalar(out=on[i % 4][:, :], in0=of[i % 4][:, :],
                                                                    scalar1=s_[:, 4 + qi:5 + qi], scalar2=1.0,
                                                                    op0=ALU.mult, op1=ALU.mult),
                                 [Rof[i % 4], Rs_], [Ron[i % 4]])
                        else:
                            S.op("dve", lambda h: h.scalar_tensor_tensor(out=on[i % 4][:, :], in0=of[i % 4][:, :],
                                                                          scalar=s_[:, 4 + qi:5 + qi], in1=gn2[:, :],
                                                                          op0=ALU.mult, op1=ALU.mult),
                                 [Rof[i % 4], Rs_, Rgn2], [Ron[i % 4]])

                even = (hd % 2 == 0)

                def epi_a(g2, even=even):
                    btr, Rbtr = self.bank(genA)
                    btrb = btr[:, :].bitcast(BF16)
                    for qi in range(2):
                        i = 2 * g2 + qi
                        self.tr(btrb[:, qi * 128:(qi + 1) * 128], on[i % 4][:, :], [Ron[i % 4]], [Rbtr])
                    if even:
                        S.op("act", lambda h: h.copy(out=oTprev[:, g2 * 256:(g2 + 1) * 256], in_=btrb[:, 0:256]),
                             [Rbtr], [RoTp[g2]])
                    else:
                        ot_, Rot_ = oT[g2 % 2], RoT[g2 % 2]
                        S.op("act", lambda h: h.copy(out=ot_[:, :, :],
                                                      in_=btrb[:, 0:256].rearrange("p (q t) -> p q t", q=2)),
                             [Rbtr], [Rot_])

                def epi_b(g2, qi, Wo=Wo, RWo=RWo, even=even, Wop=prevWo[0], RWop=prevWo[1]):
                    if even:
                        return
                    ot_, Rot_ = oT[g2 % 2], RoT[g2 % 2]
                    i = 2 * g2 + qi
                    for nh in range(2):
                        by, Rby = self.bank(genA)
                        self.mm(by[:, :], oTprev[:, i * 128:(i + 1) * 128], Wop[:, nh * 512:(nh + 1) * 512], True, False,
                                [RoTp[g2], RWop], [Rby])
                        self.mm(by[:, :], ot_[:, qi, :], Wo[:, nh * 512:(nh + 1) * 512], False, True,
                                [Rot_, RWo], [Rby])
                        xs = self.X[:, i, nh * 512:(nh + 1) * 512]
                        S.op("dve", lambda h: h.tensor_tensor(out=xs, in0=by[:, :], in1=xs, op=ALU.add),
                             [Rby, self.RX[i]], [self.RX[i]])

                def pv_parts(g2, ub, hb=hb, v_=v_, v1_=v1_):
                    p_ = pT[ub]
                    parts = []
                    for c in range(2):
                        vv, Rvv = (v_, Rva[hb]) if c == 0 else (v1_, Rvb[hb])
                        for qi in range(2):
                            i = 2 * g2 + qi
                            bi = 6 + (i % 2)
                            acc, Racc = self.banks[bi], self.Rbk[bi]
                            for j in range(i + 1):
                                parts.append(lambda acc=acc, Racc=Racc, j=j, i=i, qi=qi, c=c, vv=vv, Rvv=Rvv: self.mm(
                                    acc[:, c * 130:c * 130 + 129],
                                    p_[:, j, c * 256 + qi * 128:c * 256 + (qi + 1) * 128], vv[:, j, 0:129],
                                    j == 0, j == i,
                                    (RpT[ub][0:2 * g2 + 2] if os.environ.get("PV_COARSE") else [RpT[ub][j]]) + [Rvv],
                                    [Racc]))
                    return parts

                def pv_tail(g2):
                    chain1(g2)
                    defer(1, lambda g2=g2: chain2(g2))
                    defer(1, lambda g2=g2: chain3(g2))
                    defer(2, lambda g2=g2: epi_a(g2))
                    defer(2, lambda g2=g2: epi_b(g2, 0))
                    defer(3, lambda g2=g2: epi_b(g2, 1))

                for pc in st_parts(units[0], uu % NPT):
                    pc()
                for n_ in range(len(units)):
                    if n_ == 1 and hd + 1 < nheads:
                        Wn, RWn = self.wnext(hold=4 if hd % 2 == 1 else 2)
                        pieces = proj_pieces(hd + 1, Wn, RWn)
                    sts = st_parts(units[n_ + 1], (uu + 1) % NPT) if n_ + 1 < len(units) else []
                    pvs = pv_parts(units[n_], uu % NPT)
                    per = -(-len(pvs) // max(1, len(sts)))
                    for st_ in sts:
                        st_()
                        for _ in range(per):
                            if pvs:
                                pvs.pop(0)()
                    while pvs:
                        pvs.pop(0)()
                    pv_tail(units[n_])
                    for _ in range(2):
                        if n_ >= 1 and pieces:
                            pieces.pop(0)()
                    uu += 1
                    step[0] += 1
                    run_due()
            run_due(final=True)
            S.barrier()

    def final_alloc(self, stack):
        self.ob = [self.sb("ob", [128, D], F32, stack) for _ in range(2)]
        self.Rob = [Res("ob0"), Res("ob1")]
        self.junkf = self.sb("junkf", [128, D], BF16, stack)
        self.Rjunkf = Res("junkf")

    def final_stats(self, tiles, load_gain=True):
        S = self.S
        if load_gain:
            S.dma("sp", self.gain[:, :], self.dr["vecs"][0:1, 4 * D:5 * D].broadcast_to([128, D]),
                  writes=[self.Rgain], owner=self.Rgain)
        lo, hi = tiles[0], tiles[-1] + 1
        for t in tiles:
            self.act(self.junkf[:, :], self.X[:, t, :], AF.Square, [self.RX[t]], [self.Rjunkf, self.Rss],
                     accum_out=self.ss[:, t:t + 1])
        self.act(self.lnss[:, lo:hi], self.ss[:, lo:hi], AF.Ln, [self.Rss], [self.Rln], scale=1.0 / D, bias=EPS)
        self.act(self.rstd[:, lo:hi], self.lnss[:, lo:hi], AF.Exp, [self.Rln], [self.Rrstd], scale=-0.5)

    def final_tile(self, t):
        S = self.S
        o_, Ro_ = self.ob[t % 2], self.Rob[t % 2]
        if self.final:
            S.op("dve", lambda h: h.scalar_tensor_tensor(out=o_[:, :], in0=self.X[:, t, :],
                                                          scalar=self.rstd[:, t:t + 1], in1=self.gain[:, :],
                                                          op0=ALU.mult, op1=ALU.mult),
                 [self.RX[t], self.Rrstd, self.Rgain], [Ro_])
        else:
            S.op("dve", lambda h: h.tensor_copy(out=o_[:, :], in_=self.X[:, t, :]), [self.RX[t]], [Ro_])
        S.dma("sp", self.out_ap[t * 128:(t + 1) * 128, :], o_[:, :], reads=[Ro_], owner=Ro_)

    def final_stage(self, out):
        S = self.S
        with ExitStack() as ls:
            self.final_alloc(ls)
            tiles = [t for t in range(NT) if t not in self.final_done]
            if self.final and tiles:
                self.final_stats(tiles)
            for t in tiles:
                self.final_tile(t)
            S.finish()


def prep_weights(inp):
    f = lambda a: np.ascontiguousarray(np.asarray(a, dtype=np.float32))
    w = {}
    gw = f(inp["gla_w_in"])[0].reshape(8, 128, 3072)
    heads = []
    for h in range(4):
        cols = np.concatenate([gw[:, :, h * 128:(h + 1) * 128],
                               gw[:, :, 512 + h * 128:512 + (h + 1) * 128],
                               gw[:, :, 1024 + h * 256:1024 + (h + 1) * 256],
                               gw[:, :, 2048 + h * 256:2048 + (h + 1) * 256]], axis=2)
        heads.append(cols.transpose(1, 0, 2).reshape(128, 6144))
    w["w_gla_h"] = f(np.stack(heads))
    go = f(inp["gla_w_out"])[0].reshape(4, 2, 128, 1024)
    w["w_gla_o"] = f(go.transpose(0, 2, 1, 3).reshape(4, 128, 2048))
    ga = f(inp["gla_w_gate_a"])[0].reshape(8, 128, 16)
    w["w_gla_a"] = f(ga.transpose(1, 0, 2).reshape(128, 128))
    wbm = np.zeros((128, 512), np.float32)
    wbm[0:16] = f(inp["gla_w_gate_b"])[0]
    wbm[32] = f(inp["gla_b_gate"])[0]
    w["w_gla_b"] = wbm
    dw = f(inp["diff_w_in"])[0].reshape(8, 128, 3072)
    heads = []
    for h in range(8):
        cols = np.concatenate([dw[:, :, h * 128:(h + 1) * 128],
                               dw[:, :, 1024 + h * 128:1024 + (h + 1) * 128],
                               dw[:, :, 2048 + h * 128:2048 + (h + 1) * 128]], axis=2)
        heads.append(cols.transpose(1, 0, 2).reshape(128, 3072))
    w["w_diff_h"] = f(np.stack(heads))
    w["w_diff_o"] = f(f(inp["diff_w_out"])[0].reshape(8, 128, 1024))
    fi = f(inp["ffn_w_in"]).reshape(2, 8, 128, 2, 11, 256)
    w["w_ffn_in"] = f(fi.transpose(0, 4, 2, 1, 3, 5).reshape(2, 11, 128, 4096))
    fo = f(inp["ffn_w_out"]).reshape(2, NF, 128, 4, 256)
    w["w_ffn_out"] = f(fo.transpose(0, 3, 2, 1, 4).reshape(2, 4, 128, 5632))
    vec = np.zeros((1, V_TOT), np.float32)
    vec[0, 0:2 * D] = f(inp["norm_mixer"]).reshape(-1)
    vec[0, 2 * D:4 * D] = f(inp["norm_ffn"]).reshape(-1)
    vec[0, 4 * D:5 * D] = f(inp["norm_final"]).reshape(-1)
    vec[0, V_GLAGN:V_GLAGN + 256] = f(inp["gla_norm"]).reshape(-1)
    vec[0, V_DIFFGN:V_DIFFGN + 128] = f(inp["diff_norm"]).reshape(-1)
    vec[0, V_LAM:V_LAM + 64] = f(inp["diff_lam_q1"]).reshape(-1)
    vec[0, V_LAM + 64:V_LAM + 128] = f(inp["diff_lam_q2"]).reshape(-1)
    vec[0, V_LAM + 128:V_LAM + 192] = f(inp["diff_lam_k1"]).reshape(-1)
    vec[0, V_LAM + 192:V_LAM + 256] = f(inp["diff_lam_k2"]).reshape(-1)
    w["vecs"] = vec
    return w


def run(inputs, stages=("gla", "ffn0", "diff", "ffn1"), final=True, cores=8, trace=False):
    b = Builder(stages, final)
    nc = b.build()
    w = prep_weights(inputs)
    x = np.asarray(inputs["x"], dtype=np.float32)
    in_maps = []
    for i in range(cores):
        m = dict(w)
        m["x"] = np.ascontiguousarray(x[i])
        in_maps.append(m)
    res = run_bass_kernel_spmd(nc, in_maps, core_ids=list(range(cores)), trace=trace)
    outs = np.stack([np.asarray(r["out"], dtype=np.float32) for r in res.results], axis=0)
    return outs, res, b


def kernel(**inputs):
    outs, _, _ = run(inputs)
    return outs
```

```python
import math
import os
import numpy as np
import concourse.bass as bass
import concourse.mybir as mybir
from concourse.bass_utils import run_bass_kernel_spmd
from contextlib import ExitStack

F32 = mybir.dt.float32
BF16 = mybir.dt.bfloat16
AF = mybir.ActivationFunctionType
ALU = mybir.AluOpType
AX = mybir.AxisListType

D = 1024
SEQ = 2048
NT = 16
DFF = 2816
NF = 22
HK = 128
HV = 256
EPS = 1e-6
SUBLN_EPS = 1e-5
LAMBDA_INIT = 0.8 - 0.6 * math.exp(-0.3 * 1)
RING_N = 6144
NRING = 4
NDMASEM = 48

V_NORM = 0
V_GLAGN = 5 * 1024
V_DIFFGN = V_GLAGN + 256
V_LAM = V_DIFFGN + 128
V_TOT = V_LAM + 256


class Tok:
    __slots__ = ("sem", "key", "val", "clock")

    def __init__(self, sem, key, val, clock):
        self.sem, self.key, self.val, self.clock = sem, key, val, clock


class Res:
    __slots__ = ("name", "w", "r", "dma", "excl")

    def __init__(self, name, excl=False):
        self.name = name
        self.w = None
        self.r = {}
        self.dma = None
        self.excl = excl


class Eng:
    def __init__(self, name, h, sem, selfdep):
        self.name, self.h, self.sem, self.selfdep = name, h, sem, selfdep
        self.key = "E_" + name
        self.count = 0
        self.clock = {}
        self.last = None
        self.nwait = 0


class Sched:
    def __init__(self, nc, stack):
        self.nc = nc
        self.stack = stack
        self.engs = {}
        for name, h, selfdep in [
            ("pe", nc.tensor, False),
            ("act", nc.scalar, True),
            ("dve", nc.vector, True),
            ("pool", nc.gpsimd, True),
            ("sp", nc.sync, True),
        ]:
            sem = stack.enter_context(nc.semaphore("s_" + name))
            self.engs[name] = Eng(name, h, sem, selfdep)
        self.dmas = []
        self.ndma = 0
        self.dma_pool = [stack.enter_context(nc.semaphore("d_%d" % i)) for i in range(NDMASEM)]
        self.all_sems = [e.sem for e in self.engs.values()] + self.dma_pool
        self.clear_sems()
        nc.all_engine_barrier()

    def clear_sems(self):
        for s in self.all_sems:
            self.nc.gpsimd.sem_clear(s)

    def _sync(self, e, deps):
        need = {}
        for t in deps:
            if t is None:
                continue
            if t.key == e.key and not e.selfdep:
                continue
            if e.clock.get(t.key, 0) >= t.val:
                continue
            cur = need.get(t.key)
            if cur is None or cur.val < t.val:
                need[t.key] = t
        for key, t in need.items():
            if e.clock.get(key, 0) >= t.val:
                continue
            e.h.wait_ge(t.sem, t.val)
            e.nwait += 1
            for k, v in t.clock.items():
                if e.clock.get(k, 0) < v:
                    e.clock[k] = v
            e.clock[key] = t.val

    @staticmethod
    def _deps(reads, writes, ekey=None):
        deps = []
        for r in reads:
            if r.w is not None:
                deps.append(r.w)
            if r.excl:
                deps.extend(t for k, t in r.r.items() if k != ekey)
        for w in writes:
            if w.w is not None:
                deps.append(w.w)
            deps.extend(w.r.values())
        return deps

    @staticmethod
    def _update(tok, reads, writes):
        for r in reads:
            r.r[tok.key] = tok
        for w in writes:
            w.w = tok
            w.r = {}

    def op(self, en, fn, reads=(), writes=()):
        e = self.engs[en]
        self._sync(e, self._deps(reads, writes, e.key))
        ins = fn(e.h)
        e.count += 1
        ins.then_inc(e.sem, 1)
        tok = Tok(e.sem, e.key, e.count, dict(e.clock))
        e.last = tok
        self._update(tok, reads, writes)
        return tok

    def dma(self, qn, out, in_, reads=(), writes=(), owner=None, **kw):
        e = self.engs[qn]
        self._sync(e, self._deps(reads, writes))
        ins = e.h.dma_start(out=out, in_=in_, **kw)
        if owner.dma is None:
            sem = self.dma_pool[len(self.dmas)]
            owner.dma = [sem, "D_%d" % len(self.dmas), 0, None]
            self.dmas.append(owner.dma)
        d = owner.dma
        d[2] += 16
        ins.then_inc(d[0], 16)
        tok = Tok(d[0], d[1], d[2], dict(e.clock))
        d[3] = tok
        self.ndma += 1
        self._update(tok, reads, writes)
        return tok

    def prewait(self, en, writes):
        e = self.engs[en]
        self._sync(e, self._deps([], writes, e.key))

    def all_toks(self):
        toks = [e.last for e in self.engs.values() if e.last is not None]
        toks += [d[3] for d in self.dmas if d[3] is not None]
        return toks

    def barrier(self):
        toks = self.all_toks()
        for e in self.engs.values():
            self._sync(e, toks)

    def finish(self):
        self._sync(self.engs["sp"], self.all_toks())

    def cleanup(self):
        self.nc.all_engine_barrier()
        self.clear_sems()


class Builder:
    def __init__(self, stages=("gla", "ffn0", "diff", "ffn1"), final=True):
        self.stages = stages
        self.final = final

    def sb(self, name, shape, dt, stack=None):
        stack = stack or self.st
        self._n += 1
        return stack.enter_context(self.nc.sbuf_tensor("%s_%d" % (name, self._n), shape, dt))

    def bank(self, pool=None):
        pool = pool or self.gen_banks
        i = pool[self._bk % len(pool)]
        self._bk += 1
        return self.banks[i], self.Rbk[i]

    def mm(self, out, lhsT, rhs, start, stop, reads, writes):
        self.S.op("pe", lambda h: h.matmul(out, lhsT=lhsT, rhs=rhs, start=start, stop=stop),
                  reads, writes)

    def tr(self, out, in_, reads, writes):
        self.S.op("pe", lambda h: h.transpose(out=out, in_=in_, identity=self.ident[:, :]),
                  list(reads) + [self.Rconst], writes)

    def act(self, out, in_, func, reads, writes, **kw):
        self.S.op("act", lambda h: h.activation(out=out, in_=in_, func=func, **kw), reads, writes)

    def open_ring(self, stack, nelem):
        self.ring = [self.sb("ring", [128, nelem], BF16, stack) for _ in range(NRING)]
        self.Rring = [Res("ring%d" % i) for i in range(NRING)]
        self.wbase = self.wpos
        self.wend = self.wstage_end.pop(0)
        if self.wbase == 0:
            gate = int(os.environ.get("XGATE", "-1"))
            if gate >= 0:
                self.S._sync(self.S.engs["pool"], [self.RX[gate].w])
        self._wissue(min(self.wbase + NRING - 1, self.wend))

    def _wissue(self, upto):
        while self.wissued <= upto:
            k = self.wissued
            ap, n = self.wseq[k]
            slot = (k - self.wbase) % NRING
            self.S.dma("pool", self.ring[slot][:, 0:n], ap, writes=[self.Rring[slot]],
                       owner=self.Rring[slot], max_dma_last_dim=2048)
            self.wissued += 1

    def wnext(self, hold=1):
        i = self.wpos
        self.wpos += 1
        self._wissue(min(i + NRING - hold, self.wend))
        slot = (i - self.wbase) % NRING
        return self.ring[slot], self.Rring[slot]

    def build(self):
        nc = bass.Bass("TRN2", target_bir_lowering=False)
        self.nc = nc
        self._n = 0
        self._bk = 0
        dr = {}

        def din(name, shape):
            dr[name] = nc.dram_tensor(name, shape, F32, kind="ExternalInput").ap()

        din("x", [SEQ, D])
        din("w_gla_h", [4, 128, 6144])
        din("w_gla_o", [4, 128, 2048])
        din("w_gla_a", [128, 128])
        din("w_gla_b", [128, 512])
        din("w_diff_h", [8, 128, 3072])
        din("w_diff_o", [8, 128, 1024])
        din("w_ffn_in", [2, 11, 128, 4096])
        din("w_ffn_out", [2, 4, 128, 5632])
        din("vecs", [1, V_TOT])
        out = nc.dram_tensor("out", [SEQ, D], F32, kind="ExternalOutput").ap()
        self.dr = dr
        self.out_ap = out
        self.final_done = set()
        self.last_stage = self.stages[-1] if self.stages else None

        wseq = []
        for stg in self.stages:
            if stg == "gla":
                for h in range(4):
                    wseq.append((dr["w_gla_h"][h], 6144))
                    wseq.append((dr["w_gla_o"][h], 2048))
            elif stg == "diff":
                for h in range(8):
                    wseq.append((dr["w_diff_h"][h], 3072))
                    wseq.append((dr["w_diff_o"][h], 1024))
            else:
                l = int(stg[-1])
                for hf in range(2):
                    for fb in range(11):
                        wseq.append((dr["w_ffn_in"][l, fb], 4096))
                    for nq in range(4):
                        wseq.append((dr["w_ffn_out"][l, nq], 5632))
        self.wseq = wseq
        self.wpos = 0
        self.wissued = 0
        self.wstage_end = []
        acc_ = 0
        for stg in self.stages:
            acc_ += {"gla": 8, "diff": 16}.get(stg, 30)
            self.wstage_end.append(acc_ - 1)

        with ExitStack() as st:
            self.st = st
            S = Sched(nc, st)
            self.S = S
            self.X = self.sb("X", [128, NT, D], F32)
            self.RX = [Res("X%d" % t) for t in range(NT)]
            self.ident = self.sb("ident", [128, 128], BF16)
            self.tri = self.sb("tri", [128, 128], BF16)
            self.U32 = self.sb("U32", [128, 128], F32)
            self.ones32 = self.sb("ones32", [128, 128], F32)
            self.Rconst = Res("const")
            self.gain = self.sb("gain", [128, D], F32)
            self.Rgain = Res("gain")
            self.ss = self.sb("ss", [128, NT], F32)
            self.lnss = self.sb("lnss", [128, NT], F32)
            self.rstd = self.sb("rstd", [128, NT], F32)
            self.Rss, self.Rln, self.Rrstd = Res("ss"), Res("lnss"), Res("rstd")
            self.banks = [st.enter_context(nc.psum_tensor("bk%d" % i, [128, 512], F32)) for i in range(8)]
            self.Rbk = [Res("bk%d" % i, excl=True) for i in range(8)]
            self.gen_banks = list(range(8))

            Rc = self.Rconst
            S.op("pool", lambda h: h.memset(self.ident[:, :], 1.0), writes=[Rc])
            S.op("pool", lambda h: h.affine_select(out=self.ident[:, :], in_=self.ident[:, :], pattern=[[1, 128]],
                                                    compare_op=ALU.is_equal, fill=0.0, base=0,
                                                    channel_multiplier=-1), reads=[Rc], writes=[Rc])
            S.op("pool", lambda h: h.memset(self.tri[:, :], 1.0), reads=[Rc], writes=[Rc])
            S.op("pool", lambda h: h.affine_select(out=self.tri[:, :], in_=self.tri[:, :], pattern=[[1, 128]],
                                                    compare_op=ALU.is_ge, fill=0.0, base=0,
                                                    channel_multiplier=-1), reads=[Rc], writes=[Rc])
            S.op("pool", lambda h: h.memset(self.U32[:, :], 1.0), reads=[Rc], writes=[Rc])
            S.op("pool", lambda h: h.affine_select(out=self.U32[:, :], in_=self.U32[:, :], pattern=[[1, 128]],
                                                    compare_op=ALU.is_ge, fill=0.0, base=0,
                                                    channel_multiplier=-1), reads=[Rc], writes=[Rc])
            S.op("pool", lambda h: h.memset(self.ones32[:, :], 1.0), reads=[Rc], writes=[Rc])

            for t in range(NT):
                S.dma("sp", self.X[:, t, :], dr["x"][t * 128:(t + 1) * 128, :], writes=[self.RX[t]],
                      owner=self.RX[t])

            for stg in self.stages:
                if stg == "gla":
                    self.gla_stage()
                elif stg == "diff":
                    self.diff_stage()
                else:
                    self.ffn_stage(int(stg[-1]))

            self.final_stage(out)
            S.finish()
            S.cleanup()
        self.stats = {k: (e.count, e.nwait) for k, e in S.engs.items()}
        self.stats["ndma"] = S.ndma
        return nc

    def norm_alloc(self, ls):
        junk = self.sb("junk", [128, D], BF16, ls)
        hbf = [self.sb("hbf", [128, D], BF16, ls) for _ in range(2)]
        return (junk, Res("junk"), hbf, [Res("hbf0"), Res("hbf1")])

    def norm_stats(self, gidx, tiles, tmp, li0=0, load_gain=True, skip_sq=False):
        S = self.S
        junk, Rjunk, _, _ = tmp
        vecs = self.dr["vecs"]
        if load_gain:
            S.dma("sp", self.gain[:, :], vecs[0:1, gidx * D:(gidx + 1) * D].broadcast_to([128, D]),
                  writes=[self.Rgain], owner=self.Rgain)
        n = len(tiles)
        for k, t in enumerate(tiles):
            li = li0 + k
            if not skip_sq:
                self.act(junk[:, :], self.X[:, t, :], AF.Square, [self.RX[t]], [Rjunk, self.Rss],
                         accum_out=self.ss[:, li:li + 1])
        self.act(self.lnss[:, li0:li0 + n], self.ss[:, li0:li0 + n], AF.Ln, [self.Rss], [self.Rln],
                 scale=1.0 / D, bias=EPS)
        self.act(self.rstd[:, li0:li0 + n], self.lnss[:, li0:li0 + n], AF.Exp, [self.Rln], [self.Rrstd], scale=-0.5)

    def norm_tile(self, li, t, hT, RhT, tmp, part=None):
        S = self.S
        _, _, hbf, Rhbf = tmp
        hb, Rhb = hbf[li % 2], Rhbf[li % 2]
        if part in (None, "a"):
            S.op("dve", lambda h: h.scalar_tensor_tensor(out=hb[:, :], in0=self.X[:, t, :],
                                                          scalar=self.rstd[:, li:li + 1], in1=self.gain[:, :],
                                                          op0=ALU.mult, op1=ALU.mult),
                 [self.RX[t], self.Rrstd, self.Rgain], [Rhb])
        if part == "a":
            return
        bk, Rb = self.bank()
        bkb = bk[:, :].bitcast(BF16)
        for c in range(8):
            self.tr(bkb[:, c * 128:(c + 1) * 128], hb[:, c * 128:(c + 1) * 128], [Rhb], [Rb])
        S.op("act", lambda h: h.copy(out=hT[:, :, li * 128:(li + 1) * 128],
                                      in_=bkb.rearrange("p (c t) -> p c t", c=8)), [Rb], [RhT[li]])

    def norm_stage(self, gidx, tiles, hT, RhT, ls, batch=16):
        tmp = self.norm_alloc(ls)
        skip_sq = getattr(self, "sq_ready", False) and len(tiles) == NT
        self.sq_ready = False
        for b0 in range(0, len(tiles), batch):
            self.norm_stats(gidx, tiles[b0:b0 + batch], tmp, li0=b0, load_gain=(b0 == 0), skip_sq=skip_sq)
            for k, t in enumerate(tiles[b0:b0 + batch]):
                self.norm_tile(b0 + k, t, hT, RhT, tmp)

    def ffn_stage(self, l):
        S = self.S
        with ExitStack() as ls:
            self.open_ring(ls, 5632)
            hTs = [self.sb("hTf", [128, 8, 1024], BF16, ls) for _ in range(2)]
            RhTs = [[Res("hTf%d_%d" % (h_, i)) for i in range(8)] for h_ in range(2)]
            aT = self.sb("aT", [128, NF, 1024], BF16, ls)
            RaT = [Res("aT0"), Res("aT1")]
            sg = [self.sb("sg", [128, 512], BF16, ls) for _ in range(2)]
            Rsg = [Res("sg0"), Res("sg1")]
            ntmp = self.norm_alloc(ls)
            early_final = self.final and self.last_stage == "ffn%d" % l
            si = self.stages.index("ffn%d" % l)
            early_next = (not early_final) and si + 1 < len(self.stages) and self.stages[si + 1] in ("gla", "diff")

            def early_sq(t):
                self.act(ntmp[0][:, :], self.X[:, t, :], AF.Square, [self.RX[t]], [ntmp[1], self.Rss],
                         accum_out=self.ss[:, t:t + 1])
            if early_final:
                self.final_alloc(ls)
            self.norm_stats(2 + l, list(range(0, 8)), ntmp)
            for li in range(8):
                self.norm_tile(li, li, hTs[0], RhTs[0], ntmp)
            k = 0
            for hf in range(2):
                hT, RhT = hTs[hf], RhTs[hf]
                for fb in range(11):
                    if hf == 1 and early_next and fb == 1:
                        for t_ in range(8):
                            early_sq(t_)
                    if hf == 1 and early_final:
                        if fb == 1:
                            self.final_stats(list(range(0, 8)))
                        if 2 <= fb < 10:
                            self.final_tile(fb - 2)
                            self.final_done.add(fb - 2)
                    if hf == 0:
                        if fb == 1:
                            self.norm_stats(2 + l, list(range(8, 16)), ntmp)
                        if 3 <= fb < 11:
                            self.norm_tile(fb - 3, 8 + fb - 3, hTs[1], RhTs[1], ntmp, part="b")
                        if 2 <= fb < 10:
                            self.norm_tile(fb - 2, 8 + fb - 2, hTs[1], RhTs[1], ntmp, part="a")
                    W, RW = self.wnext()
                    Wv = W[:, 0:4096].rearrange("p (c g n) -> p c g n", c=8, g=2)
                    for tg in range(2):
                        rh = RhT[tg * 4:(tg + 1) * 4]
                        for j in range(2):
                            bg, Rbg = self.bank()
                            bu, Rbu = self.bank()
                            for c in range(8):
                                self.mm(bg[:, :], Wv[:, c, 0, j * 128:(j + 1) * 128],
                                        hT[:, c, tg * 512:(tg + 1) * 512], c == 0, c == 7, [RW] + rh, [Rbg])
                            for c in range(8):
                                self.mm(bu[:, :], Wv[:, c, 1, j * 128:(j + 1) * 128],
                                        hT[:, c, tg * 512:(tg + 1) * 512], c == 0, c == 7, [RW] + rh, [Rbu])
                            s_, Rs_ = sg[k % 2], Rsg[k % 2]
                            k += 1
                            self.act(s_[:, :], bg[:, :], AF.Silu, [Rbg], [Rs_])
                            S.op("dve", lambda h: h.tensor_tensor(out=aT[:, fb * 2 + j, tg * 512:(tg + 1) * 512],
                                                                   in0=bu[:, :], in1=s_[:, :], op=ALU.mult),
                                 [Rbu, Rs_], [RaT[tg]])
                if hf == 1 and (early_final or early_next):
                    Ws = []
                    for nq in range(4):
                        W, RW = self.wnext(hold=nq + 1)
                        Ws.append((W[:, 0:5632].rearrange("p (f n) -> p f n", f=NF), RW))
                    prev_t = None
                    for tl in range(8):
                        t = 8 * hf + tl
                        for nq in range(4):
                            Wv, RW = Ws[nq]
                            bk, Rb = self.bank()
                            for f in range(NF):
                                self.mm(bk[:, 0:256], aT[:, f, tl * 128:(tl + 1) * 128], Wv[:, f, :],
                                        f == 0, f == NF - 1, [RaT[tl // 4], RW], [Rb])
                            xs = self.X[:, t, nq * 256:(nq + 1) * 256]
                            S.op("dve", lambda h: h.tensor_tensor(out=xs, in0=bk[:, 0:256], in1=xs, op=ALU.add),
                                 [Rb, self.RX[t]], [self.RX[t]])
                        if prev_t is not None:
                            if early_final:
                                self.final_stats([prev_t], load_gain=False)
                                self.final_tile(prev_t)
                                self.final_done.add(prev_t)
                            else:
                                early_sq(prev_t)
                        prev_t = t
                    if early_final:
                        self.final_stats([prev_t], load_gain=False)
                        self.final_tile(prev_t)
                        self.final_done.add(prev_t)
                    else:
                        early_sq(prev_t)
                        self.sq_ready = True
                else:
                    for nq in range(4):
                        W, RW = self.wnext()
                        Wv = W[:, 0:5632].rearrange("p (f n) -> p f n", f=NF)
                        for tl in range(8):
                            t = 8 * hf + tl
                            bk, Rb = self.bank()
                            for f in range(NF):
                                self.mm(bk[:, 0:256], aT[:, f, tl * 128:(tl + 1) * 128], Wv[:, f, :],
                                        f == 0, f == NF - 1, [RaT[tl // 4], RW], [Rb])
                            xs = self.X[:, t, nq * 256:(nq + 1) * 256]
                            S.op("dve", lambda h: h.tensor_tensor(out=xs, in0=bk[:, 0:256], in1=xs, op=ALU.add),
                                 [Rb, self.RX[t]], [self.RX[t]])
            S.barrier()

    def gla_stage(self):
        S, nc, dr = self.S, self.nc, self.dr
        with ExitStack() as ls:
            self.open_ring(ls, 6144)
            hT = self.sb("hT", [128, 8, SEQ], BF16, ls)
            RhT = [Res("hT%d" % i) for i in range(NT)]
            self.norm_stage(0, list(range(NT)), hT, RhT, ls)
            wa = self.sb("wa", [128, 128], BF16, ls)
            wb = self.sb("wb", [128, 512], BF16, ls)
            gnb = self.sb("gnb", [128, HV], F32, ls)
            Rwa, Rwb, Rgnb = Res("wa"), Res("wb"), Res("gnb")
            S.dma("pool", wa[:, :], dr["w_gla_a"][:, :], writes=[Rwa], owner=Rwa)
            S.dma("pool", wb[:, :], dr["w_gla_b"][:, :], writes=[Rwb], owner=Rwb)
            S.dma("sp", gnb[:, :], dr["vecs"][0:1, V_GLAGN:V_GLAGN + HV].broadcast_to([128, HV]),
                  writes=[Rgnb], owner=Rgnb)
            uaug = self.sb("uaug", [128, SEQ], BF16, ls)
            Ru = Res("uaug")
            S.op("dve", lambda h: h.memset(uaug[:, :], 0.0), writes=[Ru])
            S.op("dve", lambda h: h.memset(uaug[32:33, :], 1.0), reads=[Ru], writes=[Ru])
            for tg in range(4):
                bk, Rb = self.bank()
                for c in range(8):
                    self.mm(bk[0:16, :], wa[:, c * 16:(c + 1) * 16], hT[:, c, tg * 512:(tg + 1) * 512],
                            c == 0, c == 7, [Rwa] + RhT[tg * 4:(tg + 1) * 4], [Rb])
                S.op("act", lambda h: h.copy(out=uaug[0:16, tg * 512:(tg + 1) * 512], in_=bk[0:16, :]), [Rb], [Ru])

            sr = [self.sb("sr", [128, 2, SEQ], BF16, ls) for _ in range(2)]
            Rsr = [[Res("sr%d_%d" % (b_, i)) for i in range(4)] for b_ in range(2)]
            tmpA = self.sb("tmpA", [128, 512], F32, ls)
            tmpB = self.sb("tmpB", [128, 512], F32, ls)
            RtA, RtB = Res("tmpA"), Res("tmpB")
            dd = [self.sb("dd", [128, 4], F32, ls) for _ in range(2)]
            Rdd = [Res("dd0"), Res("dd1")]
            qs = [self.sb("qs", [128, 512], BF16, ls) for _ in range(2)]
            ks = [self.sb("ks", [128, 512], BF16, ls) for _ in range(2)]
            kh = [self.sb("kh", [128, 512], BF16, ls) for _ in range(2)]
            kht = [self.sb("kht", [128, 512], BF16, ls) for _ in range(2)]
            vsb = [self.sb("vsb", [128, 1024], BF16, ls) for _ in range(2)]
            Rqs, Rks, Rkh, Rkht = ([Res(n + "0"), Res(n + "1")] for n in ("qs", "ks", "kh", "kht"))
            Rv = [[Res("v%d_%d" % (b_, i)) for i in range(2)] for b_ in range(2)]
            at = [self.sb("at", [128, 128], BF16, ls) for _ in range(2)]
            Rat = [Res("at0"), Res("at1")]
            Sst = self.sb("Sst", [128, HV], F32, ls)
            RS = Res("S")
            Sbf = [self.sb("Sbf", [128, HV], BF16, ls) for _ in range(2)]
            RSbf = [Res("Sbf0"), Res("Sbf1")]
            sso = [self.sb("sso", [128, 4], F32, ls) for _ in range(4)]
            Rsso = [Res("sso%d" % i) for i in range(4)]
            go = [self.sb("go", [128, 2, 128], BF16, ls) for _ in range(4)]
            Rgo = [Res("go%d" % i) for i in range(4)]
            sq = [self.sb("sq", [128, HV], BF16, ls) for _ in range(4)]
            Rsq = [Res("sq%d" % i) for i in range(4)]
            gcol = self.sb("gcol", [128, 2], F32, ls)
            Rgcol = Res("gcol")
            for jv in range(2):
                S.dma("sp", gcol[:, jv:jv + 1],
                      dr["vecs"][0:1, V_GLAGN + jv * 128:V_GLAGN + (jv + 1) * 128].rearrange("o (p q) -> (o p) q", q=1),
                      writes=[Rgcol], owner=Rgcol)
            ysb = [self.sb("ysb", [128, 512], F32, ls) for _ in range(2)]
            Rysb = [Res("ysb0"), Res("ysb1")]
            ycnt = [0]
            onesb = self.sb("onesb", [128, 2], BF16, ls)
            S.op("dve", lambda h: h.memset(onesb[:, :], 1.0), reads=[self.Rconst], writes=[self.Rconst])

            step = [0]
            due = []

            def defer(k, fn):
                if os.environ.get('DBG_NODEFER'):
                    fn()
                    return
                due.append((step[0] + k, fn))

            def run_due(final=False):
                rest = []
                for d_, fn in due:
                    if final or d_ <= step[0]:
                        fn()
                    else:
                        rest.append((d_, fn))
                due[:] = rest

            kk = [0]
            cur = [0]

            def make_head(hd, Whv, RWh):
                hb = hd % 2
                def vproj(tg, i2):
                    b2 = tg % 2
                    bv, Rbv = self.bank()
                    for ii in range(2):
                        t = tg * 4 + i2 * 2 + ii
                        for c in range(8):
                            self.mm(bv[:, ii * 256:(ii + 1) * 256], hT[:, c, t * 128:(t + 1) * 128],
                                    Whv[:, c, 256:512], c == 0, c == 7, [RWh, RhT[t]], [Rbv])
                    S.op("act", lambda h: h.copy(out=vsb[b2][:, i2 * 512:(i2 + 1) * 512], in_=bv[:, :]),
                         [Rbv], [Rv[b2][i2]])

                def G1(tg):
                    bx, Rbx = self.bank()
                    for i in range(4):
                        t = tg * 4 + i
                        self.mm(bx[:, i * 128:(i + 1) * 128], uaug[:, t * 128:(t + 1) * 128],
                                wb[:, hd * 128:(hd + 1) * 128], True, True, [Ru, Rwb], [Rbx])
                    self.act(tmpA[:, :], bx[:, :], AF.Exp, [Rbx], [RtA], scale=-1.0)
                    self.act(tmpB[:, :], tmpA[:, :], AF.Ln, [RtA], [RtB], bias=1.0)
                    if not os.environ.get('DBG_VLATE'):
                        vproj(tg, 0)

                def G2(tg):
                    b2 = tg % 2
                    bc, Rbc = self.bank()
                    for i in range(4):
                        self.mm(bc[:, i * 128:(i + 1) * 128], tmpB[:, i * 128:(i + 1) * 128], self.U32[:, :],
                                True, True, [RtB, self.Rconst], [Rbc])
                    self.act(tmpA[:, :], bc[:, :], AF.Exp, [Rbc], [RtA], scale=-1.0 / 16)
                    self.act(tmpB[:, :], bc[:, :], AF.Exp, [Rbc], [RtB], scale=1.0 / 16)
                    S.op("dve", lambda h: h.tensor_copy(
                        out=dd[b2][:, :], in_=tmpA[:, :].rearrange("p (i t) -> p i t", i=4)[:, :, 127]),
                         [RtA], [Rdd[b2]])
                    if not os.environ.get('DBG_VLATE'):
                        vproj(tg, 1)

                def G3(tg):
                    b2 = tg % 2
                    rh = RhT[tg * 4:(tg + 1) * 4]
                    bq, Rbq = self.bank()
                    bkk, Rbkk = self.bank()
                    for c in range(8):
                        self.mm(bq[:, :], Whv[:, c, 0:128], hT[:, c, tg * 512:(tg + 1) * 512], c == 0, c == 7,
                                [RWh] + rh, [Rbq])
                    for c in range(8):
                        self.mm(bkk[:, :], Whv[:, c, 128:256], hT[:, c, tg * 512:(tg + 1) * 512], c == 0, c == 7,
                                [RWh] + rh, [Rbkk])
                    S.op("dve", lambda h: h.scalar_tensor_tensor(out=qs[b2][:, :], in0=bq[:, :], scalar=HK ** -0.5,
                                                                  in1=tmpA[:, :], op0=ALU.mult, op1=ALU.mult),
                         [Rbq, RtA], [Rqs[b2]])
                    S.op("dve", lambda h: h.tensor_tensor(out=ks[b2][:, :], in0=bkk[:, :], in1=tmpB[:, :],
                                                           op=ALU.mult), [Rbkk, RtB], [Rks[b2]])
                    for i in range(4):
                        S.op("pool", lambda h: h.tensor_scalar(
                            out=kh[b2][:, i * 128:(i + 1) * 128], in0=ks[b2][:, i * 128:(i + 1) * 128],
                            scalar1=dd[b2][:, i:i + 1], scalar2=1.0, op0=ALU.mult, op1=ALU.mult),
                             [Rks[b2], Rdd[b2]], [Rkh[b2]])

                def G4(tg):
                    b2 = tg % 2
                    bt, Rbt = self.bank()
                    btb = bt[:, :].bitcast(BF16)
                    for i in range(4):
                        self.tr(btb[:, i * 128:(i + 1) * 128], kh[b2][:, i * 128:(i + 1) * 128], [Rkh[b2]], [Rbt])
                    S.op("act", lambda h: h.copy(out=kht[b2][:, :], in_=btb[:, 0:512]), [Rbt], [Rkht[b2]])
                    if os.environ.get('DBG_VLATE'):
                        vproj(tg, 0)
                        vproj(tg, 1)

                G = [G1, G2, G3, G4]

                def rpiece(tg, j):
                    rh = RhT[tg * 4:(tg + 1) * 4]
                    bk, Rb = self.bank()
                    for c in range(8):
                        self.mm(bk[:, :], Whv[:, c, 512 + j * 128:512 + (j + 1) * 128],
                                hT[:, c, tg * 512:(tg + 1) * 512], c == 0, c == 7, [RWh] + rh, [Rb])
                    self.act(sr[hb][:, j, tg * 512:(tg + 1) * 512], bk[:, :], AF.Silu, [Rb], [Rsr[hb][tg]])
                    S.op("pool", lambda h: h.tensor_scalar(out=sr[hb][:, j, tg * 512:(tg + 1) * 512],
                                                            in0=sr[hb][:, j, tg * 512:(tg + 1) * 512],
                                                            scalar1=gcol[:, j:j + 1], scalar2=1.0,
                                                            op0=ALU.mult, op1=ALU.mult),
                         [Rsr[hb][tg], Rgcol], [Rsr[hb][tg]])

                rp = [(lambda tg=tg, j=j: rpiece(tg, j)) for tg in range(4) for j in range(2)]
                return G, rp

            Wh, RWh = self.wnext(hold=2)
            heads = {0: make_head(0, Wh[:, 0:6144].rearrange("p (c n) -> p c n", c=8), RWh)}
            for pc in heads[0][1]:
                pc()
            for g_ in heads[0][0]:
                g_(0)
            for hd in range(4):
                hb = hd % 2
                Wo, RWo = self.wnext(hold=3)
                Wov = Wo[:, 0:2048].rearrange("p (j n) -> p j n", j=2)
                G = heads[hd][0]
                S.op("dve", lambda h: h.memset(Sst[:, :], 0.0), writes=[RS])
                S.op("dve", lambda h: h.memset(Sbf[cur[0]][:, :], 0.0), writes=[RSbf[cur[0]]])

                def chunk(tg, i, hb=hb, Wov=Wov, RWo=RWo):
                    t = tg * 4 + i
                    b2 = tg % 2
                    k4 = kk[0] % 4
                    p2 = kk[0] % 2
                    kk[0] += 1
                    q_, k_, kt_, v_, d_ = qs[b2], ks[b2], kht[b2], vsb[b2], dd[b2]
                    Rv_ = Rv[b2][i // 2]
                    ba, Rba = self.bank()
                    self.mm(ba[:, 0:128], k_[:, i * 128:(i + 1) * 128], q_[:, i * 128:(i + 1) * 128], True, True,
                            [Rks[b2], Rqs[b2]], [Rba])
                    a_ = at[p2]
                    if os.environ.get("ATMASK", "dve") == "pool":
                        S.op("act", lambda h: h.copy(out=a_[:, :], in_=ba[:, 0:128]), [Rba], [Rat[p2]])
                        S.op("pool", lambda h: h.tensor_tensor(out=a_[:, :], in0=a_[:, :], in1=self.tri[:, :],
                                                                op=ALU.mult), [Rat[p2], self.Rconst], [Rat[p2]])
                    else:
                        S.op("dve", lambda h: h.tensor_tensor(out=a_[:, :], in0=ba[:, 0:128], in1=self.tri[:, :],
                                                               op=ALU.mult), [Rba, self.Rconst], [Rat[p2]])
                    bs, Rbs = self.bank()
                    self.mm(bs[:, 0:256], kt_[:, i * 128:(i + 1) * 128], v_[:, i * 256:(i + 1) * 256], True, True,
                            [Rkht[b2], Rv_], [Rbs])
                    bo, Rbo = self.bank()
                    c_ = cur[0]
                    for jv in range(2):
                        self.mm(bo[:, jv * 128:(jv + 1) * 128], Sbf[c_][:, jv * 128:(jv + 1) * 128],
                                q_[:, i * 128:(i + 1) * 128], True, False, [Rqs[b2], RSbf[c_]], [Rbo])
                        self.mm(bo[:, jv * 128:(jv + 1) * 128], v_[:, i * 256 + jv * 128:i * 256 + (jv + 1) * 128],
                                a_[:, :], False, True, [Rat[p2], Rv_], [Rbo])
                    nxt = 1 - c_
                    S.op("dve", lambda h: h.scalar_tensor_tensor(out=Sst[:, :], in0=Sst[:, :],
                                                                  scalar=d_[:, i:i + 1], in1=bs[:, 0:256],
                                                                  op0=ALU.mult, op1=ALU.add),
                         [RS, Rdd[b2], Rbs], [RS])
                    S.op(os.environ.get("SBF_ENG", "pool"), lambda h: (h.copy if hasattr(h, "copy") else h.tensor_copy)(
                        out=Sbf[nxt][:, :], in_=Sst[:, :]), [RS], [RSbf[nxt]])
                    cur[0] = nxt
                    s_, Rs_ = sso[k4], Rsso[k4]
                    self.act(sq[k4][:, :], bo[:, 0:256], AF.Square, [Rbo], [Rsq[k4]])
                    S.op("dve", lambda h: h.tensor_tensor(
                        out=go[k4][:, :, :], in0=bo[:, 0:256].rearrange("p (j t) -> p j t", j=2),
                        in1=sr[hb][:, :, t * 128:(t + 1) * 128], op=ALU.mult), [Rbo, Rsr[hb][tg]], [Rgo[k4]])

                    def n1():
                        bq2, Rbq2 = self.bank()
                        for jv in range(2):
                            self.mm(bq2[:, 0:2], sq[k4][:, jv * 128:(jv + 1) * 128], onesb[:, 0:2], jv == 0, jv == 1,
                                    [Rsq[k4], self.Rconst], [Rbq2])
                        self.act(s_[:, 1:2], bq2[:, 0:1], AF.Ln, [Rbq2], [Rs_], scale=1.0 / HV, bias=EPS)
                        self.act(s_[:, 2:3], s_[:, 1:2], AF.Exp, [Rs_], [Rs_], scale=-0.5)

                    def n4():
                        for nh in range(2):
                            by, Rby = self.bank()
                            for j in range(2):
                                self.mm(by[:, :], go[k4][:, j, :], Wov[:, j, nh * 512:(nh + 1) * 512], j == 0, j == 1,
                                        [Rgo[k4], RWo], [Rby])
                            xs = self.X[:, t, nh * 512:(nh + 1) * 512]
                            if os.environ.get("GLA_XADD", "dve") == "pool":
                                yk = ycnt[0] % 2
                                ycnt[0] += 1
                                self.act(ysb[yk][:, :], by[:, :], AF.Copy, [Rby, Rs_], [Rysb[yk]], scale=s_[:, 2:3])
                                S.op("pool", lambda h: h.tensor_tensor(out=xs, in0=xs, in1=ysb[yk][:, :], op=ALU.add),
                                     [Rysb[yk], self.RX[t]], [self.RX[t]])
                            else:
                                S.op("dve", lambda h: h.scalar_tensor_tensor(out=xs, in0=by[:, :], scalar=s_[:, 2:3],
                                                                              in1=xs, op0=ALU.mult, op1=ALU.add),
                                     [Rby, Rs_, self.RX[t]], [self.RX[t]])

                    defer(1, n1)
                    defer(3, n4)

                nG, nR = None, []
                for tg in range(4):
                    if tg == 3 and hd + 1 < 4:
                        Wn, RWn = self.wnext(hold=2)
                        heads[hd + 1] = make_head(hd + 1, Wn[:, 0:6144].rearrange("p (c n) -> p c n", c=8), RWn)
                        nG, nR = heads[hd + 1][0], list(heads[hd + 1][1])
                    for i in range(4):
                        if tg < 3:
                            G[i](tg + 1)
                        elif nG is not None:
                            for _ in range(2):
                                if nR:
                                    nR.pop(0)()
                            nG[i](0)
                        chunk(tg, i)
                        step[0] += 1
                        run_due()
            run_due(final=True)
            S.barrier()

    def diff_stage(self):
        S, nc, dr = self.S, self.nc, self.dr
        vecs = dr["vecs"]
        with ExitStack() as ls:
            self.open_ring(ls, 3072)
            hT = self.sb("hT", [128, 8, SEQ], BF16, ls)
            RhT = [Res("hT%d" % i) for i in range(NT)]
            with ExitStack() as ns:
                self.norm_stage(1, list(range(NT)), hT, RhT, ns)
                S.barrier()
            lamv = self.sb("lamv", [1, 256], F32, ls)
            lt = self.sb("lt", [1, 128 + 8], F32, ls)
            Rlam = Res("lam")
            S.dma("sp", lamv[:, :], vecs[0:1, V_LAM:V_LAM + 256], writes=[Rlam], owner=Rlam)
            S.op("dve", lambda h: h.tensor_tensor(out=lt[:, 0:128], in0=lamv[:, 0:128], in1=lamv[:, 128:256],
                                                   op=ALU.mult), [Rlam], [Rlam])
            S.op("dve", lambda h: h.tensor_reduce(out=lt[:, 128:130],
                                                   in_=lt[:, 0:128].rearrange("p (a d) -> p a d", a=2),
                                                   axis=AX.X, op=ALU.add), [Rlam], [Rlam])
            self.act(lt[:, 130:132], lt[:, 128:130], AF.Exp, [Rlam], [Rlam])
            S.op("dve", lambda h: h.tensor_tensor(out=lt[:, 132:133], in0=lt[:, 130:131], in1=lt[:, 131:132],
                                                   op=ALU.subtract), [Rlam], [Rlam])
            for cc in (134, 135):
                S.op("dve", lambda h: h.tensor_scalar(out=lt[:, cc:cc + 1], in0=lt[:, 132:133],
                                                       scalar1=LAMBDA_INIT, scalar2=-1.0, op0=ALU.add, op1=ALU.mult),
                     [Rlam], [Rlam])
            bk, Rb = self.bank()
            self.mm(bk[:, 0:2], self.ones32[0:1, :], lt[0:1, 134:136], True, True, [self.Rconst, Rlam], [Rb])
            neglam = self.sb("neglam", [128, 2], F32, ls)
            Rnl = Res("neglam")
            S.op("dve", lambda h: h.tensor_copy(out=neglam[:, :], in_=bk[:, 0:2]), [Rb], [Rnl])
            gn2 = self.sb("gn2", [128, 128], F32, ls)
            Rgn2 = Res("gn2")
            S.dma("sp", gn2[:, :], vecs[0:1, V_DIFFGN:V_DIFFGN + 128].broadcast_to([128, 128]), writes=[Rgn2],
                  owner=Rgn2)
            S.op("dve", lambda h: h.tensor_scalar(out=gn2[:, :], in0=gn2[:, :], scalar1=1.0 - LAMBDA_INIT,
                                                   scalar2=None, op0=ALU.mult), [Rgn2], [Rgn2])

            qz = [self.sb("qz", [128, 8, 2, 256], BF16, ls) for _ in range(2)]
            kT = [self.sb("kT", [128, SEQ], BF16, ls) for _ in range(2)]
            va = [self.sb("va", [128, NT, 130], BF16, ls) for _ in range(2)]
            vb = [self.sb("vb", [128, NT, 130], BF16, ls) for _ in range(2)]
            Rq, Rk, Rva, Rvb = ([Res(n + "0"), Res(n + "1")] for n in ("qz", "kT", "va", "vb"))
            for b_ in range(2):
                S.op("dve", lambda h: h.memset(va[b_][:, :, 128:130], 1.0), writes=[Rva[b_]])
                S.op("dve", lambda h: h.memset(vb[b_][:, :, 128:130], 1.0), writes=[Rvb[b_]])
                S.op("dve", lambda h: h.memset(qz[b_][64:128, :, 0, :], 0.0), writes=[Rq[b_]])
                S.op("dve", lambda h: h.memset(qz[b_][0:64, :, 1, :], 0.0), reads=[Rq[b_]], writes=[Rq[b_]])
            NPT = 2
            pT = [self.sb("pT", [128, NT, 512], BF16, ls) for _ in range(NPT)]
            RpT = [[Res("pT%d_%d" % (b_, i)) for i in range(NT)] for b_ in range(NPT)]
            rl = [self.sb("rl", [128, 2], F32, ls) for _ in range(4)]
            Rrl = [Res("rl%d" % i) for i in range(4)]
            t2 = [self.sb("t2", [128, 128], F32, ls) for _ in range(2)]
            Rt2 = [Res("t20"), Res("t21")]
            of = [self.sb("of", [128, 128], F32, ls) for _ in range(4)]
            Rof = [Res("of%d" % i) for i in range(4)]
            junk3 = self.sb("junk3", [128, 128], BF16, ls)
            Rj3 = Res("junk3")
            sso = [self.sb("ssd", [128, 8], F32, ls) for _ in range(2)]
            Rsso = [Res("ssd0"), Res("ssd1")]
            on = [self.sb("ond", [128, 128], BF16, ls) for _ in range(4)]
            Ron = [Res("ond%d" % i) for i in range(4)]
            oT = [self.sb("oTd", [128, 2, 128], BF16, ls) for _ in range(2)]
            RoT = [Res("oTd0"), Res("oTd1")]

            oTprev = self.gain[:, :].bitcast(BF16)
            RoTp = [self.Rgain] * 8
            genA = [0, 1, 2, 3, 4, 5]
            uu = 0
            step = [0]
            due = []

            def defer(k, fn):
                due.append((step[0] + k, fn))

            def run_due(final=False):
                rest = []
                for d_, fn in due:
                    if final or d_ <= step[0]:
                        fn()
                    else:
                        rest.append((d_, fn))
                due[:] = rest

            def proj_pieces(hd, W, RW):
                hb = hd % 2
                Wv = W[:, 0:3072].rearrange("p (c n) -> p c n", c=8)
                q_, k_, v_, v1_ = qz[hb], kT[hb], va[hb], vb[hb]
                pcs = []

                def pq(tg):
                    rh = RhT[tg * 4:(tg + 1) * 4]
                    bq, Rbq = self.bank(genA)
                    for c in range(8):
                        self.mm(bq[:, :], Wv[:, c, 0:128], hT[:, c, tg * 512:(tg + 1) * 512], c == 0, c == 7,
                                [RW] + rh, [Rbq])
                    self.act(q_[0:64, 2 * tg:2 * tg + 2, 0, :], bq[0:64, :].rearrange("p (g n) -> p g n", g=2),
                             AF.Copy, [Rbq], [Rq[hb]], scale=0.125)
                    S.op("dve", lambda h: h.tensor_scalar(
                        out=q_[64:128, 2 * tg:2 * tg + 2, 1, :], in0=bq[64:128, :].rearrange("p (g n) -> p g n", g=2),
                        scalar1=0.125, scalar2=None, op0=ALU.mult), [Rbq], [Rq[hb]])

                def pk(tg):
                    rh = RhT[tg * 4:(tg + 1) * 4]
                    bkk, Rbkk = self.bank(genA)
                    for c in range(8):
                        self.mm(bkk[:, :], Wv[:, c, 128:256], hT[:, c, tg * 512:(tg + 1) * 512], c == 0, c == 7,
                                [RW] + rh, [Rbkk])
                    S.op("dve", lambda h: h.tensor_copy(out=k_[:, tg * 512:(tg + 1) * 512], in_=bkk[:, :]),
                         [Rbkk], [Rk[hb]])

                def pv(t4):
                    bv, Rbv = self.bank(genA)
                    for ii in range(4):
                        t = t4 * 4 + ii
                        for c in range(8):
                            self.mm(bv[:, ii * 128:(ii + 1) * 128], hT[:, c, t * 128:(t + 1) * 128],
                                    Wv[:, c, 256:384], c == 0, c == 7, [RW, RhT[t]], [Rbv])
                    bv3 = bv[:, :].rearrange("p (i n) -> p i n", i=4)
                    S.op("dve", lambda h: h.tensor_copy(out=v_[:, t4 * 4:(t4 + 1) * 4, 0:128], in_=bv3), [Rbv], [Rva[hb]])
                    S.op("dve", lambda h: h.tensor_scalar(out=v1_[:, t4 * 4:(t4 + 1) * 4, 0:128], in0=bv3,
                                                           scalar1=neglam[:, 0:1], scalar2=None, op0=ALU.mult),
                         [Rbv, Rnl], [Rvb[hb]])

                for tg in range(4):
                    pcs.append(lambda tg=tg: pq(tg))
                    pcs.append(lambda tg=tg: pk(tg))
                    pcs.append(lambda tg=tg: pv(tg))
                return pcs

            nheads = int(os.environ.get('DBG_HEADS', '8'))
            pieces = []
            for hd in range(nheads):
                hb = hd % 2
                if hd == 0:
                    W, RW = self.wnext(hold=2)
                    for pc in proj_pieces(0, W, RW):
                        pc()
                while pieces:
                    pieces.pop(0)()
                run_due(final=True)
                if hd % 2 == 1:
                    prevWo = (Wo, RWo)
                    Wo, RWo = self.wnext(hold=3)
                else:
                    prevWo = (None, None)
                    Wo, RWo = self.wnext(hold=2)
                q_, k_, v_, v1_ = qz[hb], kT[hb], va[hb], vb[hb]

                units = list(range(int(os.environ.get('DBG_G2', '8'))))

                def st_parts(g2, ub, hb=hb, q_=q_, k_=k_):
                    p_ = pT[ub]
                    parts = []
                    qrhs = q_[:, g2, :, :].rearrange("p c n -> p (c n)")

                    def one(j):
                        bs_, Rbs_ = self.bank(genA)
                        self.mm(bs_[:, :], k_[:, j * 128:(j + 1) * 128], qrhs, True, True, [Rk[hb], Rq[hb]], [Rbs_])
                        self.act(p_[:, j, :], bs_[:, :], AF.Exp, [Rbs_], [RpT[ub][j]])
                        if j >= 2 * g2:
                            qi = j - 2 * g2
                            for c in range(2):
                                blk = p_[:, j, c * 256 + qi * 128:c * 256 + (qi + 1) * 128]
                                S.op(os.environ.get("MASK_ENG", "pool"),
                                     lambda h: h.tensor_tensor(out=blk, in0=blk, in1=self.tri[:, :], op=ALU.mult),
                                     [RpT[ub][j], self.Rconst], [RpT[ub][j]])

                    for j in range(2 * g2 + 2):
                        parts.append(lambda j=j: one(j))
                    return parts

                def chain1(g2):
                    for qi in range(2):
                        i = 2 * g2 + qi
                        bi = 6 + (i % 2)
                        acc, Racc = self.banks[bi], self.Rbk[bi]
                        r_, Rr_ = rl[i % 4], Rrl[i % 4]
                        S.op("dve", lambda h: h.reciprocal(
                            out=r_[:, 0:2], in_=acc[:, 0:260].rearrange("p (c n) -> p c n", c=2)[:, :, 128]),
                             [Racc], [Rr_])
                        S.op("dve", lambda h: h.tensor_scalar(out=t2[qi][:, :], in0=acc[:, 130:258],
                                                               scalar1=r_[:, 1:2], scalar2=None, op0=ALU.mult),
                             [Racc, Rr_], [Rt2[qi]])
                        S.op("dve", lambda h: h.scalar_tensor_tensor(out=of[i % 4][:, :], in0=acc[:, 0:128],
                                                                      scalar=r_[:, 0:1], in1=t2[qi][:, :],
                                                                      op0=ALU.mult, op1=ALU.add),
                             [Racc, Rr_, Rt2[qi]], [Rof[i % 4]])

                def chain2(g2):
                    s_, Rs_ = sso[g2 % 2], Rsso[g2 % 2]
                    for qi in range(2):
                        i = 2 * g2 + qi
                        self.act(junk3[:, :], of[i % 4][:, :], AF.Square, [Rof[i % 4]], [Rj3, Rs_],
                                 accum_out=s_[:, qi:qi + 1])
                    self.act(s_[:, 2:4], s_[:, 0:2], AF.Ln, [Rs_], [Rs_], scale=1.0 / 128, bias=SUBLN_EPS)
                    self.act(s_[:, 4:6], s_[:, 2:4], AF.Exp, [Rs_], [Rs_], scale=-0.5)

                def chain3(g2):
                    s_, Rs_ = sso[g2 % 2], Rsso[g2 % 2]
                    for qi in range(2):
                        i = 2 * g2 + qi
                        if os.environ.get("CH3_ENG", "pool") == "pool":
                            S.op("pool", lambda h: h.tensor_tensor(out=of[i % 4][:, :], in0=of[i % 4][:, :],
                                                                    in1=gn2[:, :], op=ALU.mult),
                                 [Rof[i % 4], Rgn2], [Rof[i % 4]])
                            S.op("pool", lambda h: h.tensor_scalar(out=on[i % 4][:, :], in0=of[i % 4][:, :],
                                                                    scalar1=s_[:, 4 + qi:5 + qi], scalar2=1.0,
                                                                    op0=ALU.mult, op1=ALU.mult),
                                 [Rof[i % 4], Rs_], [Ron[i % 4]])
                        else:
                            S.op("dve", lambda h: h.scalar_tensor_tensor(out=on[i % 4][:, :], in0=of[i % 4][:, :],
                                                                          scalar=s_[:, 4 + qi:5 + qi], in1=gn2[:, :],
                                                                          op0=ALU.mult, op1=ALU.mult),
                                 [Rof[i % 4], Rs_, Rgn2], [Ron[i % 4]])

                even = (hd % 2 == 0)

                def epi_a(g2, even=even):
                    btr, Rbtr = self.bank(genA)
                    btrb = btr[:, :].bitcast(BF16)
                    for qi in range(2):
                        i = 2 * g2 + qi
                        self.tr(btrb[:, qi * 128:(qi + 1) * 128], on[i % 4][:, :], [Ron[i % 4]], [Rbtr])
                    if even:
                        S.op("act", lambda h: h.copy(out=oTprev[:, g2 * 256:(g2 + 1) * 256], in_=btrb[:, 0:256]),
                             [Rbtr], [RoTp[g2]])
                    else:
                        ot_, Rot_ = oT[g2 % 2], RoT[g2 % 2]
                        S.op("act", lambda h: h.copy(out=ot_[:, :, :],
                                                      in_=btrb[:, 0:256].rearrange("p (q t) -> p q t", q=2)),
                             [Rbtr], [Rot_])

                def epi_b(g2, qi, Wo=Wo, RWo=RWo, even=even, Wop=prevWo[0], RWop=prevWo[1]):
                    if even:
                        return
                    ot_, Rot_ = oT[g2 % 2], RoT[g2 % 2]
                    i = 2 * g2 + qi
                    for nh in range(2):
                        by, Rby = self.bank(genA)
                        self.mm(by[:, :], oTprev[:, i * 128:(i + 1) * 128], Wop[:, nh * 512:(nh + 1) * 512], True, False,
                                [RoTp[g2], RWop], [Rby])
                        self.mm(by[:, :], ot_[:, qi, :], Wo[:, nh * 512:(nh + 1) * 512], False, True,
                                [Rot_, RWo], [Rby])
                        xs = self.X[:, i, nh * 512:(nh + 1) * 512]
                        S.op("dve", lambda h: h.tensor_tensor(out=xs, in0=by[:, :], in1=xs, op=ALU.add),
                             [Rby, self.RX[i]], [self.RX[i]])

                def pv_parts(g2, ub, hb=hb, v_=v_, v1_=v1_):
                    p_ = pT[ub]
                    parts = []
                    for c in range(2):
                        vv, Rvv = (v_, Rva[hb]) if c == 0 else (v1_, Rvb[hb])
                        for qi in range(2):
                            i = 2 * g2 + qi
                            bi = 6 + (i % 2)
                            acc, Racc = self.banks[bi], self.Rbk[bi]
                            for j in range(i + 1):
                                parts.append(lambda acc=acc, Racc=Racc, j=j, i=i, qi=qi, c=c, vv=vv, Rvv=Rvv: self.mm(
                                    acc[:, c * 130:c * 130 + 129],
                                    p_[:, j, c * 256 + qi * 128:c * 256 + (qi + 1) * 128], vv[:, j, 0:129],
                                    j == 0, j == i,
                                    (RpT[ub][0:2 * g2 + 2] if os.environ.get("PV_COARSE") else [RpT[ub][j]]) + [Rvv],
                                    [Racc]))
                    return parts

                def pv_tail(g2):
                    chain1(g2)
                    defer(1, lambda g2=g2: chain2(g2))
                    defer(1, lambda g2=g2: chain3(g2))
                    defer(2, lambda g2=g2: epi_a(g2))
                    defer(2, lambda g2=g2: epi_b(g2, 0))
                    defer(3, lambda g2=g2: epi_b(g2, 1))

                for pc in st_parts(units[0], uu % NPT):
                    pc()
                for n_ in range(len(units)):
                    if n_ == 1 and hd + 1 < nheads:
                        Wn, RWn = self.wnext(hold=4 if hd % 2 == 1 else 2)
                        pieces = proj_pieces(hd + 1, Wn, RWn)
                    sts = st_parts(units[n_ + 1], (uu + 1) % NPT) if n_ + 1 < len(units) else []
                    pvs = pv_parts(units[n_], uu % NPT)
                    per = -(-len(pvs) // max(1, len(sts)))
                    for st_ in sts:
                        st_()
                        for _ in range(per):
                            if pvs:
                                pvs.pop(0)()
                    while pvs:
                        pvs.pop(0)()
                    pv_tail(units[n_])
                    for _ in range(2):
                        if n_ >= 1 and pieces:
                            pieces.pop(0)()
                    uu += 1
                    step[0] += 1
                    run_due()
            run_due(final=True)
            S.barrier()

    def final_alloc(self, stack):
        self.ob = [self.sb("ob", [128, D], F32, stack) for _ in range(2)]
        self.Rob = [Res("ob0"), Res("ob1")]
        self.junkf = self.sb("junkf", [128, D], BF16, stack)
        self.Rjunkf = Res("junkf")

    def final_stats(self, tiles, load_gain=True):
        S = self.S
        if load_gain:
            S.dma("sp", self.gain[:, :], self.dr["vecs"][0:1, 4 * D:5 * D].broadcast_to([128, D]),
                  writes=[self.Rgain], owner=self.Rgain)
        lo, hi = tiles[0], tiles[-1] + 1
        for t in tiles:
            self.act(self.junkf[:, :], self.X[:, t, :], AF.Square, [self.RX[t]], [self.Rjunkf, self.Rss],
                     accum_out=self.ss[:, t:t + 1])
        self.act(self.lnss[:, lo:hi], self.ss[:, lo:hi], AF.Ln, [self.Rss], [self.Rln], scale=1.0 / D, bias=EPS)
        self.act(self.rstd[:, lo:hi], self.lnss[:, lo:hi], AF.Exp, [self.Rln], [self.Rrstd], scale=-0.5)

    def final_tile(self, t):
        S = self.S
        o_, Ro_ = self.ob[t % 2], self.Rob[t % 2]
        if self.final:
            S.op("dve", lambda h: h.scalar_tensor_tensor(out=o_[:, :], in0=self.X[:, t, :],
                                                          scalar=self.rstd[:, t:t + 1], in1=self.gain[:, :],
                                                          op0=ALU.mult, op1=ALU.mult),
                 [self.RX[t], self.Rrstd, self.Rgain], [Ro_])
        else:
            S.op("dve", lambda h: h.tensor_copy(out=o_[:, :], in_=self.X[:, t, :]), [self.RX[t]], [Ro_])
        S.dma("sp", self.out_ap[t * 128:(t + 1) * 128, :], o_[:, :], reads=[Ro_], owner=Ro_)

    def final_stage(self, out):
        S = self.S
        with ExitStack() as ls:
            self.final_alloc(ls)
            tiles = [t for t in range(NT) if t not in self.final_done]
            if self.final and tiles:
                self.final_stats(tiles)
            for t in tiles:
                self.final_tile(t)
            S.finish()


def prep_weights(inp):
    f = lambda a: np.ascontiguousarray(np.asarray(a, dtype=np.float32))
    w = {}
    gw = f(inp["gla_w_in"])[0].reshape(8, 128, 3072)
    heads = []
    for h in range(4):
        cols = np.concatenate([gw[:, :, h * 128:(h + 1) * 128],
                               gw[:, :, 512 + h * 128:512 + (h + 1) * 128],
                               gw[:, :, 1024 + h * 256:1024 + (h + 1) * 256],
                               gw[:, :, 2048 + h * 256:2048 + (h + 1) * 256]], axis=2)
        heads.append(cols.transpose(1, 0, 2).reshape(128, 6144))
    w["w_gla_h"] = f(np.stack(heads))
    go = f(inp["gla_w_out"])[0].reshape(4, 2, 128, 1024)
    w["w_gla_o"] = f(go.transpose(0, 2, 1, 3).reshape(4, 128, 2048))
    ga = f(inp["gla_w_gate_a"])[0].reshape(8, 128, 16)
    w["w_gla_a"] = f(ga.transpose(1, 0, 2).reshape(128, 128))
    wbm = np.zeros((128, 512), np.float32)
    wbm[0:16] = f(inp["gla_w_gate_b"])[0]
    wbm[32] = f(inp["gla_b_gate"])[0]
    w["w_gla_b"] = wbm
    dw = f(inp["diff_w_in"])[0].reshape(8, 128, 3072)
    heads = []
    for h in range(8):
        cols = np.concatenate([dw[:, :, h * 128:(h + 1) * 128],
                               dw[:, :, 1024 + h * 128:1024 + (h + 1) * 128],
                               dw[:, :, 2048 + h * 128:2048 + (h + 1) * 128]], axis=2)
        heads.append(cols.transpose(1, 0, 2).reshape(128, 3072))
    w["w_diff_h"] = f(np.stack(heads))
    w["w_diff_o"] = f(f(inp["diff_w_out"])[0].reshape(8, 128, 1024))
    fi = f(inp["ffn_w_in"]).reshape(2, 8, 128, 2, 11, 256)
    w["w_ffn_in"] = f(fi.transpose(0, 4, 2, 1, 3, 5).reshape(2, 11, 128, 4096))
    fo = f(inp["ffn_w_out"]).reshape(2, NF, 128, 4, 256)
    w["w_ffn_out"] = f(fo.transpose(0, 3, 2, 1, 4).reshape(2, 4, 128, 5632))
    vec = np.zeros((1, V_TOT), np.float32)
    vec[0, 0:2 * D] = f(inp["norm_mixer"]).reshape(-1)
    vec[0, 2 * D:4 * D] = f(inp["norm_ffn"]).reshape(-1)
    vec[0, 4 * D:5 * D] = f(inp["norm_final"]).reshape(-1)
    vec[0, V_GLAGN:V_GLAGN + 256] = f(inp["gla_norm"]).reshape(-1)
    vec[0, V_DIFFGN:V_DIFFGN + 128] = f(inp["diff_norm"]).reshape(-1)
    vec[0, V_LAM:V_LAM + 64] = f(inp["diff_lam_q1"]).reshape(-1)
    vec[0, V_LAM + 64:V_LAM + 128] = f(inp["diff_lam_q2"]).reshape(-1)
    vec[0, V_LAM + 128:V_LAM + 192] = f(inp["diff_lam_k1"]).reshape(-1)
    vec[0, V_LAM + 192:V_LAM + 256] = f(inp["diff_lam_k2"]).reshape(-1)
    w["vecs"] = vec
    return w


def run(inputs, stages=("gla", "ffn0", "diff", "ffn1"), final=True, cores=8, trace=False):
    b = Builder(stages, final)
    nc = b.build()
    w = prep_weights(inputs)
    x = np.asarray(inputs["x"], dtype=np.float32)
    in_maps = []
    for i in range(cores):
        m = dict(w)
        m["x"] = np.ascontiguousarray(x[i])
        in_maps.append(m)
    res = run_bass_kernel_spmd(nc, in_maps, core_ids=list(range(cores)), trace=trace)
    outs = np.stack([np.asarray(r["out"], dtype=np.float32) for r in res.results], axis=0)
    return outs, res, b


def kernel(**inputs):
    outs, _, _ = run(inputs)
    return outs
```
